# Optimizing a Trainium2 kernel written in Bass

```python
import math
import jax
import jax.numpy as jnp
from jax import lax
import numpy as np

D_MODEL = 1024
BATCH = 4
SEQ = 4096
DEPTH = 2
DEC_BATCH = 32
DEC_SEQ = 4
PAST_LEN = 8192
PAGE_SIZE = 128

HEAD_DIM = 64
W_ATT = D_MODEL // 2
N_HEADS = W_ATT // HEAD_DIM
W_POOL = D_MODEL // 4
N_POOL_GROUPS = 4
POOL_GW = W_POOL // N_POOL_GROUPS
POOL_WINDOWS = (2, 4, 8, 16)
POOL_BUF = max(POOL_WINDOWS) - 1
W_CONV = D_MODEL // 4
CONV_W = 31
CONV_BUF = CONV_W - 1
N_BRANCH = 3
D_FF = ((8 * D_MODEL // 3 + 255) // 256) * 256
Q_BLOCK = 128
LN_EPS = 1e-5
ALPHA = (2.0 * DEPTH) ** 0.25
BETA = (8.0 * DEPTH) ** -0.25
SB_BIAS_INIT = -7.0
N_IN = W_POOL + 3 * W_ATT + 2 * W_CONV + N_BRANCH * D_MODEL

kernel_name = 'hybrid_pool_stickbreak_conformer_decoder'


def _layernorm(x, g, b):
    xf = x.astype(jnp.float32)
    mu = xf.mean(-1, keepdims=True)
    var = jnp.square(xf - mu).mean(-1, keepdims=True)
    y = (xf - mu) * lax.rsqrt(var + LN_EPS) * g.astype(jnp.float32) + b.astype(jnp.float32)
    return y.astype(x.dtype)


def _pool_branch(u, buf, start_pos):
    B, T, C = u.shape
    ext = jnp.concatenate([buf, u], axis=1)
    csum = jnp.cumsum(ext.astype(jnp.float32), axis=1)
    csum = jnp.concatenate([jnp.zeros((B, 1, C), jnp.float32), csum], axis=1)
    c_end = csum[:, POOL_BUF + 1:]
    pos = start_pos + jnp.arange(T)
    means = []
    for g, w in enumerate(POOL_WINDOWS):
        lo, hi = g * POOL_GW, (g + 1) * POOL_GW
        st = POOL_BUF + 1 - w
        win_sum = c_end[..., lo:hi] - csum[:, st:st + T, lo:hi]
        count = jnp.minimum(pos + 1, w).astype(jnp.float32)[None, :, None]
        means.append(win_sum / count)
    mean = jnp.stack(means, axis=2)
    p = mean - u.reshape(B, T, N_POOL_GROUPS, POOL_GW).astype(jnp.float32)
    return p.astype(u.dtype), ext[:, -POOL_BUF:]


def _dwconv(g, buf, w, b):
    ext = jnp.concatenate([buf, g], axis=1)
    out = lax.conv_general_dilated(ext, w[:, None, :].astype(ext.dtype), (1,), 'VALID',
                                   dimension_numbers=('NWC', 'WIO', 'NWC'),
                                   feature_group_count=g.shape[-1])
    return out + b.astype(out.dtype), ext[:, -CONV_BUF:]


def _sb_attend(q, k, v, q_pos, k_pos, sb_bias):
    z = jnp.einsum('bqhd,bkhd->bhqk', q, k, preferred_element_type=jnp.float32) * (HEAD_DIM ** -0.5)
    z = z + sb_bias.astype(jnp.float32)[None, :, None, None]
    mask = k_pos[None, :] < q_pos[:, None]
    log_rest = jnp.where(mask, jax.nn.log_sigmoid(-z), 0.0)
    log_pass = lax.cumsum(log_rest, axis=3, reverse=True) - log_rest
    a = jnp.where(mask, jnp.exp(jax.nn.log_sigmoid(z) + log_pass), 0.0)
    return jnp.einsum('bhqk,bkhd->bqhd', a.astype(v.dtype), v)


def _sb_prompt(q, k, v, sb_bias):
    B, S, H, Dh = q.shape
    k_pos = jnp.arange(S)

    def blk(i):
        qb = lax.dynamic_slice_in_dim(q, i * Q_BLOCK, Q_BLOCK, axis=1)
        return _sb_attend(qb, k, v, i * Q_BLOCK + jnp.arange(Q_BLOCK), k_pos, sb_bias)

    o = lax.map(blk, jnp.arange(S // Q_BLOCK))
    return o.transpose(1, 0, 2, 3, 4).reshape(B, S, H, Dh)


def _token_mixer(x, kv_past, pool_buf, conv_buf, start_pos, w_in, b_gate, sb_bias, pool_w, pool_scale,
                 w_att_o, conv_dw, conv_b, conv_ln_g, conv_ln_b, conv_pw, w_out):
    B, T, _ = x.shape
    h = x @ w_in
    o1 = W_POOL
    o2 = o1 + 3 * W_ATT
    o3 = o2 + 2 * W_CONV
    u_a = h[..., :o1]
    q, k, v = jnp.split(h[..., o1:o2], 3, axis=-1)
    q = q.reshape(B, T, N_HEADS, HEAD_DIM)
    k = k.reshape(B, T, N_HEADS, HEAD_DIM)
    v = v.reshape(B, T, N_HEADS, HEAD_DIM)
    c_in = h[..., o2:o3]
    gates = jax.nn.sigmoid(h[..., o3:] + b_gate).reshape(B, T, N_BRANCH, D_MODEL)
    p, new_pool = _pool_branch(u_a, pool_buf, start_pos)
    br_a = jnp.einsum('btgc,gcd->btgd', p, pool_w).reshape(B, T, D_MODEL) * pool_scale
    if kv_past is None:
        o = _sb_prompt(q, k, v, sb_bias)
    else:
        k_past, v_past = kv_past
        P = k_past.shape[1]
        o = _sb_attend(q, jnp.concatenate([k_past, k], axis=1), jnp.concatenate([v_past, v], axis=1),
                       start_pos + jnp.arange(T), jnp.arange(P + T), sb_bias)
    br_b = o.reshape(B, T, W_ATT) @ w_att_o
    glu = c_in[..., :W_CONV] * jax.nn.sigmoid(c_in[..., W_CONV:])
    cc, new_conv = _dwconv(glu, conv_buf, conv_dw, conv_b)
    br_c = jax.nn.silu(_layernorm(cc, conv_ln_g, conv_ln_b)) @ conv_pw
    merged = gates[:, :, 0] * br_a + gates[:, :, 1] * br_b + gates[:, :, 2] * br_c
    return merged @ w_out, (k, v, new_pool, new_conv)


def _ffn(x, wg, wu, wd):
    return (jax.nn.silu(x @ wg) * (x @ wu)) @ wd


def setup_inputs(seed: int = 0) -> dict:
    key = jax.random.key(seed)
    ks = jax.random.split(key, 32)
    n_pages = PAST_LEN // PAGE_SIZE
    used = DEC_BATCH * n_pages
    n_pool_pages = used + max(1, used // 4)
    f32 = jnp.float32

    def nrm(k, shape, scale):
        return jax.random.normal(k, shape, f32) * scale

    page_table = jax.random.permutation(ks[0], n_pool_pages)[:used].reshape(DEC_BATCH, n_pages).astype(jnp.int32)
    return {
        'x_prompt': nrm(ks[1], (BATCH, SEQ, D_MODEL), 1.0),
        'x_sample': nrm(ks[2], (DEC_BATCH, DEC_SEQ, D_MODEL), 1.0),
        'cache_k': nrm(ks[3], (DEPTH, n_pool_pages, PAGE_SIZE, N_HEADS, HEAD_DIM), 1.0),
        'cache_v': nrm(ks[4], (DEPTH, n_pool_pages, PAGE_SIZE, N_HEADS, HEAD_DIM), 1.0),
        'state_pool': nrm(ks[5], (DEPTH, DEC_BATCH, POOL_BUF, W_POOL), 1.0),
        'state_conv': nrm(ks[6], (DEPTH, DEC_BATCH, CONV_BUF, W_CONV), 0.5),
        'page_table': page_table,
        'w_in': nrm(ks[7], (DEPTH, D_MODEL, N_IN), D_MODEL ** -0.5),
        'b_gate': nrm(ks[8], (DEPTH, N_BRANCH * D_MODEL), 0.02),
        'sb_bias': SB_BIAS_INIT + nrm(ks[25], (DEPTH, N_HEADS), 0.5),
        'pool_w': nrm(ks[9], (DEPTH, N_POOL_GROUPS, POOL_GW, D_MODEL // N_POOL_GROUPS), POOL_GW ** -0.5),
        'pool_scale': 1.0 + nrm(ks[10], (DEPTH, D_MODEL), 0.1),
        'w_att_o': nrm(ks[11], (DEPTH, W_ATT, D_MODEL), W_ATT ** -0.5),
        'conv_dw': nrm(ks[12], (DEPTH, CONV_W, W_CONV), CONV_W ** -0.5),
        'conv_b': nrm(ks[13], (DEPTH, W_CONV), 0.02),
        'conv_ln_g': 1.0 + nrm(ks[14], (DEPTH, W_CONV), 0.05),
        'conv_ln_b': nrm(ks[15], (DEPTH, W_CONV), 0.02),
        'conv_pw': nrm(ks[16], (DEPTH, W_CONV, D_MODEL), W_CONV ** -0.5),
        'w_out': nrm(ks[17], (DEPTH, D_MODEL, D_MODEL), BETA * D_MODEL ** -0.5),
        'ln1_g': 1.0 + nrm(ks[18], (DEPTH, D_MODEL), 0.05),
        'ln1_b': nrm(ks[19], (DEPTH, D_MODEL), 0.02),
        'ffn_w_gate': nrm(ks[20], (DEPTH, D_MODEL, D_FF), D_MODEL ** -0.5),
        'ffn_w_up': nrm(ks[21], (DEPTH, D_MODEL, D_FF), D_MODEL ** -0.5),
        'ffn_w_down': nrm(ks[22], (DEPTH, D_FF, D_MODEL), BETA * D_FF ** -0.5),
        'ln2_g': 1.0 + nrm(ks[23], (DEPTH, D_MODEL), 0.05),
        'ln2_b': nrm(ks[24], (DEPTH, D_MODEL), 0.02),
    }


def reference(x_prompt, x_sample, cache_k, cache_v, state_pool, state_conv, page_table,
              w_in, b_gate, sb_bias, pool_w, pool_scale, w_att_o, conv_dw, conv_b, conv_ln_g, conv_ln_b,
              conv_pw, w_out, ln1_g, ln1_b, ffn_w_gate, ffn_w_up, ffn_w_down, ln2_g, ln2_b):
    B = x_prompt.shape[0]
    DB, n_pages = page_table.shape
    past = n_pages * cache_k.shape[2]
    xp, xs = x_prompt, x_sample
    kp_l, vp_l, pp_l, cp_l = [], [], [], []
    ks_l, vs_l, ps_l, cs_l = [], [], [], []
    for l in range(DEPTH):
        mix_w = (w_in[l], b_gate[l], sb_bias[l], pool_w[l], pool_scale[l], w_att_o[l], conv_dw[l], conv_b[l],
                 conv_ln_g[l], conv_ln_b[l], conv_pw[l], w_out[l])
        pool0 = jnp.zeros((B, POOL_BUF, W_POOL), xp.dtype)
        conv0 = jnp.zeros((B, CONV_BUF, W_CONV), xp.dtype)
        mp, (k_new, v_new, pool_new, conv_new) = _token_mixer(xp, None, pool0, conv0, 0, *mix_w)
        xp = _layernorm(ALPHA * xp + mp, ln1_g[l], ln1_b[l])
        xp = _layernorm(ALPHA * xp + _ffn(xp, ffn_w_gate[l], ffn_w_up[l], ffn_w_down[l]), ln2_g[l], ln2_b[l])
        kp_l.append(k_new); vp_l.append(v_new); pp_l.append(pool_new); cp_l.append(conv_new)
        k_past = cache_k[l][page_table].reshape(DB, past, N_HEADS, HEAD_DIM)
        v_past = cache_v[l][page_table].reshape(DB, past, N_HEADS, HEAD_DIM)
        ms, (k_new, v_new, pool_new, conv_new) = _token_mixer(xs, (k_past, v_past), state_pool[l],
                                                            state_conv[l], past, *mix_w)
        xs = _layernorm(ALPHA * xs + ms, ln1_g[l], ln1_b[l])
        xs = _layernorm(ALPHA * xs + _ffn(xs, ffn_w_gate[l], ffn_w_up[l], ffn_w_down[l]), ln2_g[l], ln2_b[l])
        ks_l.append(k_new); vs_l.append(v_new); ps_l.append(pool_new); cs_l.append(conv_new)
    return (xp, xs, jnp.stack(kp_l), jnp.stack(vp_l), jnp.stack(pp_l), jnp.stack(cp_l),
            jnp.stack(ks_l), jnp.stack(vs_l), jnp.stack(ps_l), jnp.stack(cs_l))
```

```python
import numpy as np
import concourse.bass as bass
import concourse.mybir as mybir
from concourse.bass_utils import run_bass_kernel_spmd

F32 = mybir.dt.float32
BF16 = mybir.dt.bfloat16
I32 = mybir.dt.int32
ALU = mybir.AluOpType
AF = mybir.ActivationFunctionType
AX = mybir.AxisListType

D = 1024
NIN = 5376
DFF = 2816
T = 4096
CH = 512
NSLOT = 8
NCORES = 4
NSEQ = 32 // NCORES
NS = NSEQ * 4
NPAGE = 64
DEPTH = 2
ALPHA = (2.0 * DEPTH) ** 0.25
EPS = 1e-5
CHUNKS = ([0, 3, 4, 7], [1, 2, 5, 6])
MASKV = 60.0
ENGS = ['pe', 'act', 'dve', 'pool', 'sp']


def chunk_loc(c):
    for r in range(2):
        if c in CHUNKS[r]:
            return r, CHUNKS[r].index(c)
    raise ValueError(c)


class Sched:
    def __init__(self):
        self.ops = {e: [] for e in ENGS}
        self.cnt = {e: 0 for e in ENGS}
        self.waited = {e: {} for e in ENGS}
        self.st = {}
        self.dma_cnt = {}
        self.seq = 0
        self.log = []

    def _split(self, k):
        return k if isinstance(k, tuple) else (k, None)

    def _base(self, b):
        return self.st.setdefault(b, {'w': None, 'r': {}, 'subs': {}})

    def _deps(self, reads, writes):
        deps = []

        def addev(e):
            if e is not None:
                deps.append(e)

        for k in reads:
            b, sub = self._split(k)
            B = self._base(b)
            addev(B['w'])
            if sub is None:
                for S in B['subs'].values():
                    addev(S['w'])
            else:
                S = B['subs'].setdefault(sub, {'w': None, 'r': {}})
                addev(S['w'])
        for k in writes:
            b, sub = self._split(k)
            B = self._base(b)
            addev(B['w'])
            deps.extend(B['r'].items())
            if sub is None:
                for S in B['subs'].values():
                    addev(S['w'])
                    deps.extend(S['r'].items())
            else:
                S = B['subs'].setdefault(sub, {'w': None, 'r': {}})
                addev(S['w'])
                deps.extend(S['r'].items())
        return deps

    def _update(self, reads, writes, ev):
        sk, v = ev
        for k in reads:
            b, sub = self._split(k)
            B = self._base(b)
            R = B['r'] if sub is None else B['subs'].setdefault(sub, {'w': None, 'r': {}})['r']
            R[sk] = max(R.get(sk, 0), v)
        for k in writes:
            b, sub = self._split(k)
            B = self._base(b)
            if sub is None:
                B['w'] = ev
                B['r'] = {}
                B['subs'] = {}
            else:
                S = B['subs'].setdefault(sub, {'w': None, 'r': {}})
                S['w'] = ev
                S['r'] = {}

    def _waits(self, eng, deps):
        need = {}
        for sk, v in deps:
            if sk == 'pe' and eng == 'pe':
                continue
            if isinstance(sk, tuple):
                v = max(v, 16 * self.dma_cnt.get(sk, 0))
            need[sk] = max(need.get(sk, 0), v)
        out = []
        W = self.waited[eng]
        for sk, v in need.items():
            if W.get(sk, 0) < v:
                W[sk] = v
                out.append((sk, v))
        return out

    def op(self, eng, fn, reads=(), writes=()):
        deps = self._deps(reads, writes)
        waits = self._waits(eng, deps)
        self.cnt[eng] += 1
        ev = (eng, self.cnt[eng])
        self.seq += 1
        self.log.append((self.seq, eng, 'op', list(reads), list(writes)))
        self.ops[eng].append((waits, fn, (eng, 1), self.seq))
        self._update(reads, writes, ev)

    def dma(self, q, fn, semkey, reads=(), writes=(), nowaw=False):
        deps = self._deps(reads, writes)
        if nowaw:
            deps = [d for d in deps if d[0] != ('d', semkey)]
        waits = self._waits(q, deps)
        sk = ('d', semkey)
        self.dma_cnt[sk] = self.dma_cnt.get(sk, 0) + 1
        ev = (sk, 16 * self.dma_cnt[sk])
        self.seq += 1
        self.log.append((self.seq, q, 'dma:' + str(semkey), list(reads), list(writes)))
        self.ops[q].append((waits, fn, (sk, 16), self.seq))
        self._update(reads, writes, ev)


def build_program(with_sample=True, n_layers=DEPTH, stop_after=None, npool=2560):
    nc = bass.Bass("TRN2", target_bir_lowering=False)
    S = Sched()

    def din(name, shape, dt=F32):
        return nc.dram_tensor(name, list(shape), dt, kind="ExternalInput").ap()

    def dout(name, shape, dt=F32):
        return nc.dram_tensor(name, list(shape), dt, kind="ExternalOutput").ap()

    def dscr(name, shape, dt):
        return nc.dram_tensor(name, list(shape), dt, kind="Internal").ap()

    xp = din("xp", [T, D])
    xs = din("xs", [NS, D])
    pscale = din("pscale", [128, 2, 16])
    consts_in = din("consts_in", [128, 16])
    w_in = din("w_in", [DEPTH, D, NIN])
    b_gate = din("b_gate", [DEPTH, 3 * D])
    sb_bias = din("sb_bias", [DEPTH, 8])
    pool_w = din("pool_w", [DEPTH, 4, 64, 256])
    pool_scale = din("pool_scale", [DEPTH, D])
    w_att_o = din("w_att_o", [DEPTH, 512, D])
    conv_dw = din("conv_dw", [DEPTH, 31, 256])
    conv_b = din("conv_b", [DEPTH, 256])
    conv_ln_g = din("conv_ln_g", [DEPTH, 256])
    conv_ln_b = din("conv_ln_b", [DEPTH, 256])
    conv_pw = din("conv_pw", [DEPTH, 256, D])
    w_out = din("w_out", [DEPTH, D, D])
    ln1_g = din("ln1_g", [DEPTH, D])
    ln1_b = din("ln1_b", [DEPTH, D])
    ffn_g = din("ffn_w_gate", [DEPTH, D, DFF])
    ffn_u = din("ffn_w_up", [DEPTH, D, DFF])
    ffn_d = din("ffn_w_down", [DEPTH, DFF, D])
    ln2_g = din("ln2_g", [DEPTH, D])
    ln2_b = din("ln2_b", [DEPTH, D])
    if with_sample:
        cache_k = din("cache_k", [DEPTH * npool * 128, 512])
        cache_v = din("cache_v", [DEPTH * npool * 128, 512])
        state_pool = din("state_pool", [DEPTH, NSEQ, 15, 256])
        state_conv = din("state_conv", [DEPTH, NSEQ, 30, 256])
        page_table = din("page_table", [NSEQ, NPAGE], I32)

    y_p = dout("y_p", [T, D])
    k_p = dout("k_p", [DEPTH, T, 512])
    v_p = dout("v_p", [DEPTH, T, 512])
    pool_p = dout("pool_p", [DEPTH, 15, 256])
    conv_p = dout("conv_p", [DEPTH, 30, 256])
    y_s = dout("y_s", [NS, D])
    k_s = dout("k_s", [DEPTH, NS, 512])
    v_s = dout("v_s", [DEPTH, NS, 512])
    pool_s = dout("pool_s", [DEPTH, NSEQ, 15, 256])
    conv_s = dout("conv_s", [DEPTH, NSEQ, 30, 256])

    wb_in = dscr("wb_in", [DEPTH, D, NIN], BF16)
    wb_ao = dscr("wb_ao", [DEPTH, 512, D], BF16)
    wb_pw = dscr("wb_pw", [DEPTH, 256, D], BF16)
    wb_out = dscr("wb_out", [DEPTH, D, D], BF16)
    wb_g = dscr("wb_g", [DEPTH, D, DFF], BF16)
    wb_u = dscr("wb_u", [DEPTH, D, DFF], BF16)
    wb_d = dscr("wb_d", [DEPTH, DFF, D], BF16)
    wb_pool = dscr("wb_pool", [DEPTH, 256, 256], BF16)
    TT = T + NS
    xT_scr = dscr("xT_scr", [D, TT], BF16)
    qT_scr = dscr("qT_scr", [512, TT], BF16)
    KT_loc = dscr("KT_loc", [512, T], BF16)
    V_loc = dscr("V_loc", [T, 512], BF16)
    ug_scr = dscr("ug_scr", [512, TT], F32)
    X1 = dscr("X1", [T, D], F32)
    X1s = dscr("X1s", [NS, D], F32)
    ksT_scr = dscr("ksT_scr", [512, NS], BF16)
    XM = dscr("XM", [TT, D], F32)

    import contextlib
    es = contextlib.ExitStack()

    def sb(name, shape, dt):
        return es.enter_context(nc.sbuf_tensor(name, list(shape), dt))

    ident_b = sb("ident_b", [128, 128], BF16)
    ident_f = sb("ident_f", [128, 128], F32)
    onesm = sb("onesm", [128, 128], F32)
    diagMB = sb("diagMB", [128, 4, 512], BF16)
    par = sb("par", [128, 64], F32)
    nbias = sb("nbias", [128, 5, 2, 8], F32)
    lnt = sb("lnt", [128, 4, D], F32)
    cdw = sb("cdw", [128, 2, 31], F32)
    slabs = [sb(f"slab{i}", [128, 8, 512], BF16) for i in range(4)]
    R32 = [sb(f"R32_{i}", [128, D], F32) for i in range(4)]
    BF8 = sb("BF8", [128, 4, D], BF16)
    xT = sb("xT", [128, 8, CH], BF16)
    BFA = sb("BFA", [128, 22 * CH], BF16)
    ao_sb = sb("ao_sb", [128, 4, D], BF16)
    pw_sb = sb("pw_sb", [128, 2, D], BF16)
    poolw_sb = sb("poolw_sb", [128, 2, 256], BF16)
    pscale_sb = sb("pscale_sb", [128, 2, 16], F32)
    stats = sb("stats", [128, 2, 6], F32)
    mv = sb("mv", [128, 4], F32)
    F32A = sb("F32A", [128, 5440], F32)
    vnew = sb("vnew", [128, NSEQ, 512], BF16)
    invw = sb("invw", [128, 2], F32)
    consts_sb = sb("consts_sb", [128, 16], F32)
    kn = sb("kn", [128, 4, NS], BF16)
    QZ = sb("QZ", [128, 16, 128], BF16)
    ptf_i = sb("ptf_i", [128, NSEQ * NPAGE], I32)
    ptf = sb("ptf", [128, NSEQ * NPAGE], F32)
    pidx = ptf_i
    mnew = sb("mnew", [128, 8], F32)
    Pnew = sb("Pnew", [128, 8], F32)
    anew = sb("anew", [128, 8], BF16)
    anT = sb("anT", [128, 128], BF16)
    rbias = sb("rbias", [128, 8], F32)
    pgs = [sb(f"pg{i}", [128, 512], F32) for i in range(4)]
    aB = [sb(f"aB{i}", [128, 1024], BF16) for i in range(2)]
    aTB = [sb(f"aTB{i}", [128, 1024], BF16) for i in range(2)]
    qT = sb("qT", [128, 4, CH], BF16)
    ST = [sb(f"ST{i}", [128, 512], F32) for i in range(3)]
    STb = [sb(f"STb{i}", [128, 512], BF16) for i in range(3)]
    small = sb("small", [128, 64], F32)
    psF_t = es.enter_context(nc.psum_tensor("psF", [128, 6, 512], F32))
    psB_t = es.enter_context(nc.psum_tensor("psB", [128, 2, 1024], BF16))

    oT_v = BFA[:, 0:4 * CH].rearrange("p (h t) -> p h t", h=4)
    mg_v = BFA[:, 4 * CH:12 * CH].rearrange("p (c t) -> p c t", c=8)
    gt_v = BFA[:, 12 * CH:15 * CH].rearrange("p (c t) -> p c t", c=3)
    pT_v = BFA[:, 15 * CH:17 * CH].rearrange("p (c t) -> p c t", c=2)
    sT_v = BFA[:, 17 * CH:19 * CH].rearrange("p (c t) -> p c t", c=2)
    hT_v = BFA[:, 0:22 * CH].rearrange("p (c t) -> p c t", c=22)
    mB = [F32A[:, i * 1024:(i + 1) * 1024] for i in range(2)]
    PB = [F32A[:, 2048 + i * 1025:2048 + (i + 1) * 1025] for i in range(2)]
    EXT = 30 + CH
    XL = 544
    extu = F32A[:, 0:2 * EXT].rearrange("p (c t) -> p c t", c=2)
    extg = F32A[:, 2 * EXT:4 * EXT].rearrange("p (c t) -> p c t", c=2)
    X0 = 4 * EXT
    cw = [F32A[:, X0 + i * 2 * XL:X0 + (i + 1) * 2 * XL].rearrange("p (c t) -> p c t", c=2) for i in range(2)]
    ptmp = F32A[:, X0 + 4 * XL:X0 + 5 * XL]
    ptmp2 = F32A[:, X0 + 5 * XL:X0 + 6 * XL]
    cacc = [F32A[:, X0 + i * 1024:X0 + (i + 1) * 1024].rearrange("p (c t) -> p c t", c=2) for i in range(2)]
    pg_i = [0]

    def pagebuf():
        i = pg_i[0] % 4
        pg_i[0] += 1
        return pgs[i], f"pg{i}"

    sems = {}

    psf_i = [0]

    def psf():
        i = psf_i[0] % 6
        psf_i[0] += 1
        return psF_t[:, i, :], ('psF', i)

    psb_i = [0]

    def psb():
        i = psb_i[0] % 2
        psb_i[0] += 1
        return psB_t[:, i, :], ('psB', i)

    slab_i = [0]

    def next_slab():
        i = slab_i[0] % 4
        slab_i[0] += 1
        return slabs[i], f"slab{i}"

    def mm(out, lhsT, rhs, start, stop, reads, writes):
        S.op('pe', lambda e: e.matmul(out, lhsT, rhs, start=start, stop=stop), reads, writes)

    def tr(out, in_, ident, reads, writes):
        S.op('pe', lambda e: e.transpose(out, in_, ident), reads, writes)

    def act(out, in_, func, reads, writes, bias=None, scale=None):
        kw = {}
        if bias is not None:
            kw['bias'] = bias
        if scale is not None:
            kw['scale'] = scale
        S.op('act', lambda e: e.activation(out, in_, func, **kw), reads, writes)

    def vop(eng, name, args, reads, writes, **kw):
        S.op(eng, lambda e: getattr(e, name)(*args, **kw), reads, writes)

    def load(out, in_, semkey, reads=(), writes=(), q='sp', nowaw=False):
        S.dma(q, lambda e: e.dma_start(out=out, in_=in_), semkey, reads, writes, nowaw=nowaw)

    def load_nc(out, in_, semkey, reads=(), writes=(), q='sp', nowaw=False):
        S.dma(q, lambda e: e.dma_start(out=out, in_=in_, allow_slow_non_contiguous=True), semkey, reads, writes, nowaw=nowaw)

    S.op('pool', lambda e: e.memset(ident_f[:], 0.0), (), ['ident_f'])
    S.op('pool', lambda e: e.memset(onesm[:], 1.0), (), ['onesm'])
    S.op('pool', lambda e: e.affine_select(ident_f[:], onesm[:], [[-1, 128]], ALU.is_equal, 0.0, base=0, channel_multiplier=1),
         ['onesm'], ['ident_f'])
    vop('dve', 'tensor_copy', (ident_b[:], ident_f[:]), ['ident_f'], ['ident_b'])
    S.op('pool', lambda e: e.memset(ST[0][:], -8.0 * MASKV), (), ['ST0'])
    for i in range(4):
        S.op('pool', lambda e, i=i: e.affine_select(ST[1][:], ST[0][:], [[1, 512]], ALU.is_ge, 0.0, base=-128 * i, channel_multiplier=-1),
             ['ST0'], ['ST1'])
        vop('dve', 'tensor_copy', (diagMB[:, i, :], ST[1][:]), ['ST1'], [('diagMB', i)])
    vop('dve', 'tensor_scalar', (onesm[:], onesm[:], 1.0 / 256.0, None, ALU.mult), ['onesm'], ['onesm'])
    def conv_w(dst, src, rows, l, key):
        step = 128
        for r0 in range(0, rows, step):
            S.dma('pool', lambda e, r0=r0: e.dma_start(out=dst[l, r0:r0 + step, :], in_=src[l, r0:r0 + step, :]),
                  key, (), [(key, l)])

    def convert_layer(l):
        conv_w(wb_in, w_in, D, l, 'wb_in')
        conv_w(wb_ao, w_att_o, 512, l, 'wb_ao')
        conv_w(wb_pw, conv_pw, 256, l, 'wb_pw')
        pw2 = pool_w.rearrange("l g c n -> l (g c) n")
        conv_w(wb_pool, pw2, 256, l, 'wb_pool')
        conv_w(wb_out, w_out, D, l, 'wb_out')
        conv_w(wb_g, ffn_g, D, l, 'wb_g')
        conv_w(wb_u, ffn_u, D, l, 'wb_u')
        conv_w(wb_d, ffn_d, DFF, l, 'wb_d')

    def load_slab(src2d, r0, nk, c0, ncols, wkey):
        sl, sk = next_slab()
        v = src2d[r0:r0 + nk * 128, c0:c0 + ncols].rearrange("(k p) n -> p k n", p=128)
        load(sl[:, 0:nk, 0:ncols], v, sk, [wkey], [sk])
        return sl, sk

    def layer_params(l):
        load_nc(par[:, 0:24], b_gate[l].rearrange("(c p) -> p c", p=128), 'par', (), ['par'])
        load_nc(par[:, 24:32], pool_scale[l].rearrange("(c p) -> p c", p=128), 'par', (), [('par', 1)], nowaw=True)
        load_nc(par[:, 32:34], conv_b[l].rearrange("(c p) -> p c", p=128), 'par', (), [('par', 2)], nowaw=True)
        load_nc(par[:, 34:36], conv_ln_g[l].rearrange("(c p) -> p c", p=128), 'par', (), [('par', 3)], nowaw=True)
        load_nc(par[:, 36:38], conv_ln_b[l].rearrange("(c p) -> p c", p=128), 'par', (), [('par', 4)], nowaw=True)
        load_nc(par[:, 40:48], sb_bias[l:l + 1, :].partition_broadcast(128), 'par', (), [('par', 5)], nowaw=True)
        for c in range(2):
            load_nc(cdw[:, c, :], conv_dw[l][:, c * 128:(c + 1) * 128].rearrange("k p -> p k"), 'cdw', (), ['cdw'], nowaw=(c > 0))
        for i, t in enumerate((ln1_g, ln1_b, ln2_g, ln2_b)):
            load_nc(lnt[:, i, :], t[l:l + 1, :].partition_broadcast(128), 'lnt', (), [('lnt', i)], nowaw=True)
        vop('dve', 'tensor_scalar', (nbias[:, 4, 0, :], par[:, 40:48], -1.0, None, ALU.mult), [('par', 5)], ['nbias'])

    def build_xT(src_rows, ntok, tok0):
        nblk = (ntok + 127) // 128
        for b in range(nblk):
            n = min(128, ntok - b * 128)
            r = R32[b % 4]
            rk = f"R32_{b % 4}"
            load(r[0:n, :], src_rows[b * 128:b * 128 + n, :], rk, (), [rk])
            vop('dve' if b % 2 == 0 else 'pool', 'tensor_copy', (BF8[0:n, b, :], r[0:n, :]), [rk], [('BF8', b)])
            for g in range(2):
                ps, pk = psb()
                for c in range(4):
                    kc = g * 4 + c
                    tr(ps[:, c * 128:c * 128 + n], BF8[0:n, b, kc * 128:(kc + 1) * 128], ident_b[0:n, 0:n],
                       [('BF8', b), 'ident_b'], [pk])
                outv = xT[:, g * 4:(g + 1) * 4, b * 128:b * 128 + n]
                inv = ps[:, 0:512].rearrange("p (c t) -> p c t", c=4)[:, :, 0:n]
                if g == 0:
                    act(outv, inv, AF.Copy, [pk], [('xT', b)])
                else:
                    vop('dve', 'tensor_copy', (outv, inv), [pk], [('xT', b)])
        load(xT_scr[:, tok0:tok0 + ntok].rearrange("(k p) t -> p k t", p=128), xT[:, :, 0:ntok], 'xT', ['xT'], ['xT_scr'])

    RG = [[0, 1], [2, 3], [4, 5], [6, 7]]
    cnt_bar = [0]

    def barrier(keys, eng='dve'):
        c = 56 + (cnt_bar[0] % 8)
        cnt_bar[0] += 1
        vop(eng, 'memset', (small[:, c:c + 1], 0.0), (), list(keys) + [('small', c)])

    st_i = [0]

    def stage32():
        i = st_i[0] % 3
        st_i[0] += 1
        return ST[i], f"ST{i}"

    stb_i = [0]

    def stage16():
        i = stb_i[0] % 3
        stb_i[0] += 1
        return STb[i], f"STb{i}"

    def fm_group(sl, sk, ncols_chunks, ntok, nk=8):
        outs = []
        for nn in ncols_chunks:
            ps, pk = psf()
            for kc in range(nk):
                mm(ps[:, 0:ntok], sl[:, kc, nn * 128:(nn + 1) * 128], xT[:, kc, 0:ntok], kc == 0, kc == nk - 1,
                   [sk, 'xT', 'ident_b'], [pk])
            outs.append((ps, pk))
        return outs

    def rows_ln(y, yk, n, gi, outbuf, outk):
        for hf in range(2):
            vop('dve', 'bn_stats', (stats[0:n, hf, :], y[0:n, hf * 512:(hf + 1) * 512]), [yk], [('stats', hf)])
        vop('dve', 'bn_aggr', (mv[0:n, 0:2], stats[0:n, :, :].rearrange("p a b -> p (a b)")), ['stats'], [('mv', 0)])
        vop('dve', 'tensor_scalar', (mv[0:n, 2:3], mv[0:n, 1:2], EPS, None, ALU.add), [('mv', 0)], [('mv', 1)])
        act(mv[0:n, 2:3], mv[0:n, 2:3], AF.Sqrt, [('mv', 1)], [('mv', 1)])
        vop('dve', 'reciprocal', (mv[0:n, 2:3], mv[0:n, 2:3]), [('mv', 1)], [('mv', 1)])
        vop('dve', 'tensor_scalar', (y[0:n, :], y[0:n, :], mv[0:n, 0:1], mv[0:n, 2:3], ALU.subtract, ALU.mult),
            [yk, ('mv', 0), ('mv', 1)], [yk])
        vop('pool', 'tensor_tensor', (y[0:n, :], y[0:n, :], lnt[0:n, gi, :], ALU.mult), [yk, ('lnt', gi)], [yk])
        vop('pool', 'tensor_tensor', (outbuf[0:n, :], y[0:n, :], lnt[0:n, gi + 1, :], ALU.add), [yk, ('lnt', gi + 1)], [outk])

    def rows_to_xT(r, rk, b, n):
        vop('pool', 'tensor_copy', (BF8[0:n, b, :], r[0:n, :]), [rk], [('BF8', b)])
        for g in range(2):
            ps, pk = psb()
            for c in range(4):
                kc = g * 4 + c
                tr(ps[:, c * 128:c * 128 + n], BF8[0:n, b, kc * 128:(kc + 1) * 128], ident_b[0:n, 0:n],
                   [('BF8', b), 'ident_b'], [pk])
            outv = xT[:, g * 4:(g + 1) * 4, b * 128:b * 128 + n]
            inv = ps[:, 0:512].rearrange("p (c t) -> p c t", c=4)[:, :, 0:n]
            if g == 0:
                act(outv, inv, AF.Copy, [pk], [('xT', b)])
            else:
                vop('dve', 'tensor_copy', (outv, inv), [pk], [('xT', b)])

    def p1(l, x_src, ntok, tok0, slot):
        sample = slot is None
        nblk = (ntok + 127) // 128
        for b in range(nblk):
            n = min(128, ntok - b * 128)
            r, rk = R32[b % 4], f"R32_{b % 4}"
            load(r[0:n, :], x_src[b * 128:b * 128 + n, :], rk, (), [rk])
            rows_to_xT(r, rk, b, n)
        load(xT_scr[:, tok0:tok0 + ntok].rearrange("(k p) t -> p k t", p=128), xT[:, :, 0:ntok], 'xT', ['xT'], ['xT_scr'])
        wl = wb_in[l]
        sl, sk = load_slab(wl, 0, 8, 0, 512, ('wb_in', l))
        outs = fm_group(sl, sk, range(4), ntok)
        for c in range(2):
            ps, pk = outs[c]
            st, stk = stage32()
            act(st[:, 0:ntok], ps[:, 0:ntok], AF.Copy, [pk], [stk])
            load(ug_scr[c * 128:(c + 1) * 128, tok0:tok0 + ntok], st[:, 0:ntok], stk, [stk], [('ug_scr', 'u')], nowaw=True)
        for c in range(2):
            ps, pk = outs[2 + c]
            st, stk = stage16()
            vop('dve', 'tensor_copy', (st[:, 0:ntok], ps[:, 0:ntok]), [pk], [stk])
            load(qT_scr[c * 128:(c + 1) * 128, tok0:tok0 + ntok], st[:, 0:ntok], stk, [stk], ['qT_scr'])
        last_b = nblk - 1
        nl = min(128, ntok - last_b * 128)
        if sample or slot == NSLOT - 1:
            ps, pk = psf()
            for kc in range(8):
                mm(ps[0:nl, 0:256], xT[:, kc, last_b * 128:last_b * 128 + nl], sl[:, kc, 0:256], kc == 0, kc == 7, [sk, 'xT'], [pk])
            st, stk = stage32()
            act(st[0:nl, 0:256], ps[0:nl, 0:256], AF.Copy, [pk], [stk])
            if sample:
                for s_ in range(NSEQ):
                    load(pool_s[l, s_, 11:15, :], st[4 * s_:4 * s_ + 4, 0:256], stk, [stk], ['pool_s'], nowaw=True)
            else:
                load(pool_p[l, :, :], st[128 - 15:128, 0:256], stk, [stk], ['pool_p'])
        sl, sk = load_slab(wl, 0, 8, 512, 256, ('wb_in', l))
        outs = fm_group(sl, sk, range(2), ntok)
        for c in range(2):
            ps, pk = outs[c]
            st, stk = stage16()
            vop('dve', 'tensor_copy', (st[:, 0:ntok], ps[:, 0:ntok]), [pk], [stk])
            load(qT_scr[(2 + c) * 128:(3 + c) * 128, tok0:tok0 + ntok], st[:, 0:ntok], stk, [stk], ['qT_scr'])
        sl, sk = load_slab(wl, 0, 8, 768, 512, ('wb_in', l))
        outs = fm_group(sl, sk, range(4), ntok)
        for c in range(4):
            ps, pk = outs[c]
            st, stk = stage16()
            act(st[:, 0:ntok], ps[:, 0:ntok], AF.Copy, [pk], [stk])
            if sample:
                load_nc(ksT_scr[c * 128:(c + 1) * 128, :], st[:, 0:ntok], stk, [stk], ['ksT_scr'])
            else:
                load(KT_loc[c * 128:(c + 1) * 128, tok0:tok0 + ntok], st[:, 0:ntok], stk, [stk], ['KT_loc'])
        kout = k_s if sample else k_p
        vout = v_s if sample else v_p
        for b in range(nblk):
            n = min(128, ntok - b * 128)
            ps, pk = psf()
            for kc in range(8):
                mm(ps[0:n, :], xT[:, kc, b * 128:b * 128 + n], sl[:, kc, :], kc == 0, kc == 7, [sk, 'xT'], [pk])
            st, stk = stage32()
            act(st[0:n, :], ps[0:n, :], AF.Copy, [pk], [stk])
            load(kout[l, tok0 - (T if sample else 0) + b * 128:tok0 - (T if sample else 0) + b * 128 + n, :], st[0:n, :], stk, [stk], ['kout'])
        sl, sk = load_slab(wl, 0, 8, 1280, 512, ('wb_in', l))
        for b in range(nblk):
            n = min(128, ntok - b * 128)
            ps, pk = psf()
            for kc in range(8):
                mm(ps[0:n, :], xT[:, kc, b * 128:b * 128 + n], sl[:, kc, :], kc == 0, kc == 7, [sk, 'xT'], [pk])
            st, stk = stage32()
            act(st[0:n, :], ps[0:n, :], AF.Copy, [pk], [stk])
            t0_ = tok0 - (T if sample else 0) + b * 128
            load(vout[l, t0_:t0_ + n, :], st[0:n, :], stk, [stk], ['vout'])
            if not sample:
                sb_, sbk = stage16()
                vop('dve', 'tensor_copy', (sb_[0:n, :], st[0:n, :]), [stk], [sbk])
                load(V_loc[tok0 + b * 128:tok0 + b * 128 + n, :], sb_[0:n, :], sbk, [sbk], ['V_loc'])
        if sample:
            for s_ in range(NSEQ):
                ps, pk = psf()
                for kc in range(8):
                    mm(ps[0:4, :], xT[:, kc, 4 * s_:4 * s_ + 4], sl[:, kc, :], kc == 0, kc == 7, [sk, 'xT'], [pk])
                vop('dve', 'tensor_copy', (vnew[0:4, s_, :], ps[0:4, :]), [pk], [('vnew', s_)])
        sl, sk = load_slab(wl, 0, 8, 1792, 512, ('wb_in', l))
        outs = fm_group(sl, sk, range(4), ntok)
        for c in range(2):
            pa, pak = outs[c]
            pg, pgk = outs[2 + c]
            st, stk = stage32()
            act(st[:, 0:ntok], pg[:, 0:ntok], AF.Sigmoid, [pgk], [stk])
            st2, st2k = stage32()
            vop('dve', 'tensor_tensor', (st2[:, 0:ntok], pa[:, 0:ntok], st[:, 0:ntok], ALU.mult), [pak, stk], [st2k])
            load(ug_scr[256 + c * 128:256 + (c + 1) * 128, tok0:tok0 + ntok], st2[:, 0:ntok], st2k, [st2k], [('ug_scr', 'g')], nowaw=True)
        if sample or slot == NSLOT - 1:
            ps, pk = psf()
            for kc in range(8):
                mm(ps[0:nl, :], xT[:, kc, last_b * 128:last_b * 128 + nl], sl[:, kc, :], kc == 0, kc == 7, [sk, 'xT'], [pk])
            st, stk = stage32()
            act(st[0:nl, 0:256], ps[0:nl, 256:512], AF.Sigmoid, [pk], [stk])
            st2, st2k = stage32()
            vop('dve', 'tensor_tensor', (st2[0:nl, 0:256], ps[0:nl, 0:256], st[0:nl, 0:256], ALU.mult), [pk, stk], [st2k])
            if sample:
                for s_ in range(NSEQ):
                    load(conv_s[l, s_, 26:30, :], st2[4 * s_:4 * s_ + 4, 0:256], st2k, [st2k], ['conv_s'], nowaw=True)
            else:
                load(conv_p[l, :, :], st2[128 - 30:128, 0:256], st2k, [st2k], ['conv_p'])

    def attention(l, j, tok0):
        nchunk = j + 1
        load(qT[:, :, :], qT_scr[:, tok0:tok0 + CH].rearrange("(c p) t -> p c t", p=128), 'qT', ['qT_scr'], ['qT'])
        npiece = (nchunk + 1) // 2
        P = []
        for pr in range(4):
            for hh in range(2):
                for i in range(4):
                    for pc in range(npiece - 1, -1, -1):
                        P.append(dict(pr=pr, hh=hh, i=i, pc=pc, first=(pc == npiece - 1), last=(pc == 0), chain=(pr * 2 + hh) * 4 + i))
        slabs_pr = {}

        def get_slabs(pr):
            if pr not in slabs_pr:
                ksl, kk = next_slab()
                vsl, vk = next_slab()
                vview = vsl[:, :, :].rearrange("p a b -> p (a b)").rearrange("p (k c) -> p k c", c=128)
                load(ksl[:, 0:nchunk, :], KT_loc[pr * 128:(pr + 1) * 128, 0:nchunk * CH].rearrange("p (c t) -> p c t", t=CH), kk, ['KT_loc'], [kk])
                for c in range(nchunk):
                    load_nc(vview[:, 4 * c:4 * c + 4, :],
                            V_loc[c * CH:(c + 1) * CH, pr * 128:(pr + 1) * 128].rearrange("(b p) n -> p b n", p=128),
                            vk, ['V_loc'], [vk], nowaw=(c > 0))
                slabs_pr[pr] = (ksl, kk, vview, vk)
            return slabs_pr[pr]

        def keys(n):
            s_ = n % 2
            return s_, ('F32A', f'm{s_}'), ('F32A', f'P{s_}'), f'aB{s_}', f'aTB{s_}'

        def A1(n):
            p = P[n]
            pr, hh, i, pc = p['pr'], p['hh'], p['i'], p['pc']
            ksl, kk, vview, vk = get_slabs(pr)
            h = 2 * pr + hh
            hs = slice(hh * 64, (hh + 1) * 64)
            s_, mk, Pk, ak, atk = keys(n)
            nt = min(2, nchunk - 2 * pc)
            for tl in range(nt - 1, -1, -1):
                c = 2 * pc + tl
                zi = psf_i[0] % 4
                psf_i[0] += 1
                ps, pk = psF_t[:, zi, :], ('psF', zi)
                masked = (c == j)
                mm(ps, qT[hs, pr, i * 128:(i + 1) * 128], ksl[hs, c, :], True, not masked, ['qT', kk], [pk])
                if masked:
                    mm(ps, ident_b[:, :], diagMB[:, i, :], False, True, ['ident_b', 'diagMB'], [pk])
                act(mB[s_][:, tl * 512:(tl + 1) * 512], ps, AF.Sigmoid, [pk, 'nbias'], [mk],
                    bias=nbias[:, 4, 0, h:h + 1], scale=-0.125)

        def A2(n):
            p = P[n]
            pc = p['pc']
            s_, mk, Pk, ak, atk = keys(n)
            nt = min(2, nchunk - 2 * pc)
            Lp = nt * CH
            if p['first']:
                vop('dve', 'memset', (PB[s_][:, Lp:Lp + 1], 1.0), (), [Pk])
                init = 1.0
                rd = [mk]
            else:
                prev = (n - 1) % 2
                vop('dve', 'tensor_copy', (PB[s_][:, Lp:Lp + 1], PB[prev][:, 0:1]), [('F32A', f'P{prev}')], [Pk])
                init = PB[prev][:, 0:1]
                rd = [mk, ('F32A', f'P{prev}')]
            vop('dve', 'tensor_tensor_scan', (PB[s_][:, 0:Lp][:, ::-1], mB[s_][:, 0:Lp][:, ::-1], mB[s_][:, 0:Lp][:, ::-1], init, ALU.mult, ALU.min),
                rd, [Pk])
            vop('pool', 'tensor_tensor', (aB[s_][:, 0:Lp], PB[s_][:, 1:Lp + 1], PB[s_][:, 0:Lp], ALU.subtract), [Pk], [ak])

        def Bst(n):
            p = P[n]
            pr, hh, i, pc = p['pr'], p['hh'], p['i'], p['pc']
            ksl, kk, vview, vk = get_slabs(pr)
            hs = slice(hh * 64, (hh + 1) * 64)
            s_, mk, Pk, ak, atk = keys(n)
            nt = min(2, nchunk - 2 * pc)
            Lp = nt * CH
            oi = 4 + (p['chain'] % 2)
            ops_, opk = psF_t[hs, oi, 0:128], ('psF', oi)
            ps, pk = psb()
            for k in range(4 * nt):
                tr(ps[:, k * 128:(k + 1) * 128], aB[s_][:, k * 128:(k + 1) * 128], ident_b[:, :], [ak, 'ident_b'], [pk])
            act(aTB[s_][:, 0:Lp], ps[:, 0:Lp], AF.Copy, [pk], [atk])
            for k in range(4 * nt):
                blk = pc * 8 + k
                lastmm = (p['last'] and k == 4 * nt - 1)
                mm(ops_, vview[:, blk, hs], aTB[s_][:, k * 128:(k + 1) * 128], p['first'] and k == 0, lastmm, [vk, atk], [opk])
            if p['last']:
                vop('dve', 'tensor_copy', (oT_v[hs, pr, i * 128:(i + 1) * 128], ops_), [opk], [('BFA', 'oT')])

        N = len(P)
        for n in range(N + 2):
            if n < N:
                A1(n)
            if 1 <= n <= N:
                A2(n - 1)
            if n >= 2:
                Bst(n - 2)

    def poolconv(n, off, first16=False):
        L = 30 + n
        s2, s4 = cw[0], cw[1]
        vop('dve', 'tensor_tensor', (s2[:, :, 1:L], extu[:, :, 1:L], extu[:, :, 0:L - 1], ALU.add), [('F32A', 'extu')], [('F32A', 'cw0')])
        vop('dve', 'tensor_tensor', (s4[:, :, 3:L], s2[:, :, 3:L], s2[:, :, 1:L - 2], ALU.add), [('F32A', 'cw0')], [('F32A', 'cw1')])
        tk = ('F32A', 'tmp')
        vop('dve', 'tensor_tensor', (ptmp[:, 7:L], s4[:, 1, 7:L], s4[:, 1, 3:L - 4], ALU.add), [('F32A', 'cw1')], [tk])
        vop('dve', 'tensor_tensor', (ptmp2[:, 15:L], ptmp[:, 15:L], ptmp[:, 7:L - 8], ALU.add), [tk], [('F32A', 'tmp2')])
        srcs = [(s2, 0, 0, ('F32A', 'cw0')), (s4, 0, 1, ('F32A', 'cw1')), (None, 1, 0, tk), (None, 1, 1, ('F32A', 'tmp2'))]
        for g, (sbuf_, c, hh, key) in enumerate(srcs):
            hs = slice(hh * 64, (hh + 1) * 64)
            src = sbuf_[hs, c, 30:L] if sbuf_ is not None else (ptmp[hs, 30:L] if g == 2 else ptmp2[hs, 30:L])
            if first16:
                st, stk = stage32()
                vop('dve', 'tensor_tensor', (st[hs, 0:16], (sbuf_[hs, c, 30:46] if sbuf_ is not None else (ptmp[hs, 30:46] if g == 2 else ptmp2[hs, 30:46])),
                                             pscale_sb[hs, c, :], ALU.mult), [key, 'pscale_sb'], [stk])
                vop('dve', 'tensor_tensor', (pT_v[hs, c, off:off + 16], st[hs, 0:16], extu[hs, c, 30:46], ALU.subtract),
                    [stk, ('F32A', 'extu')], [('BFA', 'pT')])
                vop('dve', 'scalar_tensor_tensor', (pT_v[hs, c, off + 16:off + n], src[:, 16:n], invw[hs, c:c + 1], extu[hs, c, 46:L], ALU.mult, ALU.subtract),
                    [key, ('F32A', 'extu'), 'invw'], [('BFA', 'pT')])
            else:
                vop('dve', 'scalar_tensor_tensor', (pT_v[hs, c, off:off + n], src, invw[hs, c:c + 1], extu[hs, c, 30:L], ALU.mult, ALU.subtract),
                    [key, ('F32A', 'extu'), 'invw'], [('BFA', 'pT')])
        barrier(['F32A'])
        for c in range(2):
            eng = 'dve'
            for k in range(31):
                dst = cacc[k % 2][:, c, 0:n]
                dk = ('F32A', f'cacc{k % 2}_{c}')
                if k == 0:
                    vop(eng, 'tensor_scalar', (dst, extg[:, c, 0:n], cdw[:, c, 0:1], None, ALU.mult), [('F32A', 'extg'), 'cdw'], [dk])
                else:
                    srck = ('F32A', f'cacc{(k - 1) % 2}_{c}')
                    vop(eng, 'scalar_tensor_tensor', (dst, extg[:, c, k:k + n], cdw[:, c, k:k + 1], cacc[(k - 1) % 2][:, c, 0:n], ALU.mult, ALU.add),
                        [('F32A', 'extg'), 'cdw', srck], [dk])
            vop(eng, 'tensor_scalar', (cacc[0][:, c, 0:n], cacc[0][:, c, 0:n], par[:, 32 + c:33 + c], None, ALU.add),
                [('F32A', f'cacc0_{c}'), ('par', 2)], [('F32A', f'cacc0_{c}')])
            act(cacc[1][:, c, 0:n], cacc[0][:, c, 0:n], AF.Square, [('F32A', f'cacc0_{c}')], [('F32A', f'cacc1_{c}')])
        pm, pmk = psf()
        pe2, pe2k = psf()
        for c in range(2):
            mm(pm[:, 0:n], onesm[:, :], cacc[0][:, c, 0:n], c == 0, c == 1, ['onesm', ('F32A', f'cacc0_{c}')], [pmk])
        for c in range(2):
            mm(pe2[:, 0:n], onesm[:, :], cacc[1][:, c, 0:n], c == 0, c == 1, ['onesm', ('F32A', f'cacc1_{c}')], [pe2k])
        st, stk = stage32()
        act(st[:, 0:n], pm[:, 0:n], AF.Copy, [pmk], [stk])
        st2, st2k = stage32()
        vop('dve', 'tensor_tensor', (st2[:, 0:n], st[:, 0:n], st[:, 0:n], ALU.mult), [stk], [st2k])
        vop('dve', 'tensor_tensor', (st2[:, 0:n], pe2[:, 0:n], st2[:, 0:n], ALU.subtract), [pe2k, st2k], [st2k])
        vop('dve', 'tensor_scalar', (st2[:, 0:n], st2[:, 0:n], EPS, None, ALU.add), [st2k], [st2k])
        act(st2[:, 0:n], st2[:, 0:n], AF.Sqrt, [st2k], [st2k])
        vop('dve', 'reciprocal', (st2[:, 0:n], st2[:, 0:n]), [st2k], [st2k])
        for c in range(2):
            ck = ('F32A', f'cacc0_{c}')
            vop('dve', 'tensor_tensor', (cacc[0][:, c, 0:n], cacc[0][:, c, 0:n], st[:, 0:n], ALU.subtract), [ck, stk], [ck])
            vop('dve', 'tensor_tensor', (cacc[0][:, c, 0:n], cacc[0][:, c, 0:n], st2[:, 0:n], ALU.mult), [ck, st2k], [ck])
            act(sT_v[:, c, off:off + n], cacc[0][:, c, 0:n], AF.Silu, [ck, ('par', 3), ('par', 4)], [('BFA', 'sT')],
                bias=par[:, 36 + c:37 + c], scale=par[:, 34 + c:35 + c])

    def layer_weights(l):
        load(ao_sb[:, :, :], wb_ao[l].rearrange("(h p) n -> p h n", p=128), 'ao_sb', [('wb_ao', l)], ['ao_sb'])
        load(pw_sb[:, :, :], wb_pw[l].rearrange("(c p) n -> p c n", p=128), 'pw_sb', [('wb_pw', l)], ['pw_sb'])
        load_nc(poolw_sb[:, :, :], wb_pool[l].rearrange("(c p) n -> p c n", p=128), 'poolw_sb', [('wb_pool', l)], ['poolw_sb'])

    def mixer_ffn(l, x_src, xm_scr, out_dst, ntok, tok0):
        nblk = (ntok + 127) // 128
        wl = wb_in[l]
        load(xT[:, :, 0:ntok], xT_scr[:, tok0:tok0 + ntok].rearrange("(k p) t -> p k t", p=128), 'xT', ['xT_scr'], ['xT'])
        for grp in range(2):
            gsl = [load_slab(wl, 0, 8, 2304 + b_ * 1024 + grp * 512, 512, ('wb_in', l)) for b_ in range(3)]
            for nn in range(4):
                nch = grp * 4 + nn
                for b_ in range(3):
                    sl, sk = gsl[b_]
                    ps, pk = psf()
                    for kc in range(8):
                        mm(ps[:, 0:ntok], sl[:, kc, nn * 128:(nn + 1) * 128], xT[:, kc, 0:ntok], kc == 0, kc == 7, [sk, 'xT'], [pk])
                    act(gt_v[:, b_, 0:ntok], ps[:, 0:ntok], AF.Sigmoid, [pk, 'par'], [('BFA', f'gt{b_}')], bias=par[:, b_ * 8 + nch:b_ * 8 + nch + 1])
                g = nch // 2
                hs = slice((g % 2) * 64, (g % 2) * 64 + 64)
                pa, pak = psf()
                mm(pa[:, 0:ntok], poolw_sb[hs, g // 2, (nch % 2) * 128:(nch % 2) * 128 + 128], pT_v[hs, g // 2, 0:ntok], True, True,
                   ['poolw_sb', ('BFA', 'pT')], [pak])
                pb, pbk = psf()
                for pr in range(4):
                    mm(pb[:, 0:ntok], ao_sb[:, pr, nch * 128:(nch + 1) * 128], oT_v[:, pr, 0:ntok], pr == 0, pr == 3, ['ao_sb', ('BFA', 'oT')], [pbk])
                pc_, pck = psf()
                for c in range(2):
                    mm(pc_[:, 0:ntok], pw_sb[:, c, nch * 128:(nch + 1) * 128], sT_v[:, c, 0:ntok], c == 0, c == 1, ['pw_sb', ('BFA', 'sT')], [pck])
                t1, t1k = stage32()
                vop('dve', 'scalar_tensor_tensor', (t1[:, 0:ntok], pa[:, 0:ntok], par[:, 24 + nch:25 + nch], gt_v[:, 0, 0:ntok], ALU.mult, ALU.mult),
                    [pak, ('par', 1), ('BFA', 'gt0')], [t1k])
                t2, t2k = stage32()
                vop('dve', 'tensor_tensor', (t2[:, 0:ntok], pb[:, 0:ntok], gt_v[:, 1, 0:ntok], ALU.mult), [pbk, ('BFA', 'gt1')], [t2k])
                vop('pool', 'tensor_tensor', (t1[:, 0:ntok], t1[:, 0:ntok], t2[:, 0:ntok], ALU.add), [t1k, t2k], [t1k])
                t3, t3k = stage32()
                vop('dve', 'tensor_tensor', (t3[:, 0:ntok], pc_[:, 0:ntok], gt_v[:, 2, 0:ntok], ALU.mult), [pck, ('BFA', 'gt2')], [t3k])
                vop('pool', 'tensor_tensor', (mg_v[:, nch, 0:ntok], t1[:, 0:ntok], t3[:, 0:ntok], ALU.add), [t1k, t3k], [('BFA', f'mg{nch}')])
        wsl = [load_slab(wb_out[l], 0, 8, hf * 512, 512, ('wb_out', l)) for hf in range(2)]
        mgk = [('BFA', f'mg{k}') for k in range(8)]
        for b in range(nblk):
            n = min(128, ntok - b * 128)
            xr, xrk = R32[0], 'R32_0'
            load(xr[0:n, :], x_src[b * 128:b * 128 + n, :], xrk, (), [xrk])
            yr, yrk = R32[1], 'R32_1'
            for hf in range(2):
                sl, sk = wsl[hf]
                ps, pk = psf()
                for kc in range(8):
                    mm(ps[0:n, :], mg_v[:, kc, b * 128:b * 128 + n], sl[:, kc, :], kc == 0, kc == 7, [sk] + mgk, [pk])
                vop('dve', 'scalar_tensor_tensor', (yr[0:n, hf * 512:(hf + 1) * 512], xr[0:n, hf * 512:(hf + 1) * 512], ALPHA, ps[0:n, :], ALU.mult, ALU.add),
                    [xrk, pk], [yrk])
            om, omk = R32[2 + (b % 2)], f"R32_{2 + (b % 2)}"
            rows_ln(yr, yrk, n, 0, om, omk)
            load(xm_scr[tok0 + b * 128:tok0 + b * 128 + n, :], om[0:n, :], omk, [omk], ['xm_scr'])
            rows_to_xT(om, omk, b, n)
        barrier(['BFA'])
        nsl = (DFF + 511) // 512
        for si in range(nsl):
            ncol = min(512, DFF - si * 512)
            gs, gk = load_slab(wb_g[l], 0, 8, si * 512, ncol, ('wb_g', l))
            us, uk = load_slab(wb_u[l], 0, 8, si * 512, ncol, ('wb_u', l))
            for nn in range(ncol // 128):
                ch = si * 4 + nn
                pg, pgk = psf()
                for kc in range(8):
                    mm(pg[:, 0:ntok], gs[:, kc, nn * 128:(nn + 1) * 128], xT[:, kc, 0:ntok], kc == 0, kc == 7, [gk, 'xT'], [pgk])
                pu, puk = psf()
                for kc in range(8):
                    mm(pu[:, 0:ntok], us[:, kc, nn * 128:(nn + 1) * 128], xT[:, kc, 0:ntok], kc == 0, kc == 7, [uk, 'xT'], [puk])
                st, stk = stage32()
                act(st[:, 0:ntok], pg[:, 0:ntok], AF.Silu, [pgk], [stk])
                vop('dve', 'tensor_tensor', (hT_v[:, ch, 0:ntok], pu[:, 0:ntok], st[:, 0:ntok], ALU.mult), [puk, stk], [('BFA', f'h{ch}')])
        hk = [('BFA', f'h{k}') for k in range(22)]
        for hf in range(2):
            for kg in range(3):
                nk = 8 if kg < 2 else 6
                sl, sk = load_slab(wb_d[l], kg * 1024, nk, hf * 512, 512, ('wb_d', l))
                for b in range(nblk):
                    n = min(128, ntok - b * 128)
                    ps, pk = psF_t[:, b, :], ('psF', b)
                    for k in range(nk):
                        kc = kg * 8 + k
                        mm(ps[0:n, :], hT_v[:, kc, b * 128:b * 128 + n], sl[:, k, :], kc == 0, kc == 21, [sk] + hk, [pk])
            for b in range(nblk):
                n = min(128, ntok - b * 128)
                ps, pk = psF_t[:, b, :], ('psF', b)
                st, stk = stage32()
                load(st[0:n, :], xm_scr[tok0 + b * 128:tok0 + b * 128 + n, hf * 512:(hf + 1) * 512], stk, ['xm_scr'], [stk])
                yr, yrk = R32[b], f"R32_{b}"
                vop('dve', 'scalar_tensor_tensor', (yr[0:n, hf * 512:(hf + 1) * 512], st[0:n, :], ALPHA, ps[0:n, :], ALU.mult, ALU.add),
                    [stk, pk], [(yrk, hf)])
        for b in range(nblk):
            n = min(128, ntok - b * 128)
            yr, yrk = R32[b], f"R32_{b}"
            rows_ln(yr, yrk, n, 2, yr, yrk)
            load(out_dst[b * 128:b * 128 + n, :], yr[0:n, :], yrk, [yrk], ['out_dst'])

    def load_ext_prompt(j, tok0):
        for c in range(2):
            if j == 0:
                vop('pool', 'memset', (extu[:, c, 0:30], 0.0), (), [('F32A', 'extu')])
                vop('pool', 'memset', (extg[:, c, 0:30], 0.0), (), [('F32A', 'extg')])
                load(extu[:, c, 30:30 + CH], ug_scr[c * 128:(c + 1) * 128, 0:CH], 'extld', [('ug_scr', 'u')], [('F32A', 'extu')], nowaw=True)
                load(extg[:, c, 30:30 + CH], ug_scr[256 + c * 128:256 + (c + 1) * 128, 0:CH], 'extld', [('ug_scr', 'g')], [('F32A', 'extg')], nowaw=True)
            else:
                load(extu[:, c, 0:30 + CH], ug_scr[c * 128:(c + 1) * 128, tok0 - 30:tok0 + CH], 'extld', [('ug_scr', 'u')], [('F32A', 'extu')], nowaw=True)
                load(extg[:, c, 0:30 + CH], ug_scr[256 + c * 128:256 + (c + 1) * 128, tok0 - 30:tok0 + CH], 'extld', [('ug_scr', 'g')], [('F32A', 'extg')], nowaw=True)

    def load_ext_sample(l, s_):
        vop('pool', 'memset', (extu[:, :, 0:30], 0.0), (), [('F32A', 'extu')])
        for c in range(2):
            load_nc(extu[:, c, 15:30], state_pool[l, s_, :, c * 128:(c + 1) * 128].rearrange("t p -> p t"), 'extld', (), [('F32A', 'extu')], nowaw=True)
            load_nc(extg[:, c, 0:30], state_conv[l, s_, :, c * 128:(c + 1) * 128].rearrange("t p -> p t"), 'extld', (), [('F32A', 'extg')], nowaw=True)
            load_nc(extu[:, c, 30:34], ug_scr[c * 128:(c + 1) * 128, T + 4 * s_:T + 4 * s_ + 4], 'extld', [('ug_scr', 'u')], [('F32A', 'extu')], nowaw=True)
            load_nc(extg[:, c, 30:34], ug_scr[256 + c * 128:256 + (c + 1) * 128, T + 4 * s_:T + 4 * s_ + 4], 'extld', [('ug_scr', 'g')], [('F32A', 'extg')], nowaw=True)

    def sample_prep(l):
        load_nc(ptf_i[:, :], page_table[:, :].rearrange("s j -> (s j)").partition_broadcast(128), 'ptf', (), ['ptf_i'])
        vop('dve', 'tensor_copy', (ptf[:, :], ptf_i[:, :]), ['ptf_i'], ['ptf'])
        vop('dve', 'tensor_scalar', (ptf[:, :], ptf[:, :], 128.0, float(l * npool * 128), ALU.mult, ALU.add), ['ptf'], ['ptf'])
        vop('dve', 'tensor_scalar', (ptf[:, :], ptf[:, :], consts_sb[:, 12:13], None, ALU.add), ['ptf', 'consts_sb'], ['ptf'])
        vop('dve', 'tensor_copy', (pidx[:, :], ptf[:, :]), ['ptf'], ['ptf_i'])
        sk0 = ('small', 0)
        vop('dve', 'tensor_tensor', (small[:, 0:8], consts_sb[:, 0:8], nbias[:, 4, 0, :], ALU.mult), ['consts_sb', 'nbias'], [sk0])
        vop('dve', 'tensor_tensor', (small[:, 0:4], small[:, 0:4], small[:, 4:8], ALU.add), [sk0], [sk0])
        vop('dve', 'tensor_tensor', (small[:, 0:2], small[:, 0:2], small[:, 2:4], ALU.add), [sk0], [sk0])
        vop('dve', 'tensor_tensor', (rbias[:, 0:1], small[:, 0:1], small[:, 1:2], ALU.add), [sk0], ['rbias'])

    def sample_attention(l, g):
        oacc = [R32[2 + s_ // 2][:, (s_ % 2) * 512:(s_ % 2 + 1) * 512] for s_ in range(4)]
        oacck = [(f"R32_{2 + s_ // 2}", s_ % 2) for s_ in range(4)]
        t0s = T + 16 * g
        load_nc(qT[:, :, 0:16], qT_scr[:, t0s:t0s + 16].rearrange("(c p) t -> p c t", p=128), 'qT', ['qT_scr'], ['qT'])
        load_nc(kn[:, :, 0:16], ksT_scr[:, 16 * g:16 * g + 16].rearrange("(c p) t -> p c t", p=128), 'kn', ['ksT_scr'], ['kn'])
        vop('pool', 'memset', (QZ[:, :, :], 0.0), (), ['QZ'])
        for s_ in range(4):
            for pr in range(4):
                for hh in range(2):
                    r0 = s_ * 32 + (2 * pr + hh) * 4
                    vop('dve', 'tensor_copy', (QZ[hh * 64:(hh + 1) * 64, s_ * 4 + pr, r0:r0 + 4], qT[hh * 64:(hh + 1) * 64, pr, 4 * s_:4 * s_ + 4]),
                        ['qT', 'QZ'], [('QZ', s_ * 4 + pr)])
        ps, pk = psF_t[:, 0, 0:4], ('psF', 0)
        n_mm = 0
        for s_ in range(4):
            for pr in range(4):
                mm(ps, QZ[:, s_ * 4 + pr, :], kn[:, pr, 4 * s_:4 * s_ + 4], n_mm == 0, n_mm == 15, [('QZ', s_ * 4 + pr), 'kn'], [pk])
                n_mm += 1
        st, stk = stage32()
        vop('dve', 'tensor_tensor', (st[:, 0:4], ps, consts_sb[:, 8:12], ALU.add), [pk, 'consts_sb'], [stk])
        act(mnew[:, 0:4], st[:, 0:4], AF.Sigmoid, [stk, 'rbias'], ['mnew'], bias=rbias[:, 0:1], scale=-0.125)
        vop('dve', 'memset', (Pnew[:, 4:5], 1.0), (), ['Pnew'])
        vop('dve', 'tensor_tensor_scan', (Pnew[:, 0:4][:, ::-1], mnew[:, 0:4][:, ::-1], mnew[:, 0:4][:, ::-1], 1.0, ALU.mult, ALU.min),
            ['mnew'], ['Pnew'])
        vop('dve', 'tensor_tensor', (anew[:, 0:4], Pnew[:, 1:5], Pnew[:, 0:4], ALU.subtract), ['Pnew'], ['anew'])
        pst, pstk = psb()
        tr(pst[0:4, 0:128], anew[:, 0:4], ident_b[:, :], ['anew', 'ident_b'], [pstk])
        vop('dve', 'tensor_copy', (anT[0:4, :], pst[0:4, 0:128]), [pstk], ['anT'])
        for s_ in range(4):
            ps2, pk2 = psf()
            mm(ps2[0:32, :], anT[0:4, s_ * 32:(s_ + 1) * 32], vnew[0:4, 4 * g + s_, :], True, True, ['anT', ('vnew', 4 * g + s_)], [pk2])
            vop('dve', 'tensor_copy', (oacc[s_][0:32, :], ps2[0:32, :]), [pk2], [oacck[s_]])
        prevP = ('new', None)
        sidx = [0]
        for pc in range(7, -1, -1):
            s2_ = sidx[0] % 2
            sidx[0] += 1
            mk, Pk, ak, atk = ('F32A', f'm{s2_}'), ('F32A', f'P{s2_}'), f'aB{s2_}', f'aTB{s2_}'
            for tl in (1, 0):
                zi = 1 + tl
                psz, pzk = psF_t[:, zi, :], ('psF', zi)
                n_mm = 0
                for s_ in range(4):
                    sg = 4 * g + s_
                    ksl, kk = next_slab()
                    kv = ksl[:, 0:4, :]
                    for pg4 in range(4):
                        jpage = pc * 8 + tl * 4 + pg4
                        kpg, kpk = pagebuf()
                        col = sg * NPAGE + jpage
                        S.dma('pool', lambda e, kpg=kpg, col=col: e.indirect_dma_start(
                            out=kpg[:, :], out_offset=None, in_=cache_k[:, :],
                            in_offset=bass.IndirectOffsetOnAxis(ap=pidx[:, col:col + 1], axis=0)), kpk, ['ptf_i'], [kpk])
                        kb, kbk = stage16()
                        vop('dve', 'tensor_copy', (kb[:, :], kpg[:, :]), [kpk], [kbk])
                        pt_, ptk = psb()
                        for pr in range(4):
                            tr(pt_[:, pr * 128:(pr + 1) * 128], kb[:, pr * 128:(pr + 1) * 128], ident_b[:, :], [kbk, 'ident_b'], [ptk])
                        act(kv[:, :, pg4 * 128:(pg4 + 1) * 128], pt_[:, 0:512].rearrange("p (c t) -> p c t", c=4), AF.Copy, [ptk], [(kk, pg4)])
                    for pr in range(4):
                        mm(psz, QZ[:, s_ * 4 + pr, :], kv[:, pr, :], n_mm == 0, n_mm == 15, [('QZ', s_ * 4 + pr), kk], [pzk])
                        n_mm += 1
                act(mB[s2_][:, tl * 512:(tl + 1) * 512], psz, AF.Sigmoid, [pzk, 'rbias'], [mk], bias=rbias[:, 0:1], scale=-0.125)
            if prevP[0] == 'new':
                src_c, srck = Pnew[:, 0:1], 'Pnew'
            else:
                src_c, srck = PB[prevP[1]][:, 0:1], ('F32A', f'P{prevP[1]}')
            vop('dve', 'tensor_copy', (PB[s2_][:, 1024:1025], src_c), [srck], [Pk])
            vop('dve', 'tensor_tensor_scan', (PB[s2_][:, 0:1024][:, ::-1], mB[s2_][:, ::-1], mB[s2_][:, ::-1], src_c, ALU.mult, ALU.min),
                [mk, srck], [Pk])
            vop('pool', 'tensor_tensor', (aB[s2_][:, :], PB[s2_][:, 1:1025], PB[s2_][:, 0:1024], ALU.subtract), [Pk], [ak])
            pst, pstk = psb()
            for k in range(8):
                tr(pst[:, k * 128:(k + 1) * 128], aB[s2_][:, k * 128:(k + 1) * 128], ident_b[:, :], [ak, 'ident_b'], [pstk])
            act(aTB[s2_][:, :], pst, AF.Copy, [pstk], [atk])
            for s_ in range(4):
                sg = 4 * g + s_
                ps2, pk2 = psf()
                for k in range(8):
                    jpage = pc * 8 + k
                    vpg, vpk = pagebuf()
                    col = sg * NPAGE + jpage
                    S.dma('pool', lambda e, vpg=vpg, col=col: e.indirect_dma_start(
                        out=vpg[:, :], out_offset=None, in_=cache_v[:, :],
                        in_offset=bass.IndirectOffsetOnAxis(ap=pidx[:, col:col + 1], axis=0)), vpk, ['ptf_i'], [vpk])
                    vb, vbk = stage16()
                    act(vb[:, :], vpg[:, :], AF.Copy, [vpk], [vbk])
                    mm(ps2[0:32, :], aTB[s2_][:, k * 128 + s_ * 32:k * 128 + (s_ + 1) * 32], vb[:, :], k == 0, k == 7, [atk, vbk], [pk2])
                vop('dve', 'tensor_tensor', (oacc[s_][0:32, :], oacc[s_][0:32, :], ps2[0:32, :], ALU.add), [pk2, oacck[s_]], [oacck[s_]])
            prevP = ('past', s2_)
        for s_ in range(4):
            ob, obk = aB[s_ // 2][:, (s_ % 2) * 512:(s_ % 2 + 1) * 512], f'aB{s_ // 2}'
            vop('pool', 'tensor_copy', (ob[0:32, :], oacc[s_][0:32, :]), [oacck[s_]], [obk])
            pst, pstk = psb()
            for pr in range(4):
                tr(pst[:, pr * 32:(pr + 1) * 32], ob[0:32, pr * 128:(pr + 1) * 128], ident_b[0:32, 0:32], [obk, 'ident_b'], [pstk])
            for pr in range(4):
                for hh in range(2):
                    h = 2 * pr + hh
                    vop('dve', 'tensor_copy', (oT_v[hh * 64:(hh + 1) * 64, pr, 16 * g + 4 * s_:16 * g + 4 * s_ + 4],
                                               pst[hh * 64:(hh + 1) * 64, pr * 32 + h * 4:pr * 32 + h * 4 + 4]),
                        [pstk], [('BFA', 'oT')])

    load(pscale_sb[:, :, :], pscale[:, :, :], 'pscale_sb', (), ['pscale_sb'])
    load(consts_sb[:, :], consts_in[:, :], 'consts_sb', (), ['consts_sb'])
    vop('pool', 'memset', (invw[0:64, 0:1], 1.0 / 2), (), [('invw', 0)])
    vop('pool', 'memset', (invw[64:128, 0:1], 1.0 / 4), (), [('invw', 1)])
    vop('pool', 'memset', (invw[0:64, 1:2], 1.0 / 8), (), [('invw', 2)])
    vop('pool', 'memset', (invw[64:128, 1:2], 1.0 / 16), (), [('invw', 3)])
    for l in range(n_layers):
        convert_layer(l)
    for l in range(n_layers):
        last = (l == n_layers - 1)
        x_src = xp if l == 0 else X1
        xs_src = xs if l == 0 else X1s
        o_dst = y_p if last else X1
        os_dst = y_s if last else X1s
        layer_params(l)
        layer_weights(l)
        for j in range(NSLOT):
            p1(l, x_src[j * CH:(j + 1) * CH, :], CH, j * CH, j)
        if with_sample:
            p1(l, xs_src, NS, T, None)
            for s_ in range(NSEQ):
                load(pool_s[l, s_, 0:11, :], state_pool[l, s_, 4:15, :], 'd2d', (), ['pool_s'], nowaw=True)
                load(conv_s[l, s_, 0:26, :], state_conv[l, s_, 4:30, :], 'd2d', (), ['conv_s'], nowaw=True)
        for j in range(NSLOT):
            if stop_after in ('p1', 'p1nox'):
                break
            tok0 = j * CH
            barrier(['F32A', 'BFA'])
            attention(l, j, tok0)
            barrier(['F32A'])
            load_ext_prompt(j, tok0)
            poolconv(CH, 0, first16=(j == 0))
            mixer_ffn(l, x_src[tok0:tok0 + CH, :], XM, o_dst[tok0:tok0 + CH, :], CH, tok0)
        if with_sample and stop_after not in ('p1', 'p1nox'):
            barrier(['F32A', 'BFA'])
            sample_prep(l)
            for g in range(NSEQ // 4):
                sample_attention(l, g)
            barrier(['F32A'])
            for s_ in range(NSEQ):
                load_ext_sample(l, s_)
                poolconv(4, 4 * s_)
                barrier(['F32A'])
            mixer_ffn(l, xs_src, XM, os_dst, NS, T)

    semnames = {}
    for e_ in ENGS:
        semnames[e_] = es.enter_context(nc.semaphore(f"sem_{e_}"))
    for sk in S.dma_cnt:
        semnames[sk] = es.enter_context(nc.semaphore("dsem_" + str(sk[1]).replace(" ", "").replace("'", "").replace("(", "").replace(")", "").replace(",", "_")))
    print("n semaphores", len(semnames), "ops", {e_: len(S.ops[e_]) for e_ in ENGS}, "seq", S.seq)

    import os
    if os.environ.get("KDUMP"):
        with open(os.environ["KDUMP"], "w") as f_:
            for rec in S.log:
                f_.write(repr(rec) + "\n")
    LIMIT = int(os.environ.get("KLIMIT", "0")) or 10 ** 9
    final_cnt = {}
    for e_ in ENGS:
        for waits, fn, inc, seq in S.ops[e_]:
            if seq <= LIMIT and isinstance(inc[0], tuple):
                final_cnt[inc[0]] = final_cnt.get(inc[0], 0) + 16

    def emit(engname, e):
        for waits, fn, inc, seq in S.ops[engname]:
            if seq > LIMIT:
                continue
            for sk, v in waits:
                e.wait_ge(semnames[sk], v)
            ins = fn(e)
            ins.then_inc(semnames[inc[0]], inc[1])
        if engname == 'sp':
            for sk, c in final_cnt.items():
                e.wait_ge(semnames[sk], c)

    with nc.Block() as block:
        @block.tensor
        def _(e):
            emit('pe', e)

        @block.scalar
        def _(e):
            emit('act', e)

        @block.vector
        def _(e):
            emit('dve', e)

        @block.gpsimd
        def _(e):
            emit('pool', e)

        @block.sync
        def _(e):
            emit('sp', e)
    es.close()
    return nc


_CACHE = {}


def _get_prog(key):
    if key not in _CACHE:
        _CACHE[key] = build_program(*key)
    return _CACHE[key]


def kernel(x_prompt, x_sample, cache_k, cache_v, state_pool, state_conv, page_table,
           w_in, b_gate, sb_bias, pool_w, pool_scale, w_att_o, conv_dw, conv_b, conv_ln_g, conv_ln_b,
           conv_pw, w_out, ln1_g, ln1_b, ffn_w_gate, ffn_w_up, ffn_w_down, ln2_g, ln2_b,
           _with_sample=True, _n_layers=DEPTH, _stop_after=None):
    f32 = np.float32
    nc = _get_prog((_with_sample, _n_layers, _stop_after))
    x_prompt = np.asarray(x_prompt, f32)
    shared = dict(
        w_in=np.asarray(w_in, f32), b_gate=np.asarray(b_gate, f32), sb_bias=np.asarray(sb_bias, f32),
        pool_w=np.asarray(pool_w, f32), pool_scale=np.asarray(pool_scale, f32), w_att_o=np.asarray(w_att_o, f32),
        conv_dw=np.asarray(conv_dw, f32), conv_b=np.asarray(conv_b, f32), conv_ln_g=np.asarray(conv_ln_g, f32),
        conv_ln_b=np.asarray(conv_ln_b, f32), conv_pw=np.asarray(conv_pw, f32), w_out=np.asarray(w_out, f32),
        ln1_g=np.asarray(ln1_g, f32), ln1_b=np.asarray(ln1_b, f32), ffn_w_gate=np.asarray(ffn_w_gate, f32),
        ffn_w_up=np.asarray(ffn_w_up, f32), ffn_w_down=np.asarray(ffn_w_down, f32), ln2_g=np.asarray(ln2_g, f32),
        ln2_b=np.asarray(ln2_b, f32))
    if _with_sample:
        shared['cache_k'] = np.asarray(cache_k, f32).reshape(DEPTH * 2560 * 128, 512)
        shared['cache_v'] = np.asarray(cache_v, f32).reshape(DEPTH * 2560 * 128, 512)
    consts = np.zeros((128, 16), f32)
    for r in range(128):
        h_, t_ = (r // 4) % 8, r % 4
        consts[r, h_] = 1.0
        for tn in range(4):
            consts[r, 8 + tn] = 0.0 if tn < t_ else -8.0 * MASKV
        consts[r, 12] = r
    ps_ = np.zeros((128, 2, 16), f32)
    for cc in range(2):
        for p in range(128):
            w = (2, 4, 8, 16)[cc * 2 + p // 64]
            for t_ in range(16):
                ps_[p, cc, t_] = 1.0 / min(t_ + 1, w)
    in_maps = []
    for c in range(NCORES):
        m = dict(shared)
        m['xp'] = np.ascontiguousarray(x_prompt[c])
        m['xs'] = np.ascontiguousarray(np.asarray(x_sample, f32)[NSEQ * c:NSEQ * (c + 1)].reshape(NS, D))
        m['pscale'] = ps_
        m['consts_in'] = consts
        if _with_sample:
            m['state_pool'] = np.ascontiguousarray(np.asarray(state_pool, f32)[:, NSEQ * c:NSEQ * (c + 1)])
            m['state_conv'] = np.ascontiguousarray(np.asarray(state_conv, f32)[:, NSEQ * c:NSEQ * (c + 1)])
            m['page_table'] = np.ascontiguousarray(np.asarray(page_table, np.int32)[NSEQ * c:NSEQ * (c + 1)])
        in_maps.append(m)
    res = run_bass_kernel_spmd(nc, in_maps, core_ids=list(range(NCORES)))
    R = res.results
    y_prompt = np.zeros((4, 4096, D), f32)
    k_prompt = np.zeros((DEPTH, 4, 4096, 8, 64), f32)
    v_prompt = np.zeros((DEPTH, 4, 4096, 8, 64), f32)
    pool_prompt = np.zeros((DEPTH, 4, 15, 256), f32)
    conv_prompt = np.zeros((DEPTH, 4, 30, 256), f32)
    y_sample = np.zeros((32, 4, D), f32)
    k_sample = np.zeros((DEPTH, 32, 4, 8, 64), f32)
    v_sample = np.zeros((DEPTH, 32, 4, 8, 64), f32)
    pool_sample = np.zeros((DEPTH, 32, 15, 256), f32)
    conv_sample = np.zeros((DEPTH, 32, 30, 256), f32)
    for c in range(NCORES):
        o = R[c]
        sl = slice(NSEQ * c, NSEQ * (c + 1))
        y_prompt[c] = o['y_p']
        k_prompt[:, c] = o['k_p'].reshape(DEPTH, T, 8, 64)
        v_prompt[:, c] = o['v_p'].reshape(DEPTH, T, 8, 64)
        pool_prompt[:, c] = o['pool_p']
        conv_prompt[:, c] = o['conv_p']
        y_sample[sl] = o['y_s'].reshape(NSEQ, 4, D)
        k_sample[:, sl] = o['k_s'].reshape(DEPTH, NSEQ, 4, 8, 64)
        v_sample[:, sl] = o['v_s'].reshape(DEPTH, NSEQ, 4, 8, 64)
        pool_sample[:, sl] = o['pool_s']
        conv_sample[:, sl] = o['conv_s']
    return (y_prompt, y_sample, k_prompt, v_prompt, pool_prompt, conv_prompt,
            k_sample, v_sample, pool_sample, conv_sample)
```

```python
import numpy as np
import concourse.bass as bass
import concourse.mybir as mybir
from concourse.bass_utils import run_bass_kernel_spmd

F32 = mybir.dt.float32
BF16 = mybir.dt.bfloat16
I32 = mybir.dt.int32
ALU = mybir.AluOpType
AF = mybir.ActivationFunctionType
AX = mybir.AxisListType

D = 1024
NIN = 5376
DFF = 2816
T = 4096
CH = 512
NSLOT = 8
NCORES = 4
NSEQ = 32 // NCORES
NS = NSEQ * 4
NPAGE = 64
DEPTH = 2
ALPHA = (2.0 * DEPTH) ** 0.25
EPS = 1e-5
CHUNKS = ([0, 3, 4, 7], [1, 2, 5, 6])
MASKV = 60.0
ENGS = ['pe', 'act', 'dve', 'pool', 'sp']


def chunk_loc(c):
    for r in range(2):
        if c in CHUNKS[r]:
            return r, CHUNKS[r].index(c)
    raise ValueError(c)


class Sched:
    def __init__(self):
        self.ops = {e: [] for e in ENGS}
        self.cnt = {e: 0 for e in ENGS}
        self.waited = {e: {} for e in ENGS}
        self.st = {}
        self.dma_cnt = {}
        self.seq = 0
        self.log = []

    def _split(self, k):
        return k if isinstance(k, tuple) else (k, None)

    def _base(self, b):
        return self.st.setdefault(b, {'w': None, 'r': {}, 'subs': {}})

    def _deps(self, reads, writes):
        deps = []

        def addev(e):
            if e is not None:
                deps.append(e)

        for k in reads:
            b, sub = self._split(k)
            B = self._base(b)
            addev(B['w'])
            if sub is None:
                for S in B['subs'].values():
                    addev(S['w'])
            else:
                S = B['subs'].setdefault(sub, {'w': None, 'r': {}})
                addev(S['w'])
        for k in writes:
            b, sub = self._split(k)
            B = self._base(b)
            addev(B['w'])
            deps.extend(B['r'].items())
            if sub is None:
                for S in B['subs'].values():
                    addev(S['w'])
                    deps.extend(S['r'].items())
            else:
                S = B['subs'].setdefault(sub, {'w': None, 'r': {}})
                addev(S['w'])
                deps.extend(S['r'].items())
        return deps

    def _update(self, reads, writes, ev):
        sk, v = ev
        for k in reads:
            b, sub = self._split(k)
            B = self._base(b)
            R = B['r'] if sub is None else B['subs'].setdefault(sub, {'w': None, 'r': {}})['r']
            R[sk] = max(R.get(sk, 0), v)
        for k in writes:
            b, sub = self._split(k)
            B = self._base(b)
            if sub is None:
                B['w'] = ev
                B['r'] = {}
                B['subs'] = {}
            else:
                S = B['subs'].setdefault(sub, {'w': None, 'r': {}})
                S['w'] = ev
                S['r'] = {}

    def _waits(self, eng, deps):
        need = {}
        for sk, v in deps:
            if sk == 'pe' and eng == 'pe':
                continue
            if isinstance(sk, tuple):
                v = max(v, 16 * self.dma_cnt.get(sk, 0))
            need[sk] = max(need.get(sk, 0), v)
        out = []
        W = self.waited[eng]
        for sk, v in need.items():
            if W.get(sk, 0) < v:
                W[sk] = v
                out.append((sk, v))
        return out

    def op(self, eng, fn, reads=(), writes=()):
        deps = self._deps(reads, writes)
        waits = self._waits(eng, deps)
        self.cnt[eng] += 1
        ev = (eng, self.cnt[eng])
        self.seq += 1
        self.log.append((self.seq, eng, 'op', list(reads), list(writes)))
        self.ops[eng].append((waits, fn, (eng, 1), self.seq))
        self._update(reads, writes, ev)

    def dma(self, q, fn, semkey, reads=(), writes=(), nowaw=False):
        deps = self._deps(reads, writes)
        if nowaw:
            deps = [d for d in deps if d[0] != ('d', semkey)]
        waits = self._waits(q, deps)
        sk = ('d', semkey)
        self.dma_cnt[sk] = self.dma_cnt.get(sk, 0) + 1
        ev = (sk, 16 * self.dma_cnt[sk])
        self.seq += 1
        self.log.append((self.seq, q, 'dma:' + str(semkey), list(reads), list(writes)))
        self.ops[q].append((waits, fn, (sk, 16), self.seq))
        self._update(reads, writes, ev)


def build_program(with_sample=True, n_layers=DEPTH, stop_after=None, npool=2560):
    nc = bass.Bass("TRN2", target_bir_lowering=False)
    S = Sched()

    def din(name, shape, dt=F32):
        return nc.dram_tensor(name, list(shape), dt, kind="ExternalInput").ap()

    def dout(name, shape, dt=F32):
        return nc.dram_tensor(name, list(shape), dt, kind="ExternalOutput").ap()

    def dscr(name, shape, dt):
        return nc.dram_tensor(name, list(shape), dt, kind="Internal").ap()

    xp = din("xp", [T, D])
    xs = din("xs", [NS, D])
    pscale = din("pscale", [128, 2, 16])
    consts_in = din("consts_in", [128, 16])
    w_in = din("w_in", [DEPTH, D, NIN])
    b_gate = din("b_gate", [DEPTH, 3 * D])
    sb_bias = din("sb_bias", [DEPTH, 8])
    pool_w = din("pool_w", [DEPTH, 4, 64, 256])
    pool_scale = din("pool_scale", [DEPTH, D])
    w_att_o = din("w_att_o", [DEPTH, 512, D])
    conv_dw = din("conv_dw", [DEPTH, 31, 256])
    conv_b = din("conv_b", [DEPTH, 256])
    conv_ln_g = din("conv_ln_g", [DEPTH, 256])
    conv_ln_b = din("conv_ln_b", [DEPTH, 256])
    conv_pw = din("conv_pw", [DEPTH, 256, D])
    w_out = din("w_out", [DEPTH, D, D])
    ln1_g = din("ln1_g", [DEPTH, D])
    ln1_b = din("ln1_b", [DEPTH, D])
    ffn_g = din("ffn_w_gate", [DEPTH, D, DFF])
    ffn_u = din("ffn_w_up", [DEPTH, D, DFF])
    ffn_d = din("ffn_w_down", [DEPTH, DFF, D])
    ln2_g = din("ln2_g", [DEPTH, D])
    ln2_b = din("ln2_b", [DEPTH, D])
    if with_sample:
        cache_k = din("cache_k", [DEPTH * npool * 128, 512])
        cache_v = din("cache_v", [DEPTH * npool * 128, 512])
        state_pool = din("state_pool", [DEPTH, NSEQ, 15, 256])
        state_conv = din("state_conv", [DEPTH, NSEQ, 30, 256])
        page_table = din("page_table", [NSEQ, NPAGE], I32)

    y_p = dout("y_p", [T, D])
    k_p = dout("k_p", [DEPTH, T, 512])
    v_p = dout("v_p", [DEPTH, T, 512])
    pool_p = dout("pool_p", [DEPTH, 15, 256])
    conv_p = dout("conv_p", [DEPTH, 30, 256])
    y_s = dout("y_s", [NS, D])
    k_s = dout("k_s", [DEPTH, NS, 512])
    v_s = dout("v_s", [DEPTH, NS, 512])
    pool_s = dout("pool_s", [DEPTH, NSEQ, 15, 256])
    conv_s = dout("conv_s", [DEPTH, NSEQ, 30, 256])

    wb_in = dscr("wb_in", [DEPTH, D, NIN], BF16)
    wb_ao = dscr("wb_ao", [DEPTH, 512, D], BF16)
    wb_pw = dscr("wb_pw", [DEPTH, 256, D], BF16)
    wb_out = dscr("wb_out", [DEPTH, D, D], BF16)
    wb_g = dscr("wb_g", [DEPTH, D, DFF], BF16)
    wb_u = dscr("wb_u", [DEPTH, D, DFF], BF16)
    wb_d = dscr("wb_d", [DEPTH, DFF, D], BF16)
    wb_pool = dscr("wb_pool", [DEPTH, 256, 256], BF16)
    TT = T + NS
    xT_scr = dscr("xT_scr", [D, TT], BF16)
    qT_scr = dscr("qT_scr", [512, TT], BF16)
    KT_loc = dscr("KT_loc", [512, T], BF16)
    V_loc = dscr("V_loc", [T, 512], BF16)
    ug_scr = dscr("ug_scr", [512, TT], F32)
    X1 = dscr("X1", [T, D], F32)
    X1s = dscr("X1s", [NS, D], F32)
    ksT_scr = dscr("ksT_scr", [512, NS], BF16)
    XM = dscr("XM", [TT, D], F32)

    import contextlib
    es = contextlib.ExitStack()

    def sb(name, shape, dt):
        return es.enter_context(nc.sbuf_tensor(name, list(shape), dt))

    ident_b = sb("ident_b", [128, 128], BF16)
    ident_f = sb("ident_f", [128, 128], F32)
    onesm = sb("onesm", [128, 128], F32)
    diagMB = sb("diagMB", [128, 4, 512], BF16)
    par = sb("par", [128, 64], F32)
    nbias = sb("nbias", [128, 5, 2, 8], F32)
    lnt = sb("lnt", [128, 4, D], F32)
    cdw = sb("cdw", [128, 2, 31], F32)
    slabs = [sb(f"slab{i}", [128, 8, 512], BF16) for i in range(4)]
    R32 = [sb(f"R32_{i}", [128, D], F32) for i in range(4)]
    BF8 = sb("BF8", [128, 4, D], BF16)
    xT = sb("xT", [128, 8, CH], BF16)
    BFA = sb("BFA", [128, 22 * CH], BF16)
    ao_sb = sb("ao_sb", [128, 4, D], BF16)
    pw_sb = sb("pw_sb", [128, 2, D], BF16)
    poolw_sb = sb("poolw_sb", [128, 2, 256], BF16)
    pscale_sb = sb("pscale_sb", [128, 2, 16], F32)
    stats = sb("stats", [128, 2, 6], F32)
    mv = sb("mv", [128, 4], F32)
    F32A = sb("F32A", [128, 5440], F32)
    vnew = sb("vnew", [128, NSEQ, 512], BF16)
    invw = sb("invw", [128, 2], F32)
    consts_sb = sb("consts_sb", [128, 16], F32)
    kn = sb("kn", [128, 4, NS], BF16)
    QZ = sb("QZ", [128, 16, 128], BF16)
    ptf_i = sb("ptf_i", [128, NSEQ * NPAGE], I32)
    ptf = sb("ptf", [128, NSEQ * NPAGE], F32)
    pidx = ptf_i
    mnew = sb("mnew", [128, 8], F32)
    Pnew = sb("Pnew", [128, 8], F32)
    anew = sb("anew", [128, 8], BF16)
    anT = sb("anT", [128, 128], BF16)
    rbias = sb("rbias", [128, 8], F32)
    pgs = [sb(f"pg{i}", [128, 512], F32) for i in range(4)]
    aB = [sb(f"aB{i}", [128, 1024], BF16) for i in range(2)]
    aTB = [sb(f"aTB{i}", [128, 1024], BF16) for i in range(2)]
    qT = sb("qT", [128, 4, CH], BF16)
    ST = [sb(f"ST{i}", [128, 512], F32) for i in range(3)]
    STb = [sb(f"STb{i}", [128, 512], BF16) for i in range(3)]
    small = sb("small", [128, 64], F32)
    psF_t = es.enter_context(nc.psum_tensor("psF", [128, 6, 512], F32))
    psB_t = es.enter_context(nc.psum_tensor("psB", [128, 2, 1024], BF16))

    oT_v = BFA[:, 0:4 * CH].rearrange("p (h t) -> p h t", h=4)
    mg_v = BFA[:, 4 * CH:12 * CH].rearrange("p (c t) -> p c t", c=8)
    gt_v = BFA[:, 12 * CH:15 * CH].rearrange("p (c t) -> p c t", c=3)
    pT_v = BFA[:, 15 * CH:17 * CH].rearrange("p (c t) -> p c t", c=2)
    sT_v = BFA[:, 17 * CH:19 * CH].rearrange("p (c t) -> p c t", c=2)
    hT_v = BFA[:, 0:22 * CH].rearrange("p (c t) -> p c t", c=22)
    mB = [F32A[:, i * 1024:(i + 1) * 1024] for i in range(2)]
    PB = [F32A[:, 2048 + i * 1025:2048 + (i + 1) * 1025] for i in range(2)]
    EXT = 30 + CH
    XL = 544
    extu = F32A[:, 0:2 * EXT].rearrange("p (c t) -> p c t", c=2)
    extg = F32A[:, 2 * EXT:4 * EXT].rearrange("p (c t) -> p c t", c=2)
    X0 = 4 * EXT
    cw = [F32A[:, X0 + i * 2 * XL:X0 + (i + 1) * 2 * XL].rearrange("p (c t) -> p c t", c=2) for i in range(2)]
    ptmp = F32A[:, X0 + 4 * XL:X0 + 5 * XL]
    ptmp2 = F32A[:, X0 + 5 * XL:X0 + 6 * XL]
    cacc = [F32A[:, X0 + i * 1024:X0 + (i + 1) * 1024].rearrange("p (c t) -> p c t", c=2) for i in range(2)]
    pg_i = [0]

    def pagebuf():
        i = pg_i[0] % 4
        pg_i[0] += 1
        return pgs[i], f"pg{i}"

    sems = {}

    psf_i = [0]

    def psf():
        i = psf_i[0] % 6
        psf_i[0] += 1
        return psF_t[:, i, :], ('psF', i)

    psb_i = [0]

    def psb():
        i = psb_i[0] % 2
        psb_i[0] += 1
        return psB_t[:, i, :], ('psB', i)

    slab_i = [0]

    def next_slab():
        i = slab_i[0] % 4
        slab_i[0] += 1
        return slabs[i], f"slab{i}"

    def mm(out, lhsT, rhs, start, stop, reads, writes):
        S.op('pe', lambda e: e.matmul(out, lhsT, rhs, start=start, stop=stop), reads, writes)

    def tr(out, in_, ident, reads, writes):
        S.op('pe', lambda e: e.transpose(out, in_, ident), reads, writes)

    def act(out, in_, func, reads, writes, bias=None, scale=None):
        kw = {}
        if bias is not None:
            kw['bias'] = bias
        if scale is not None:
            kw['scale'] = scale
        S.op('act', lambda e: e.activation(out, in_, func, **kw), reads, writes)

    def vop(eng, name, args, reads, writes, **kw):
        S.op(eng, lambda e: getattr(e, name)(*args, **kw), reads, writes)

    def load(out, in_, semkey, reads=(), writes=(), q='sp', nowaw=False):
        S.dma(q, lambda e: e.dma_start(out=out, in_=in_), semkey, reads, writes, nowaw=nowaw)

    def load_nc(out, in_, semkey, reads=(), writes=(), q='sp', nowaw=False):
        S.dma(q, lambda e: e.dma_start(out=out, in_=in_, allow_slow_non_contiguous=True), semkey, reads, writes, nowaw=nowaw)

    S.op('pool', lambda e: e.memset(ident_f[:], 0.0), (), ['ident_f'])
    S.op('pool', lambda e: e.memset(onesm[:], 1.0), (), ['onesm'])
    S.op('pool', lambda e: e.affine_select(ident_f[:], onesm[:], [[-1, 128]], ALU.is_equal, 0.0, base=0, channel_multiplier=1),
         ['onesm'], ['ident_f'])
    vop('dve', 'tensor_copy', (ident_b[:], ident_f[:]), ['ident_f'], ['ident_b'])
    S.op('pool', lambda e: e.memset(ST[0][:], -8.0 * MASKV), (), ['ST0'])
    for i in range(4):
        S.op('pool', lambda e, i=i: e.affine_select(ST[1][:], ST[0][:], [[1, 512]], ALU.is_ge, 0.0, base=-128 * i, channel_multiplier=-1),
             ['ST0'], ['ST1'])
        vop('dve', 'tensor_copy', (diagMB[:, i, :], ST[1][:]), ['ST1'], [('diagMB', i)])
    vop('dve', 'tensor_scalar', (onesm[:], onesm[:], 1.0 / 256.0, None, ALU.mult), ['onesm'], ['onesm'])
    def conv_w(dst, src, rows, l, key):
        step = 128
        for r0 in range(0, rows, step):
            S.dma('pool', lambda e, r0=r0: e.dma_start(out=dst[l, r0:r0 + step, :], in_=src[l, r0:r0 + step, :]),
                  key, (), [(key, l)])

    def convert_layer(l):
        conv_w(wb_in, w_in, D, l, 'wb_in')
        conv_w(wb_ao, w_att_o, 512, l, 'wb_ao')
        conv_w(wb_pw, conv_pw, 256, l, 'wb_pw')
        pw2 = pool_w.rearrange("l g c n -> l (g c) n")
        conv_w(wb_pool, pw2, 256, l, 'wb_pool')
        conv_w(wb_out, w_out, D, l, 'wb_out')
        conv_w(wb_g, ffn_g, D, l, 'wb_g')
        conv_w(wb_u, ffn_u, D, l, 'wb_u')
        conv_w(wb_d, ffn_d, DFF, l, 'wb_d')

    def load_slab(src2d, r0, nk, c0, ncols, wkey):
        sl, sk = next_slab()
        v = src2d[r0:r0 + nk * 128, c0:c0 + ncols].rearrange("(k p) n -> p k n", p=128)
        load(sl[:, 0:nk, 0:ncols], v, sk, [wkey], [sk])
        return sl, sk

    def layer_params(l):
        load_nc(par[:, 0:24], b_gate[l].rearrange("(c p) -> p c", p=128), 'par', (), ['par'])
        load_nc(par[:, 24:32], pool_scale[l].rearrange("(c p) -> p c", p=128), 'par', (), [('par', 1)], nowaw=True)
        load_nc(par[:, 32:34], conv_b[l].rearrange("(c p) -> p c", p=128), 'par', (), [('par', 2)], nowaw=True)
        load_nc(par[:, 34:36], conv_ln_g[l].rearrange("(c p) -> p c", p=128), 'par', (), [('par', 3)], nowaw=True)
        load_nc(par[:, 36:38], conv_ln_b[l].rearrange("(c p) -> p c", p=128), 'par', (), [('par', 4)], nowaw=True)
        load_nc(par[:, 40:48], sb_bias[l:l + 1, :].partition_broadcast(128), 'par', (), [('par', 5)], nowaw=True)
        for c in range(2):
            load_nc(cdw[:, c, :], conv_dw[l][:, c * 128:(c + 1) * 128].rearrange("k p -> p k"), 'cdw', (), ['cdw'], nowaw=(c > 0))
        for i, t in enumerate((ln1_g, ln1_b, ln2_g, ln2_b)):
            load_nc(lnt[:, i, :], t[l:l + 1, :].partition_broadcast(128), 'lnt', (), [('lnt', i)], nowaw=True)
        vop('dve', 'tensor_scalar', (nbias[:, 4, 0, :], par[:, 40:48], -1.0, None, ALU.mult), [('par', 5)], ['nbias'])

    def build_xT(src_rows, ntok, tok0):
        nblk = (ntok + 127) // 128
        for b in range(nblk):
            n = min(128, ntok - b * 128)
            r = R32[b % 4]
            rk = f"R32_{b % 4}"
            load(r[0:n, :], src_rows[b * 128:b * 128 + n, :], rk, (), [rk])
            vop('dve' if b % 2 == 0 else 'pool', 'tensor_copy', (BF8[0:n, b, :], r[0:n, :]), [rk], [('BF8', b)])
            for g in range(2):
                ps, pk = psb()
                for c in range(4):
                    kc = g * 4 + c
                    tr(ps[:, c * 128:c * 128 + n], BF8[0:n, b, kc * 128:(kc + 1) * 128], ident_b[0:n, 0:n],
                       [('BF8', b), 'ident_b'], [pk])
                outv = xT[:, g * 4:(g + 1) * 4, b * 128:b * 128 + n]
                inv = ps[:, 0:512].rearrange("p (c t) -> p c t", c=4)[:, :, 0:n]
                if g == 0:
                    act(outv, inv, AF.Copy, [pk], [('xT', b)])
                else:
                    vop('dve', 'tensor_copy', (outv, inv), [pk], [('xT', b)])
        load(xT_scr[:, tok0:tok0 + ntok].rearrange("(k p) t -> p k t", p=128), xT[:, :, 0:ntok], 'xT', ['xT'], ['xT_scr'])

    RG = [[0, 1], [2, 3], [4, 5], [6, 7]]
    cnt_bar = [0]

    def barrier(keys, eng='dve'):
        c = 56 + (cnt_bar[0] % 8)
        cnt_bar[0] += 1
        vop(eng, 'memset', (small[:, c:c + 1], 0.0), (), list(keys) + [('small', c)])

    st_i = [0]

    def stage32():
        i = st_i[0] % 3
        st_i[0] += 1
        return ST[i], f"ST{i}"

    stb_i = [0]

    def stage16():
        i = stb_i[0] % 3
        stb_i[0] += 1
        return STb[i], f"STb{i}"

    def fm_group(sl, sk, ncols_chunks, ntok, nk=8):
        outs = []
        for nn in ncols_chunks:
            ps, pk = psf()
            for kc in range(nk):
                mm(ps[:, 0:ntok], sl[:, kc, nn * 128:(nn + 1) * 128], xT[:, kc, 0:ntok], kc == 0, kc == nk - 1,
                   [sk, 'xT', 'ident_b'], [pk])
            outs.append((ps, pk))
        return outs

    def rows_ln(y, yk, n, gi, outbuf, outk):
        for hf in range(2):
            vop('dve', 'bn_stats', (stats[0:n, hf, :], y[0:n, hf * 512:(hf + 1) * 512]), [yk], [('stats', hf)])
        vop('dve', 'bn_aggr', (mv[0:n, 0:2], stats[0:n, :, :].rearrange("p a b -> p (a b)")), ['stats'], [('mv', 0)])
        vop('dve', 'tensor_scalar', (mv[0:n, 2:3], mv[0:n, 1:2], EPS, None, ALU.add), [('mv', 0)], [('mv', 1)])
        act(mv[0:n, 2:3], mv[0:n, 2:3], AF.Sqrt, [('mv', 1)], [('mv', 1)])
        vop('dve', 'reciprocal', (mv[0:n, 2:3], mv[0:n, 2:3]), [('mv', 1)], [('mv', 1)])
        vop('dve', 'tensor_scalar', (y[0:n, :], y[0:n, :], mv[0:n, 0:1], mv[0:n, 2:3], ALU.subtract, ALU.mult),
            [yk, ('mv', 0), ('mv', 1)], [yk])
        vop('dve', 'tensor_tensor', (y[0:n, :], y[0:n, :], lnt[0:n, gi, :], ALU.mult), [yk, ('lnt', gi)], [yk])
        vop('pool', 'tensor_tensor', (outbuf[0:n, :], y[0:n, :], lnt[0:n, gi + 1, :], ALU.add), [yk, ('lnt', gi + 1)], [outk])

    def rows_to_xT(r, rk, b, n):
        vop('pool', 'tensor_copy', (BF8[0:n, b, :], r[0:n, :]), [rk], [('BF8', b)])
        for g in range(2):
            ps, pk = psb()
            for c in range(4):
                kc = g * 4 + c
                tr(ps[:, c * 128:c * 128 + n], BF8[0:n, b, kc * 128:(kc + 1) * 128], ident_b[0:n, 0:n],
                   [('BF8', b), 'ident_b'], [pk])
            outv = xT[:, g * 4:(g + 1) * 4, b * 128:b * 128 + n]
            inv = ps[:, 0:512].rearrange("p (c t) -> p c t", c=4)[:, :, 0:n]
            if g == 0:
                act(outv, inv, AF.Copy, [pk], [('xT', b)])
            else:
                vop('dve', 'tensor_copy', (outv, inv), [pk], [('xT', b)])

    def p1(l, x_src, ntok, tok0, slot):
        sample = slot is None
        nblk = (ntok + 127) // 128
        for b in range(nblk):
            n = min(128, ntok - b * 128)
            r, rk = R32[b % 4], f"R32_{b % 4}"
            load(r[0:n, :], x_src[b * 128:b * 128 + n, :], rk, (), [rk])
            rows_to_xT(r, rk, b, n)
        load(xT_scr[:, tok0:tok0 + ntok].rearrange("(k p) t -> p k t", p=128), xT[:, :, 0:ntok], 'xT', ['xT'], ['xT_scr'])
        wl = wb_in[l]
        sl, sk = load_slab(wl, 0, 8, 0, 512, ('wb_in', l))
        outs = fm_group(sl, sk, range(4), ntok)
        for c in range(2):
            ps, pk = outs[c]
            st, stk = stage32()
            act(st[:, 0:ntok], ps[:, 0:ntok], AF.Copy, [pk], [stk])
            load(ug_scr[c * 128:(c + 1) * 128, tok0:tok0 + ntok], st[:, 0:ntok], stk, [stk], [('ug_scr', 'u')], nowaw=True)
        for c in range(2):
            ps, pk = outs[2 + c]
            st, stk = stage16()
            vop('dve', 'tensor_copy', (st[:, 0:ntok], ps[:, 0:ntok]), [pk], [stk])
            load(qT_scr[c * 128:(c + 1) * 128, tok0:tok0 + ntok], st[:, 0:ntok], stk, [stk], ['qT_scr'])
        last_b = nblk - 1
        nl = min(128, ntok - last_b * 128)
        if sample or slot == NSLOT - 1:
            ps, pk = psf()
            for kc in range(8):
                mm(ps[0:nl, 0:256], xT[:, kc, last_b * 128:last_b * 128 + nl], sl[:, kc, 0:256], kc == 0, kc == 7, [sk, 'xT'], [pk])
            st, stk = stage32()
            act(st[0:nl, 0:256], ps[0:nl, 0:256], AF.Copy, [pk], [stk])
            if sample:
                for s_ in range(NSEQ):
                    load(pool_s[l, s_, 11:15, :], st[4 * s_:4 * s_ + 4, 0:256], stk, [stk], ['pool_s'], nowaw=True)
            else:
                load(pool_p[l, :, :], st[128 - 15:128, 0:256], stk, [stk], ['pool_p'])
        sl, sk = load_slab(wl, 0, 8, 512, 256, ('wb_in', l))
        outs = fm_group(sl, sk, range(2), ntok)
        for c in range(2):
            ps, pk = outs[c]
            st, stk = stage16()
            vop('dve', 'tensor_copy', (st[:, 0:ntok], ps[:, 0:ntok]), [pk], [stk])
            load(qT_scr[(2 + c) * 128:(3 + c) * 128, tok0:tok0 + ntok], st[:, 0:ntok], stk, [stk], ['qT_scr'])
        sl, sk = load_slab(wl, 0, 8, 768, 512, ('wb_in', l))
        outs = fm_group(sl, sk, range(4), ntok)
        for c in range(4):
            ps, pk = outs[c]
            st, stk = stage16()
            act(st[:, 0:ntok], ps[:, 0:ntok], AF.Copy, [pk], [stk])
            if sample:
                load_nc(ksT_scr[c * 128:(c + 1) * 128, :], st[:, 0:ntok], stk, [stk], ['ksT_scr'])
            else:
                load(KT_loc[c * 128:(c + 1) * 128, tok0:tok0 + ntok], st[:, 0:ntok], stk, [stk], ['KT_loc'])
        kout = k_s if sample else k_p
        vout = v_s if sample else v_p
        for b in range(nblk):
            n = min(128, ntok - b * 128)
            ps, pk = psf()
            for kc in range(8):
                mm(ps[0:n, :], xT[:, kc, b * 128:b * 128 + n], sl[:, kc, :], kc == 0, kc == 7, [sk, 'xT'], [pk])
            st, stk = stage32()
            act(st[0:n, :], ps[0:n, :], AF.Copy, [pk], [stk])
            load(kout[l, tok0 - (T if sample else 0) + b * 128:tok0 - (T if sample else 0) + b * 128 + n, :], st[0:n, :], stk, [stk], ['kout'])
        sl, sk = load_slab(wl, 0, 8, 1280, 512, ('wb_in', l))
        for b in range(nblk):
            n = min(128, ntok - b * 128)
            ps, pk = psf()
            for kc in range(8):
                mm(ps[0:n, :], xT[:, kc, b * 128:b * 128 + n], sl[:, kc, :], kc == 0, kc == 7, [sk, 'xT'], [pk])
            st, stk = stage32()
            act(st[0:n, :], ps[0:n, :], AF.Copy, [pk], [stk])
            t0_ = tok0 - (T if sample else 0) + b * 128
            load(vout[l, t0_:t0_ + n, :], st[0:n, :], stk, [stk], ['vout'])
            if not sample:
                sb_, sbk = stage16()
                vop('dve', 'tensor_copy', (sb_[0:n, :], st[0:n, :]), [stk], [sbk])
                load(V_loc[tok0 + b * 128:tok0 + b * 128 + n, :], sb_[0:n, :], sbk, [sbk], ['V_loc'])
        if sample:
            for s_ in range(NSEQ):
                ps, pk = psf()
                for kc in range(8):
                    mm(ps[0:4, :], xT[:, kc, 4 * s_:4 * s_ + 4], sl[:, kc, :], kc == 0, kc == 7, [sk, 'xT'], [pk])
                vop('dve', 'tensor_copy', (vnew[0:4, s_, :], ps[0:4, :]), [pk], [('vnew', s_)])
        sl, sk = load_slab(wl, 0, 8, 1792, 512, ('wb_in', l))
        outs = fm_group(sl, sk, range(4), ntok)
        for c in range(2):
            pa, pak = outs[c]
            pg, pgk = outs[2 + c]
            st, stk = stage32()
            act(st[:, 0:ntok], pg[:, 0:ntok], AF.Sigmoid, [pgk], [stk])
            st2, st2k = stage32()
            vop('dve', 'tensor_tensor', (st2[:, 0:ntok], pa[:, 0:ntok], st[:, 0:ntok], ALU.mult), [pak, stk], [st2k])
            load(ug_scr[256 + c * 128:256 + (c + 1) * 128, tok0:tok0 + ntok], st2[:, 0:ntok], st2k, [st2k], [('ug_scr', 'g')], nowaw=True)
        if sample or slot == NSLOT - 1:
            ps, pk = psf()
            for kc in range(8):
                mm(ps[0:nl, :], xT[:, kc, last_b * 128:last_b * 128 + nl], sl[:, kc, :], kc == 0, kc == 7, [sk, 'xT'], [pk])
            st, stk = stage32()
            act(st[0:nl, 0:256], ps[0:nl, 256:512], AF.Sigmoid, [pk], [stk])
            st2, st2k = stage32()
            vop('dve', 'tensor_tensor', (st2[0:nl, 0:256], ps[0:nl, 0:256], st[0:nl, 0:256], ALU.mult), [pk, stk], [st2k])
            if sample:
                for s_ in range(NSEQ):
                    load(conv_s[l, s_, 26:30, :], st2[4 * s_:4 * s_ + 4, 0:256], st2k, [st2k], ['conv_s'], nowaw=True)
            else:
                load(conv_p[l, :, :], st2[128 - 30:128, 0:256], st2k, [st2k], ['conv_p'])

    def attention(l, j, tok0):
        nchunk = j + 1
        load(qT[:, :, :], qT_scr[:, tok0:tok0 + CH].rearrange("(c p) t -> p c t", p=128), 'qT', ['qT_scr'], ['qT'])
        npiece = (nchunk + 1) // 2
        P = []
        for pr in range(4):
            for hh in range(2):
                for i in range(4):
                    for pc in range(npiece - 1, -1, -1):
                        P.append(dict(pr=pr, hh=hh, i=i, pc=pc, first=(pc == npiece - 1), last=(pc == 0), chain=(pr * 2 + hh) * 4 + i))
        slabs_pr = {}

        def get_slabs(pr):
            if pr not in slabs_pr:
                ksl, kk = next_slab()
                vsl, vk = next_slab()
                vview = vsl[:, :, :].rearrange("p a b -> p (a b)").rearrange("p (k c) -> p k c", c=128)
                load(ksl[:, 0:nchunk, :], KT_loc[pr * 128:(pr + 1) * 128, 0:nchunk * CH].rearrange("p (c t) -> p c t", t=CH), kk, ['KT_loc'], [kk])
                for c in range(nchunk):
                    load_nc(vview[:, 4 * c:4 * c + 4, :],
                            V_loc[c * CH:(c + 1) * CH, pr * 128:(pr + 1) * 128].rearrange("(b p) n -> p b n", p=128),
                            vk, ['V_loc'], [vk], nowaw=(c > 0))
                slabs_pr[pr] = (ksl, kk, vview, vk)
            return slabs_pr[pr]

        def keys(n):
            s_ = n % 2
            return s_, ('F32A', f'm{s_}'), ('F32A', f'P{s_}'), f'aB{s_}', f'aTB{s_}'

        def A1(n):
            p = P[n]
            pr, hh, i, pc = p['pr'], p['hh'], p['i'], p['pc']
            ksl, kk, vview, vk = get_slabs(pr)
            h = 2 * pr + hh
            hs = slice(hh * 64, (hh + 1) * 64)
            s_, mk, Pk, ak, atk = keys(n)
            nt = min(2, nchunk - 2 * pc)
            for tl in range(nt - 1, -1, -1):
                c = 2 * pc + tl
                zi = psf_i[0] % 4
                psf_i[0] += 1
                ps, pk = psF_t[:, zi, :], ('psF', zi)
                masked = (c == j)
                mm(ps, qT[hs, pr, i * 128:(i + 1) * 128], ksl[hs, c, :], True, not masked, ['qT', kk], [pk])
                if masked:
                    mm(ps, ident_b[:, :], diagMB[:, i, :], False, True, ['ident_b', 'diagMB'], [pk])
                act(mB[s_][:, tl * 512:(tl + 1) * 512], ps, AF.Sigmoid, [pk, 'nbias'], [mk],
                    bias=nbias[:, 4, 0, h:h + 1], scale=-0.125)

        def A2(n):
            p = P[n]
            pc = p['pc']
            s_, mk, Pk, ak, atk = keys(n)
            nt = min(2, nchunk - 2 * pc)
            Lp = nt * CH
            if p['first']:
                vop('dve', 'memset', (PB[s_][:, Lp:Lp + 1], 1.0), (), [Pk])
                init = 1.0
                rd = [mk]
            else:
                prev = (n - 1) % 2
                vop('dve', 'tensor_copy', (PB[s_][:, Lp:Lp + 1], PB[prev][:, 0:1]), [('F32A', f'P{prev}')], [Pk])
                init = PB[prev][:, 0:1]
                rd = [mk, ('F32A', f'P{prev}')]
            vop('dve', 'tensor_tensor_scan', (PB[s_][:, 0:Lp][:, ::-1], mB[s_][:, 0:Lp][:, ::-1], mB[s_][:, 0:Lp][:, ::-1], init, ALU.mult, ALU.min),
                rd, [Pk])
            vop('pool', 'tensor_tensor', (aB[s_][:, 0:Lp], PB[s_][:, 1:Lp + 1], PB[s_][:, 0:Lp], ALU.subtract), [Pk], [ak])

        def Bst(n):
            p = P[n]
            pr, hh, i, pc = p['pr'], p['hh'], p['i'], p['pc']
            ksl, kk, vview, vk = get_slabs(pr)
            hs = slice(hh * 64, (hh + 1) * 64)
            s_, mk, Pk, ak, atk = keys(n)
            nt = min(2, nchunk - 2 * pc)
            Lp = nt * CH
            oi = 4 + (p['chain'] % 2)
            ops_, opk = psF_t[hs, oi, 0:128], ('psF', oi)
            ps, pk = psb()
            for k in range(4 * nt):
                tr(ps[:, k * 128:(k + 1) * 128], aB[s_][:, k * 128:(k + 1) * 128], ident_b[:, :], [ak, 'ident_b'], [pk])
            act(aTB[s_][:, 0:Lp], ps[:, 0:Lp], AF.Copy, [pk], [atk])
            for k in range(4 * nt):
                blk = pc * 8 + k
                lastmm = (p['last'] and k == 4 * nt - 1)
                mm(ops_, vview[:, blk, hs], aTB[s_][:, k * 128:(k + 1) * 128], p['first'] and k == 0, lastmm, [vk, atk], [opk])
            if p['last']:
                vop('dve', 'tensor_copy', (oT_v[hs, pr, i * 128:(i + 1) * 128], ops_), [opk], [('BFA', 'oT')])

        N = len(P)
        for n in range(N + 2):
            if n < N:
                A1(n)
            if 1 <= n <= N:
                A2(n - 1)
            if n >= 2:
                Bst(n - 2)

    def poolconv(n, off, first16=False):
        L = 30 + n
        s2, s4 = cw[0], cw[1]
        vop('dve', 'tensor_tensor', (s2[:, :, 1:L], extu[:, :, 1:L], extu[:, :, 0:L - 1], ALU.add), [('F32A', 'extu')], [('F32A', 'cw0')])
        vop('dve', 'tensor_tensor', (s4[:, :, 3:L], s2[:, :, 3:L], s2[:, :, 1:L - 2], ALU.add), [('F32A', 'cw0')], [('F32A', 'cw1')])
        tk = ('F32A', 'tmp')
        vop('dve', 'tensor_tensor', (ptmp[:, 7:L], s4[:, 1, 7:L], s4[:, 1, 3:L - 4], ALU.add), [('F32A', 'cw1')], [tk])
        vop('dve', 'tensor_tensor', (ptmp2[:, 15:L], ptmp[:, 15:L], ptmp[:, 7:L - 8], ALU.add), [tk], [('F32A', 'tmp2')])
        srcs = [(s2, 0, 0, ('F32A', 'cw0')), (s4, 0, 1, ('F32A', 'cw1')), (None, 1, 0, tk), (None, 1, 1, ('F32A', 'tmp2'))]
        for g, (sbuf_, c, hh, key) in enumerate(srcs):
            hs = slice(hh * 64, (hh + 1) * 64)
            src = sbuf_[hs, c, 30:L] if sbuf_ is not None else (ptmp[hs, 30:L] if g == 2 else ptmp2[hs, 30:L])
            if first16:
                st, stk = stage32()
                vop('dve', 'tensor_tensor', (st[hs, 0:16], (sbuf_[hs, c, 30:46] if sbuf_ is not None else (ptmp[hs, 30:46] if g == 2 else ptmp2[hs, 30:46])),
                                             pscale_sb[hs, c, :], ALU.mult), [key, 'pscale_sb'], [stk])
                vop('dve', 'tensor_tensor', (pT_v[hs, c, off:off + 16], st[hs, 0:16], extu[hs, c, 30:46], ALU.subtract),
                    [stk, ('F32A', 'extu')], [('BFA', 'pT')])
                vop('dve', 'scalar_tensor_tensor', (pT_v[hs, c, off + 16:off + n], src[:, 16:n], invw[hs, c:c + 1], extu[hs, c, 46:L], ALU.mult, ALU.subtract),
                    [key, ('F32A', 'extu'), 'invw'], [('BFA', 'pT')])
            else:
                vop('dve', 'scalar_tensor_tensor', (pT_v[hs, c, off:off + n], src, invw[hs, c:c + 1], extu[hs, c, 30:L], ALU.mult, ALU.subtract),
                    [key, ('F32A', 'extu'), 'invw'], [('BFA', 'pT')])
        barrier(['F32A'])
        for c in range(2):
            eng = 'dve'
            for k in range(31):
                dst = cacc[k % 2][:, c, 0:n]
                dk = ('F32A', f'cacc{k % 2}_{c}')
                if k == 0:
                    vop(eng, 'tensor_scalar', (dst, extg[:, c, 0:n], cdw[:, c, 0:1], None, ALU.mult), [('F32A', 'extg'), 'cdw'], [dk])
                else:
                    srck = ('F32A', f'cacc{(k - 1) % 2}_{c}')
                    vop(eng, 'scalar_tensor_tensor', (dst, extg[:, c, k:k + n], cdw[:, c, k:k + 1], cacc[(k - 1) % 2][:, c, 0:n], ALU.mult, ALU.add),
                        [('F32A', 'extg'), 'cdw', srck], [dk])
            vop(eng, 'tensor_scalar', (cacc[0][:, c, 0:n], cacc[0][:, c, 0:n], par[:, 32 + c:33 + c], None, ALU.add),
                [('F32A', f'cacc0_{c}'), ('par', 2)], [('F32A', f'cacc0_{c}')])
            act(cacc[1][:, c, 0:n], cacc[0][:, c, 0:n], AF.Square, [('F32A', f'cacc0_{c}')], [('F32A', f'cacc1_{c}')])
        pm, pmk = psf()
        pe2, pe2k = psf()
        for c in range(2):
            mm(pm[:, 0:n], onesm[:, :], cacc[0][:, c, 0:n], c == 0, c == 1, ['onesm', ('F32A', f'cacc0_{c}')], [pmk])
        for c in range(2):
            mm(pe2[:, 0:n], onesm[:, :], cacc[1][:, c, 0:n], c == 0, c == 1, ['onesm', ('F32A', f'cacc1_{c}')], [pe2k])
        st, stk = stage32()
        act(st[:, 0:n], pm[:, 0:n], AF.Copy, [pmk], [stk])
        st2, st2k = stage32()
        vop('dve', 'tensor_tensor', (st2[:, 0:n], st[:, 0:n], st[:, 0:n], ALU.mult), [stk], [st2k])
        vop('dve', 'tensor_tensor', (st2[:, 0:n], pe2[:, 0:n], st2[:, 0:n], ALU.subtract), [pe2k, st2k], [st2k])
        vop('dve', 'tensor_scalar', (st2[:, 0:n], st2[:, 0:n], EPS, None, ALU.add), [st2k], [st2k])
        act(st2[:, 0:n], st2[:, 0:n], AF.Sqrt, [st2k], [st2k])
        vop('dve', 'reciprocal', (st2[:, 0:n], st2[:, 0:n]), [st2k], [st2k])
        for c in range(2):
            ck = ('F32A', f'cacc0_{c}')
            vop('dve', 'tensor_tensor', (cacc[0][:, c, 0:n], cacc[0][:, c, 0:n], st[:, 0:n], ALU.subtract), [ck, stk], [ck])
            vop('dve', 'tensor_tensor', (cacc[0][:, c, 0:n], cacc[0][:, c, 0:n], st2[:, 0:n], ALU.mult), [ck, st2k], [ck])
            act(sT_v[:, c, off:off + n], cacc[0][:, c, 0:n], AF.Silu, [ck, ('par', 3), ('par', 4)], [('BFA', 'sT')],
                bias=par[:, 36 + c:37 + c], scale=par[:, 34 + c:35 + c])

    def layer_weights(l):
        load(ao_sb[:, :, :], wb_ao[l].rearrange("(h p) n -> p h n", p=128), 'ao_sb', [('wb_ao', l)], ['ao_sb'])
        load(pw_sb[:, :, :], wb_pw[l].rearrange("(c p) n -> p c n", p=128), 'pw_sb', [('wb_pw', l)], ['pw_sb'])
        load_nc(poolw_sb[:, :, :], wb_pool[l].rearrange("(c p) n -> p c n", p=128), 'poolw_sb', [('wb_pool', l)], ['poolw_sb'])

    def mixer_ffn(l, x_src, xm_scr, out_dst, ntok, tok0):
        nblk = (ntok + 127) // 128
        wl = wb_in[l]
        load(xT[:, :, 0:ntok], xT_scr[:, tok0:tok0 + ntok].rearrange("(k p) t -> p k t", p=128), 'xT', ['xT_scr'], ['xT'])
        for grp in range(2):
            gsl = [load_slab(wl, 0, 8, 2304 + b_ * 1024 + grp * 512, 512, ('wb_in', l)) for b_ in range(3)]
            for nn in range(4):
                nch = grp * 4 + nn
                for b_ in range(3):
                    sl, sk = gsl[b_]
                    ps, pk = psf()
                    for kc in range(8):
                        mm(ps[:, 0:ntok], sl[:, kc, nn * 128:(nn + 1) * 128], xT[:, kc, 0:ntok], kc == 0, kc == 7, [sk, 'xT'], [pk])
                    act(gt_v[:, b_, 0:ntok], ps[:, 0:ntok], AF.Sigmoid, [pk, 'par'], [('BFA', f'gt{b_}')], bias=par[:, b_ * 8 + nch:b_ * 8 + nch + 1])
                g = nch // 2
                hs = slice((g % 2) * 64, (g % 2) * 64 + 64)
                pa, pak = psf()
                mm(pa[:, 0:ntok], poolw_sb[hs, g // 2, (nch % 2) * 128:(nch % 2) * 128 + 128], pT_v[hs, g // 2, 0:ntok], True, True,
                   ['poolw_sb', ('BFA', 'pT')], [pak])
                pb, pbk = psf()
                for pr in range(4):
                    mm(pb[:, 0:ntok], ao_sb[:, pr, nch * 128:(nch + 1) * 128], oT_v[:, pr, 0:ntok], pr == 0, pr == 3, ['ao_sb', ('BFA', 'oT')], [pbk])
                pc_, pck = psf()
                for c in range(2):
                    mm(pc_[:, 0:ntok], pw_sb[:, c, nch * 128:(nch + 1) * 128], sT_v[:, c, 0:ntok], c == 0, c == 1, ['pw_sb', ('BFA', 'sT')], [pck])
                t1, t1k = stage32()
                vop('dve', 'scalar_tensor_tensor', (t1[:, 0:ntok], pa[:, 0:ntok], par[:, 24 + nch:25 + nch], gt_v[:, 0, 0:ntok], ALU.mult, ALU.mult),
                    [pak, ('par', 1), ('BFA', 'gt0')], [t1k])
                t2, t2k = stage32()
                vop('dve', 'tensor_tensor', (t2[:, 0:ntok], pb[:, 0:ntok], gt_v[:, 1, 0:ntok], ALU.mult), [pbk, ('BFA', 'gt1')], [t2k])
                vop('pool', 'tensor_tensor', (t1[:, 0:ntok], t1[:, 0:ntok], t2[:, 0:ntok], ALU.add), [t1k, t2k], [t1k])
                t3, t3k = stage32()
                vop('dve', 'tensor_tensor', (t3[:, 0:ntok], pc_[:, 0:ntok], gt_v[:, 2, 0:ntok], ALU.mult), [pck, ('BFA', 'gt2')], [t3k])
                vop('pool', 'tensor_tensor', (mg_v[:, nch, 0:ntok], t1[:, 0:ntok], t3[:, 0:ntok], ALU.add), [t1k, t3k], [('BFA', f'mg{nch}')])
        wsl = [load_slab(wb_out[l], 0, 8, hf * 512, 512, ('wb_out', l)) for hf in range(2)]
        mgk = [('BFA', f'mg{k}') for k in range(8)]
        def stX(b):
            n = min(128, ntok - b * 128)
            xr, xrk = R32[b % 2], f"R32_{b % 2}"
            yr, yrk = R32[2 + (b % 2)], f"R32_{2 + (b % 2)}"
            load(xr[0:n, :], x_src[b * 128:b * 128 + n, :], xrk, (), [xrk])
            for hf in range(2):
                sl, sk = wsl[hf]
                ps, pk = psf()
                for kc in range(8):
                    mm(ps[0:n, :], mg_v[:, kc, b * 128:b * 128 + n], sl[:, kc, :], kc == 0, kc == 7, [sk] + mgk, [pk])
                vop('dve', 'scalar_tensor_tensor', (yr[0:n, hf * 512:(hf + 1) * 512], xr[0:n, hf * 512:(hf + 1) * 512], ALPHA, ps[0:n, :], ALU.mult, ALU.add),
                    [xrk, pk], [yrk])

        def stY(b):
            n = min(128, ntok - b * 128)
            om, omk = R32[b % 2], f"R32_{b % 2}"
            yr, yrk = R32[2 + (b % 2)], f"R32_{2 + (b % 2)}"
            rows_ln(yr, yrk, n, 0, om, omk)
            load(xm_scr[tok0 + b * 128:tok0 + b * 128 + n, :], om[0:n, :], omk, [omk], ['xm_scr'])

        def stZ(b):
            n = min(128, ntok - b * 128)
            rows_to_xT(R32[b % 2], f"R32_{b % 2}", b, n)

        for it in range(nblk + 1):
            if it < nblk:
                stX(it)
            if it >= 1:
                stY(it - 1)
                stZ(it - 1)
        barrier(['BFA'])
        nsl = (DFF + 511) // 512
        for si in range(nsl):
            ncol = min(512, DFF - si * 512)
            gs, gk = load_slab(wb_g[l], 0, 8, si * 512, ncol, ('wb_g', l))
            us, uk = load_slab(wb_u[l], 0, 8, si * 512, ncol, ('wb_u', l))
            for nn in range(ncol // 128):
                ch = si * 4 + nn
                pg, pgk = psf()
                for kc in range(8):
                    mm(pg[:, 0:ntok], gs[:, kc, nn * 128:(nn + 1) * 128], xT[:, kc, 0:ntok], kc == 0, kc == 7, [gk, 'xT'], [pgk])
                pu, puk = psf()
                for kc in range(8):
                    mm(pu[:, 0:ntok], us[:, kc, nn * 128:(nn + 1) * 128], xT[:, kc, 0:ntok], kc == 0, kc == 7, [uk, 'xT'], [puk])
                st, stk = stage32()
                act(st[:, 0:ntok], pg[:, 0:ntok], AF.Silu, [pgk], [stk])
                vop('dve', 'tensor_tensor', (hT_v[:, ch, 0:ntok], pu[:, 0:ntok], st[:, 0:ntok], ALU.mult), [puk, stk], [('BFA', f'h{ch}')])
        hk = [('BFA', f'h{k}') for k in range(22)]
        for hf in range(2):
            for kg in range(3):
                nk = 8 if kg < 2 else 6
                sl, sk = load_slab(wb_d[l], kg * 1024, nk, hf * 512, 512, ('wb_d', l))
                for b in range(nblk):
                    n = min(128, ntok - b * 128)
                    ps, pk = psF_t[:, b, :], ('psF', b)
                    for k in range(nk):
                        kc = kg * 8 + k
                        mm(ps[0:n, :], hT_v[:, kc, b * 128:b * 128 + n], sl[:, k, :], kc == 0, kc == 21, [sk] + hk, [pk])
            for b in range(nblk):
                n = min(128, ntok - b * 128)
                ps, pk = psF_t[:, b, :], ('psF', b)
                st, stk = stage32()
                load(st[0:n, :], xm_scr[tok0 + b * 128:tok0 + b * 128 + n, hf * 512:(hf + 1) * 512], stk, ['xm_scr'], [stk])
                yr, yrk = R32[b], f"R32_{b}"
                vop('dve', 'scalar_tensor_tensor', (yr[0:n, hf * 512:(hf + 1) * 512], st[0:n, :], ALPHA, ps[0:n, :], ALU.mult, ALU.add),
                    [stk, pk], [(yrk, hf)])
        for b in range(nblk):
            n = min(128, ntok - b * 128)
            yr, yrk = R32[b], f"R32_{b}"
            rows_ln(yr, yrk, n, 2, yr, yrk)
            load(out_dst[b * 128:b * 128 + n, :], yr[0:n, :], yrk, [yrk], ['out_dst'])

    def load_ext_prompt(j, tok0):
        for c in range(2):
            if j == 0:
                vop('pool', 'memset', (extu[:, c, 0:30], 0.0), (), [('F32A', 'extu')])
                vop('pool', 'memset', (extg[:, c, 0:30], 0.0), (), [('F32A', 'extg')])
                load(extu[:, c, 30:30 + CH], ug_scr[c * 128:(c + 1) * 128, 0:CH], 'extld', [('ug_scr', 'u')], [('F32A', 'extu')], nowaw=True)
                load(extg[:, c, 30:30 + CH], ug_scr[256 + c * 128:256 + (c + 1) * 128, 0:CH], 'extld', [('ug_scr', 'g')], [('F32A', 'extg')], nowaw=True)
            else:
                load(extu[:, c, 0:30 + CH], ug_scr[c * 128:(c + 1) * 128, tok0 - 30:tok0 + CH], 'extld', [('ug_scr', 'u')], [('F32A', 'extu')], nowaw=True)
                load(extg[:, c, 0:30 + CH], ug_scr[256 + c * 128:256 + (c + 1) * 128, tok0 - 30:tok0 + CH], 'extld', [('ug_scr', 'g')], [('F32A', 'extg')], nowaw=True)

    def load_ext_sample(l, s_):
        vop('pool', 'memset', (extu[:, :, 0:30], 0.0), (), [('F32A', 'extu')])
        for c in range(2):
            load_nc(extu[:, c, 15:30], state_pool[l, s_, :, c * 128:(c + 1) * 128].rearrange("t p -> p t"), 'extld', (), [('F32A', 'extu')], nowaw=True)
            load_nc(extg[:, c, 0:30], state_conv[l, s_, :, c * 128:(c + 1) * 128].rearrange("t p -> p t"), 'extld', (), [('F32A', 'extg')], nowaw=True)
            load_nc(extu[:, c, 30:34], ug_scr[c * 128:(c + 1) * 128, T + 4 * s_:T + 4 * s_ + 4], 'extld', [('ug_scr', 'u')], [('F32A', 'extu')], nowaw=True)
            load_nc(extg[:, c, 30:34], ug_scr[256 + c * 128:256 + (c + 1) * 128, T + 4 * s_:T + 4 * s_ + 4], 'extld', [('ug_scr', 'g')], [('F32A', 'extg')], nowaw=True)

    def sample_prep(l):
        load_nc(ptf_i[:, :], page_table[:, :].rearrange("s j -> (s j)").partition_broadcast(128), 'ptf', (), ['ptf_i'])
        vop('dve', 'tensor_copy', (ptf[:, :], ptf_i[:, :]), ['ptf_i'], ['ptf'])
        vop('dve', 'tensor_scalar', (ptf[:, :], ptf[:, :], 128.0, float(l * npool * 128), ALU.mult, ALU.add), ['ptf'], ['ptf'])
        vop('dve', 'tensor_scalar', (ptf[:, :], ptf[:, :], consts_sb[:, 12:13], None, ALU.add), ['ptf', 'consts_sb'], ['ptf'])
        vop('dve', 'tensor_copy', (pidx[:, :], ptf[:, :]), ['ptf'], ['ptf_i'])
        sk0 = ('small', 0)
        vop('dve', 'tensor_tensor', (small[:, 0:8], consts_sb[:, 0:8], nbias[:, 4, 0, :], ALU.mult), ['consts_sb', 'nbias'], [sk0])
        vop('dve', 'tensor_tensor', (small[:, 0:4], small[:, 0:4], small[:, 4:8], ALU.add), [sk0], [sk0])
        vop('dve', 'tensor_tensor', (small[:, 0:2], small[:, 0:2], small[:, 2:4], ALU.add), [sk0], [sk0])
        vop('dve', 'tensor_tensor', (rbias[:, 0:1], small[:, 0:1], small[:, 1:2], ALU.add), [sk0], ['rbias'])

    def sample_attention(l, g):
        oacc = [R32[2 + s_ // 2][:, (s_ % 2) * 512:(s_ % 2 + 1) * 512] for s_ in range(4)]
        oacck = [(f"R32_{2 + s_ // 2}", s_ % 2) for s_ in range(4)]
        t0s = T + 16 * g
        load_nc(qT[:, :, 0:16], qT_scr[:, t0s:t0s + 16].rearrange("(c p) t -> p c t", p=128), 'qT', ['qT_scr'], ['qT'])
        load_nc(kn[:, :, 0:16], ksT_scr[:, 16 * g:16 * g + 16].rearrange("(c p) t -> p c t", p=128), 'kn', ['ksT_scr'], ['kn'])
        vop('pool', 'memset', (QZ[:, :, :], 0.0), (), ['QZ'])
        for s_ in range(4):
            for pr in range(4):
                for hh in range(2):
                    r0 = s_ * 32 + (2 * pr + hh) * 4
                    vop('dve', 'tensor_copy', (QZ[hh * 64:(hh + 1) * 64, s_ * 4 + pr, r0:r0 + 4], qT[hh * 64:(hh + 1) * 64, pr, 4 * s_:4 * s_ + 4]),
                        ['qT', 'QZ'], [('QZ', s_ * 4 + pr)])
        ps, pk = psF_t[:, 0, 0:4], ('psF', 0)
        n_mm = 0
        for s_ in range(4):
            for pr in range(4):
                mm(ps, QZ[:, s_ * 4 + pr, :], kn[:, pr, 4 * s_:4 * s_ + 4], n_mm == 0, n_mm == 15, [('QZ', s_ * 4 + pr), 'kn'], [pk])
                n_mm += 1
        st, stk = stage32()
        vop('dve', 'tensor_tensor', (st[:, 0:4], ps, consts_sb[:, 8:12], ALU.add), [pk, 'consts_sb'], [stk])
        act(mnew[:, 0:4], st[:, 0:4], AF.Sigmoid, [stk, 'rbias'], ['mnew'], bias=rbias[:, 0:1], scale=-0.125)
        vop('dve', 'memset', (Pnew[:, 4:5], 1.0), (), ['Pnew'])
        vop('dve', 'tensor_tensor_scan', (Pnew[:, 0:4][:, ::-1], mnew[:, 0:4][:, ::-1], mnew[:, 0:4][:, ::-1], 1.0, ALU.mult, ALU.min),
            ['mnew'], ['Pnew'])
        vop('dve', 'tensor_tensor', (anew[:, 0:4], Pnew[:, 1:5], Pnew[:, 0:4], ALU.subtract), ['Pnew'], ['anew'])
        pst, pstk = psb()
        tr(pst[0:4, 0:128], anew[:, 0:4], ident_b[:, :], ['anew', 'ident_b'], [pstk])
        vop('dve', 'tensor_copy', (anT[0:4, :], pst[0:4, 0:128]), [pstk], ['anT'])
        for s_ in range(4):
            ps2, pk2 = psf()
            mm(ps2[0:32, :], anT[0:4, s_ * 32:(s_ + 1) * 32], vnew[0:4, 4 * g + s_, :], True, True, ['anT', ('vnew', 4 * g + s_)], [pk2])
            vop('dve', 'tensor_copy', (oacc[s_][0:32, :], ps2[0:32, :]), [pk2], [oacck[s_]])
        prevP = ('new', None)
        sidx = [0]
        for pc in range(7, -1, -1):
            s2_ = sidx[0] % 2
            sidx[0] += 1
            mk, Pk, ak, atk = ('F32A', f'm{s2_}'), ('F32A', f'P{s2_}'), f'aB{s2_}', f'aTB{s2_}'
            for tl in (1, 0):
                zi = 1 + tl
                psz, pzk = psF_t[:, zi, :], ('psF', zi)
                n_mm = 0
                for s_ in range(4):
                    sg = 4 * g + s_
                    ksl, kk = next_slab()
                    kv = ksl[:, 0:4, :]
                    for pg4 in range(4):
                        jpage = pc * 8 + tl * 4 + pg4
                        kpg, kpk = pagebuf()
                        col = sg * NPAGE + jpage
                        S.dma('pool', lambda e, kpg=kpg, col=col: e.indirect_dma_start(
                            out=kpg[:, :], out_offset=None, in_=cache_k[:, :],
                            in_offset=bass.IndirectOffsetOnAxis(ap=pidx[:, col:col + 1], axis=0)), kpk, ['ptf_i'], [kpk])
                        kb, kbk = stage16()
                        vop('dve', 'tensor_copy', (kb[:, :], kpg[:, :]), [kpk], [kbk])
                        pt_, ptk = psb()
                        for pr in range(4):
                            tr(pt_[:, pr * 128:(pr + 1) * 128], kb[:, pr * 128:(pr + 1) * 128], ident_b[:, :], [kbk, 'ident_b'], [ptk])
                        act(kv[:, :, pg4 * 128:(pg4 + 1) * 128], pt_[:, 0:512].rearrange("p (c t) -> p c t", c=4), AF.Copy, [ptk], [(kk, pg4)])
                    for pr in range(4):
                        mm(psz, QZ[:, s_ * 4 + pr, :], kv[:, pr, :], n_mm == 0, n_mm == 15, [('QZ', s_ * 4 + pr), kk], [pzk])
                        n_mm += 1
                act(mB[s2_][:, tl * 512:(tl + 1) * 512], psz, AF.Sigmoid, [pzk, 'rbias'], [mk], bias=rbias[:, 0:1], scale=-0.125)
            if prevP[0] == 'new':
                src_c, srck = Pnew[:, 0:1], 'Pnew'
            else:
                src_c, srck = PB[prevP[1]][:, 0:1], ('F32A', f'P{prevP[1]}')
            vop('dve', 'tensor_copy', (PB[s2_][:, 1024:1025], src_c), [srck], [Pk])
            vop('dve', 'tensor_tensor_scan', (PB[s2_][:, 0:1024][:, ::-1], mB[s2_][:, ::-1], mB[s2_][:, ::-1], src_c, ALU.mult, ALU.min),
                [mk, srck], [Pk])
            vop('dve', 'tensor_tensor', (aB[s2_][:, :], PB[s2_][:, 1:1025], PB[s2_][:, 0:1024], ALU.subtract), [Pk], [ak])
            pst, pstk = psb()
            for k in range(8):
                tr(pst[:, k * 128:(k + 1) * 128], aB[s2_][:, k * 128:(k + 1) * 128], ident_b[:, :], [ak, 'ident_b'], [pstk])
            act(aTB[s2_][:, :], pst, AF.Copy, [pstk], [atk])
            for s_ in range(4):
                sg = 4 * g + s_
                ps2, pk2 = psf()
                for k in range(8):
                    jpage = pc * 8 + k
                    vpg, vpk = pagebuf()
                    col = sg * NPAGE + jpage
                    S.dma('pool', lambda e, vpg=vpg, col=col: e.indirect_dma_start(
                        out=vpg[:, :], out_offset=None, in_=cache_v[:, :],
                        in_offset=bass.IndirectOffsetOnAxis(ap=pidx[:, col:col + 1], axis=0)), vpk, ['ptf_i'], [vpk])
                    vb, vbk = stage16()
                    act(vb[:, :], vpg[:, :], AF.Copy, [vpk], [vbk])
                    mm(ps2[0:32, :], aTB[s2_][:, k * 128 + s_ * 32:k * 128 + (s_ + 1) * 32], vb[:, :], k == 0, k == 7, [atk, vbk], [pk2])
                vop('dve', 'tensor_tensor', (oacc[s_][0:32, :], oacc[s_][0:32, :], ps2[0:32, :], ALU.add), [pk2, oacck[s_]], [oacck[s_]])
            prevP = ('past', s2_)
        for s_ in range(4):
            ob, obk = aB[s_ // 2][:, (s_ % 2) * 512:(s_ % 2 + 1) * 512], f'aB{s_ // 2}'
            vop('pool', 'tensor_copy', (ob[0:32, :], oacc[s_][0:32, :]), [oacck[s_]], [obk])
            pst, pstk = psb()
            for pr in range(4):
                tr(pst[:, pr * 32:(pr + 1) * 32], ob[0:32, pr * 128:(pr + 1) * 128], ident_b[0:32, 0:32], [obk, 'ident_b'], [pstk])
            for pr in range(4):
                for hh in range(2):
                    h = 2 * pr + hh
                    vop('dve', 'tensor_copy', (oT_v[hh * 64:(hh + 1) * 64, pr, 16 * g + 4 * s_:16 * g + 4 * s_ + 4],
                                               pst[hh * 64:(hh + 1) * 64, pr * 32 + h * 4:pr * 32 + h * 4 + 4]),
                        [pstk], [('BFA', 'oT')])

    load(pscale_sb[:, :, :], pscale[:, :, :], 'pscale_sb', (), ['pscale_sb'])
    load(consts_sb[:, :], consts_in[:, :], 'consts_sb', (), ['consts_sb'])
    vop('pool', 'memset', (invw[0:64, 0:1], 1.0 / 2), (), [('invw', 0)])
    vop('pool', 'memset', (invw[64:128, 0:1], 1.0 / 4), (), [('invw', 1)])
    vop('pool', 'memset', (invw[0:64, 1:2], 1.0 / 8), (), [('invw', 2)])
    vop('pool', 'memset', (invw[64:128, 1:2], 1.0 / 16), (), [('invw', 3)])
    for l in range(n_layers):
        convert_layer(l)
    for l in range(n_layers):
        last = (l == n_layers - 1)
        x_src = xp if l == 0 else X1
        xs_src = xs if l == 0 else X1s
        o_dst = y_p if last else X1
        os_dst = y_s if last else X1s
        layer_params(l)
        layer_weights(l)
        for j in range(NSLOT):
            p1(l, x_src[j * CH:(j + 1) * CH, :], CH, j * CH, j)
        if with_sample:
            p1(l, xs_src, NS, T, None)
            for s_ in range(NSEQ):
                load(pool_s[l, s_, 0:11, :], state_pool[l, s_, 4:15, :], 'd2d', (), ['pool_s'], nowaw=True)
                load(conv_s[l, s_, 0:26, :], state_conv[l, s_, 4:30, :], 'd2d', (), ['conv_s'], nowaw=True)
        for j in range(NSLOT):
            if stop_after in ('p1', 'p1nox'):
                break
            tok0 = j * CH
            barrier(['F32A', 'BFA'])
            attention(l, j, tok0)
            barrier(['F32A'])
            load_ext_prompt(j, tok0)
            poolconv(CH, 0, first16=(j == 0))
            mixer_ffn(l, x_src[tok0:tok0 + CH, :], XM, o_dst[tok0:tok0 + CH, :], CH, tok0)
        if with_sample and stop_after not in ('p1', 'p1nox'):
            barrier(['F32A', 'BFA'])
            sample_prep(l)
            for g in range(NSEQ // 4):
                sample_attention(l, g)
            barrier(['F32A'])
            for s_ in range(NSEQ):
                load_ext_sample(l, s_)
                poolconv(4, 4 * s_)
                barrier(['F32A'])
            mixer_ffn(l, xs_src, XM, os_dst, NS, T)

    semnames = {}
    for e_ in ENGS:
        semnames[e_] = es.enter_context(nc.semaphore(f"sem_{e_}"))
    for sk in S.dma_cnt:
        semnames[sk] = es.enter_context(nc.semaphore("dsem_" + str(sk[1]).replace(" ", "").replace("'", "").replace("(", "").replace(")", "").replace(",", "_")))
    print("n semaphores", len(semnames), "ops", {e_: len(S.ops[e_]) for e_ in ENGS}, "seq", S.seq)

    import os
    if os.environ.get("KDUMP"):
        with open(os.environ["KDUMP"], "w") as f_:
            for rec in S.log:
                f_.write(repr(rec) + "\n")
    LIMIT = int(os.environ.get("KLIMIT", "0")) or 10 ** 9
    final_cnt = {}
    for e_ in ENGS:
        for waits, fn, inc, seq in S.ops[e_]:
            if seq <= LIMIT and isinstance(inc[0], tuple):
                final_cnt[inc[0]] = final_cnt.get(inc[0], 0) + 16

    def emit(engname, e):
        for waits, fn, inc, seq in S.ops[engname]:
            if seq > LIMIT:
                continue
            for sk, v in waits:
                e.wait_ge(semnames[sk], v)
            ins = fn(e)
            ins.then_inc(semnames[inc[0]], inc[1])
        if engname == 'sp':
            for sk, c in final_cnt.items():
                e.wait_ge(semnames[sk], c)

    with nc.Block() as block:
        @block.tensor
        def _(e):
            emit('pe', e)

        @block.scalar
        def _(e):
            emit('act', e)

        @block.vector
        def _(e):
            emit('dve', e)

        @block.gpsimd
        def _(e):
            emit('pool', e)

        @block.sync
        def _(e):
            emit('sp', e)
    es.close()
    return nc


_CACHE = {}


def _get_prog(key):
    if key not in _CACHE:
        _CACHE[key] = build_program(*key)
    return _CACHE[key]


def kernel(x_prompt, x_sample, cache_k, cache_v, state_pool, state_conv, page_table,
           w_in, b_gate, sb_bias, pool_w, pool_scale, w_att_o, conv_dw, conv_b, conv_ln_g, conv_ln_b,
           conv_pw, w_out, ln1_g, ln1_b, ffn_w_gate, ffn_w_up, ffn_w_down, ln2_g, ln2_b,
           _with_sample=True, _n_layers=DEPTH, _stop_after=None):
    f32 = np.float32
    nc = _get_prog((_with_sample, _n_layers, _stop_after))
    x_prompt = np.asarray(x_prompt, f32)
    shared = dict(
        w_in=np.asarray(w_in, f32), b_gate=np.asarray(b_gate, f32), sb_bias=np.asarray(sb_bias, f32),
        pool_w=np.asarray(pool_w, f32), pool_scale=np.asarray(pool_scale, f32), w_att_o=np.asarray(w_att_o, f32),
        conv_dw=np.asarray(conv_dw, f32), conv_b=np.asarray(conv_b, f32), conv_ln_g=np.asarray(conv_ln_g, f32),
        conv_ln_b=np.asarray(conv_ln_b, f32), conv_pw=np.asarray(conv_pw, f32), w_out=np.asarray(w_out, f32),
        ln1_g=np.asarray(ln1_g, f32), ln1_b=np.asarray(ln1_b, f32), ffn_w_gate=np.asarray(ffn_w_gate, f32),
        ffn_w_up=np.asarray(ffn_w_up, f32), ffn_w_down=np.asarray(ffn_w_down, f32), ln2_g=np.asarray(ln2_g, f32),
        ln2_b=np.asarray(ln2_b, f32))
    if _with_sample:
        shared['cache_k'] = np.asarray(cache_k, f32).reshape(DEPTH * 2560 * 128, 512)
        shared['cache_v'] = np.asarray(cache_v, f32).reshape(DEPTH * 2560 * 128, 512)
    consts = np.zeros((128, 16), f32)
    for r in range(128):
        h_, t_ = (r // 4) % 8, r % 4
        consts[r, h_] = 1.0
        for tn in range(4):
            consts[r, 8 + tn] = 0.0 if tn < t_ else -8.0 * MASKV
        consts[r, 12] = r
    ps_ = np.zeros((128, 2, 16), f32)
    for cc in range(2):
        for p in range(128):
            w = (2, 4, 8, 16)[cc * 2 + p // 64]
            for t_ in range(16):
                ps_[p, cc, t_] = 1.0 / min(t_ + 1, w)
    in_maps = []
    for c in range(NCORES):
        m = dict(shared)
        m['xp'] = np.ascontiguousarray(x_prompt[c])
        m['xs'] = np.ascontiguousarray(np.asarray(x_sample, f32)[NSEQ * c:NSEQ * (c + 1)].reshape(NS, D))
        m['pscale'] = ps_
        m['consts_in'] = consts
        if _with_sample:
            m['state_pool'] = np.ascontiguousarray(np.asarray(state_pool, f32)[:, NSEQ * c:NSEQ * (c + 1)])
            m['state_conv'] = np.ascontiguousarray(np.asarray(state_conv, f32)[:, NSEQ * c:NSEQ * (c + 1)])
            m['page_table'] = np.ascontiguousarray(np.asarray(page_table, np.int32)[NSEQ * c:NSEQ * (c + 1)])
        in_maps.append(m)
    res = run_bass_kernel_spmd(nc, in_maps, core_ids=list(range(NCORES)))
    R = res.results
    y_prompt = np.zeros((4, 4096, D), f32)
    k_prompt = np.zeros((DEPTH, 4, 4096, 8, 64), f32)
    v_prompt = np.zeros((DEPTH, 4, 4096, 8, 64), f32)
    pool_prompt = np.zeros((DEPTH, 4, 15, 256), f32)
    conv_prompt = np.zeros((DEPTH, 4, 30, 256), f32)
    y_sample = np.zeros((32, 4, D), f32)
    k_sample = np.zeros((DEPTH, 32, 4, 8, 64), f32)
    v_sample = np.zeros((DEPTH, 32, 4, 8, 64), f32)
    pool_sample = np.zeros((DEPTH, 32, 15, 256), f32)
    conv_sample = np.zeros((DEPTH, 32, 30, 256), f32)
    for c in range(NCORES):
        o = R[c]
        sl = slice(NSEQ * c, NSEQ * (c + 1))
        y_prompt[c] = o['y_p']
        k_prompt[:, c] = o['k_p'].reshape(DEPTH, T, 8, 64)
        v_prompt[:, c] = o['v_p'].reshape(DEPTH, T, 8, 64)
        pool_prompt[:, c] = o['pool_p']
        conv_prompt[:, c] = o['conv_p']
        y_sample[sl] = o['y_s'].reshape(NSEQ, 4, D)
        k_sample[:, sl] = o['k_s'].reshape(DEPTH, NSEQ, 4, 8, 64)
        v_sample[:, sl] = o['v_s'].reshape(DEPTH, NSEQ, 4, 8, 64)
        pool_sample[:, sl] = o['pool_s']
        conv_sample[:, sl] = o['conv_s']
    return (y_prompt, y_sample, k_prompt, v_prompt, pool_prompt, conv_prompt,
            k_sample, v_sample, pool_sample, conv_sample)
```

```python
import numpy as np
import concourse.bass as bass
import concourse.mybir as mybir
from concourse.bass_utils import run_bass_kernel_spmd

F32 = mybir.dt.float32
BF16 = mybir.dt.bfloat16
I32 = mybir.dt.int32
ALU = mybir.AluOpType
AF = mybir.ActivationFunctionType
AX = mybir.AxisListType

D = 1024
NIN = 5376
DFF = 2816
T = 4096
CH = 512
NSLOT = 8
NCORES = 4
NSEQ = 32 // NCORES
NS = NSEQ * 4
NPAGE = 64
DEPTH = 2
ALPHA = (2.0 * DEPTH) ** 0.25
EPS = 1e-5
CHUNKS = ([0, 3, 4, 7], [1, 2, 5, 6])
MASKV = 60.0
ENGS = ['pe', 'act', 'dve', 'pool', 'sp']


def chunk_loc(c):
    for r in range(2):
        if c in CHUNKS[r]:
            return r, CHUNKS[r].index(c)
    raise ValueError(c)


class Sched:
    def __init__(self):
        self.ops = {e: [] for e in ENGS}
        self.cnt = {e: 0 for e in ENGS}
        self.waited = {e: {} for e in ENGS}
        self.st = {}
        self.dma_cnt = {}
        self.seq = 0
        self.log = []

    def _split(self, k):
        return k if isinstance(k, tuple) else (k, None)

    def _base(self, b):
        return self.st.setdefault(b, {'w': None, 'r': {}, 'subs': {}})

    def _deps(self, reads, writes):
        deps = []

        def addev(e):
            if e is not None:
                deps.append(e)

        for k in reads:
            b, sub = self._split(k)
            B = self._base(b)
            addev(B['w'])
            if sub is None:
                for S in B['subs'].values():
                    addev(S['w'])
            else:
                S = B['subs'].setdefault(sub, {'w': None, 'r': {}})
                addev(S['w'])
        for k in writes:
            b, sub = self._split(k)
            B = self._base(b)
            addev(B['w'])
            deps.extend(B['r'].items())
            if sub is None:
                for S in B['subs'].values():
                    addev(S['w'])
                    deps.extend(S['r'].items())
            else:
                S = B['subs'].setdefault(sub, {'w': None, 'r': {}})
                addev(S['w'])
                deps.extend(S['r'].items())
        return deps

    def _update(self, reads, writes, ev):
        sk, v = ev
        for k in reads:
            b, sub = self._split(k)
            B = self._base(b)
            R = B['r'] if sub is None else B['subs'].setdefault(sub, {'w': None, 'r': {}})['r']
            R[sk] = max(R.get(sk, 0), v)
        for k in writes:
            b, sub = self._split(k)
            B = self._base(b)
            if sub is None:
                B['w'] = ev
                B['r'] = {}
                B['subs'] = {}
            else:
                S = B['subs'].setdefault(sub, {'w': None, 'r': {}})
                S['w'] = ev
                S['r'] = {}

    def _waits(self, eng, deps):
        need = {}
        for sk, v in deps:
            if sk == 'pe' and eng == 'pe':
                continue
            if isinstance(sk, tuple):
                v = max(v, 16 * self.dma_cnt.get(sk, 0))
            need[sk] = max(need.get(sk, 0), v)
        out = []
        W = self.waited[eng]
        for sk, v in need.items():
            if W.get(sk, 0) < v:
                W[sk] = v
                out.append((sk, v))
        return out

    def op(self, eng, fn, reads=(), writes=()):
        deps = self._deps(reads, writes)
        waits = self._waits(eng, deps)
        self.cnt[eng] += 1
        ev = (eng, self.cnt[eng])
        self.seq += 1
        self.log.append((self.seq, eng, 'op', list(reads), list(writes)))
        self.ops[eng].append((waits, fn, (eng, 1), self.seq))
        self._update(reads, writes, ev)

    def dma(self, q, fn, semkey, reads=(), writes=(), nowaw=False):
        deps = self._deps(reads, writes)
        if nowaw:
            deps = [d for d in deps if d[0] != ('d', semkey)]
        waits = self._waits(q, deps)
        sk = ('d', semkey)
        self.dma_cnt[sk] = self.dma_cnt.get(sk, 0) + 1
        ev = (sk, 16 * self.dma_cnt[sk])
        self.seq += 1
        self.log.append((self.seq, q, 'dma:' + str(semkey), list(reads), list(writes)))
        self.ops[q].append((waits, fn, (sk, 16), self.seq))
        self._update(reads, writes, ev)


def build_program(with_sample=True, n_layers=DEPTH, stop_after=None, npool=2560):
    nc = bass.Bass("TRN2", target_bir_lowering=False)
    S = Sched()

    def din(name, shape, dt=F32):
        return nc.dram_tensor(name, list(shape), dt, kind="ExternalInput").ap()

    def dout(name, shape, dt=F32):
        return nc.dram_tensor(name, list(shape), dt, kind="ExternalOutput").ap()

    def dscr(name, shape, dt):
        return nc.dram_tensor(name, list(shape), dt, kind="Internal").ap()

    xp = din("xp", [T, D])
    xs = din("xs", [NS, D])
    pscale = din("pscale", [128, 2, 16])
    consts_in = din("consts_in", [128, 16])
    w_in = din("w_in", [DEPTH, D, NIN])
    b_gate = din("b_gate", [DEPTH, 3 * D])
    sb_bias = din("sb_bias", [DEPTH, 8])
    pool_w = din("pool_w", [DEPTH, 4, 64, 256])
    pool_scale = din("pool_scale", [DEPTH, D])
    w_att_o = din("w_att_o", [DEPTH, 512, D])
    conv_dw = din("conv_dw", [DEPTH, 31, 256])
    conv_b = din("conv_b", [DEPTH, 256])
    conv_ln_g = din("conv_ln_g", [DEPTH, 256])
    conv_ln_b = din("conv_ln_b", [DEPTH, 256])
    conv_pw = din("conv_pw", [DEPTH, 256, D])
    w_out = din("w_out", [DEPTH, D, D])
    ln1_g = din("ln1_g", [DEPTH, D])
    ln1_b = din("ln1_b", [DEPTH, D])
    ffn_g = din("ffn_w_gate", [DEPTH, D, DFF])
    ffn_u = din("ffn_w_up", [DEPTH, D, DFF])
    ffn_d = din("ffn_w_down", [DEPTH, DFF, D])
    ln2_g = din("ln2_g", [DEPTH, D])
    ln2_b = din("ln2_b", [DEPTH, D])
    if with_sample:
        cache_k = din("cache_k", [DEPTH * npool * 64, 1024])
        cache_v = din("cache_v", [DEPTH * npool * 64, 1024])
        state_pool = din("state_pool", [DEPTH, NSEQ, 15, 256])
        state_conv = din("state_conv", [DEPTH, NSEQ, 30, 256])
        page_table = din("page_table", [NSEQ, NPAGE], I32)

    y_p = dout("y_p", [T, D])
    k_p = dout("k_p", [DEPTH, T, 512])
    v_p = dout("v_p", [DEPTH, T, 512])
    pool_p = dout("pool_p", [DEPTH, 15, 256])
    conv_p = dout("conv_p", [DEPTH, 30, 256])
    y_s = dout("y_s", [NS, D])
    k_s = dout("k_s", [DEPTH, NS, 512])
    v_s = dout("v_s", [DEPTH, NS, 512])
    pool_s = dout("pool_s", [DEPTH, NSEQ, 15, 256])
    conv_s = dout("conv_s", [DEPTH, NSEQ, 30, 256])

    wb_in = dscr("wb_in", [DEPTH, D, NIN], BF16)
    wb_ao = dscr("wb_ao", [DEPTH, 512, D], BF16)
    wb_pw = dscr("wb_pw", [DEPTH, 256, D], BF16)
    wb_out = dscr("wb_out", [DEPTH, D, D], BF16)
    wb_g = dscr("wb_g", [DEPTH, D, DFF], BF16)
    wb_u = dscr("wb_u", [DEPTH, D, DFF], BF16)
    wb_d = dscr("wb_d", [DEPTH, DFF, D], BF16)
    wb_pool = dscr("wb_pool", [DEPTH, 256, 256], BF16)
    TT = T + NS
    xT_scr = dscr("xT_scr", [D, TT], BF16)
    qT_scr = dscr("qT_scr", [512, TT], BF16)
    KT_loc = dscr("KT_loc", [512, T], BF16)
    V_loc = dscr("V_loc", [T, 512], BF16)
    ug_scr = dscr("ug_scr", [512, TT], F32)
    X1 = dscr("X1", [T, D], F32)
    X1s = dscr("X1s", [NS, D], F32)
    ksT_scr = dscr("ksT_scr", [512, NS], BF16)
    XM = dscr("XM", [TT, D], F32)

    import contextlib
    es = contextlib.ExitStack()

    def sb(name, shape, dt):
        return es.enter_context(nc.sbuf_tensor(name, list(shape), dt))

    ident_b = sb("ident_b", [128, 128], BF16)
    ident_f = sb("ident_f", [128, 128], F32)
    onesm = sb("onesm", [128, 128], F32)
    diagMB = sb("diagMB", [128, 4, 512], BF16)
    par = sb("par", [128, 64], F32)
    nbias = sb("nbias", [128, 5, 2, 8], F32)
    lnt = sb("lnt", [128, 4, D], F32)
    cdw = sb("cdw", [128, 2, 31], F32)
    slabs = [sb(f"slab{i}", [128, 8, 512], BF16) for i in range(4)]
    R32 = [sb(f"R32_{i}", [128, D], F32) for i in range(4)]
    BF8 = sb("BF8", [128, 4, D], BF16)
    xT = sb("xT", [128, 8, CH], BF16)
    BFA = sb("BFA", [128, 22 * CH], BF16)
    ao_sb = sb("ao_sb", [128, 4, D], BF16)
    pw_sb = sb("pw_sb", [128, 2, D], BF16)
    poolw_sb = sb("poolw_sb", [128, 2, 256], BF16)
    pscale_sb = sb("pscale_sb", [128, 2, 16], F32)
    stats = sb("stats", [128, 2, 6], F32)
    mv = sb("mv", [128, 4], F32)
    F32A = sb("F32A", [128, 5440], F32)
    vnew = sb("vnew", [128, NSEQ, 512], BF16)
    invw = sb("invw", [128, 2], F32)
    consts_sb = sb("consts_sb", [128, 16], F32)
    kn = sb("kn", [128, 4, NS], BF16)
    QZ = sb("QZ", [128, 16, 128], BF16)
    ptf_i = sb("ptf_i", [128, NSEQ * NPAGE], I32)
    ptf = sb("ptf", [128, NSEQ * NPAGE], F32)
    pidx = ptf_i
    mnew = sb("mnew", [128, 8], F32)
    Pnew = sb("Pnew", [128, 8], F32)
    anew = sb("anew", [128, 8], BF16)
    anT = sb("anT", [128, 128], BF16)
    rbias = sb("rbias", [128, 8], F32)
    pgs = [sb(f"pg{i}", [128, 1024], F32) for i in range(4)]
    STw = [sb(f"STw{i}", [128, 1024], BF16) for i in range(2)]
    ptf2 = sb("ptf2", [128, NSEQ * 32], F32)
    pidx2 = sb("pidx2", [128, NSEQ * 32], I32)
    aB = [sb(f"aB{i}", [128, 1024], BF16) for i in range(2)]
    aTB = [sb(f"aTB{i}", [128, 1024], BF16) for i in range(2)]
    qT = sb("qT", [128, 4, CH], BF16)
    ST = [sb(f"ST{i}", [128, 512], F32) for i in range(3)]
    STb = [sb(f"STb{i}", [128, 512], BF16) for i in range(3)]
    small = sb("small", [128, 64], F32)
    psF_t = es.enter_context(nc.psum_tensor("psF", [128, 6, 512], F32))
    psB_t = es.enter_context(nc.psum_tensor("psB", [128, 2, 1024], BF16))

    oT_v = BFA[:, 0:4 * CH].rearrange("p (h t) -> p h t", h=4)
    mg_v = BFA[:, 4 * CH:12 * CH].rearrange("p (c t) -> p c t", c=8)
    gt_v = BFA[:, 12 * CH:15 * CH].rearrange("p (c t) -> p c t", c=3)
    pT_v = BFA[:, 15 * CH:17 * CH].rearrange("p (c t) -> p c t", c=2)
    sT_v = BFA[:, 17 * CH:19 * CH].rearrange("p (c t) -> p c t", c=2)
    hT_v = BFA[:, 0:22 * CH].rearrange("p (c t) -> p c t", c=22)
    mB = [F32A[:, i * 1024:(i + 1) * 1024] for i in range(2)]
    PB = [F32A[:, 2048 + i * 1025:2048 + (i + 1) * 1025] for i in range(2)]
    EXT = 30 + CH
    XL = 544
    extu = F32A[:, 0:2 * EXT].rearrange("p (c t) -> p c t", c=2)
    extg = F32A[:, 2 * EXT:4 * EXT].rearrange("p (c t) -> p c t", c=2)
    X0 = 4 * EXT
    cw = [F32A[:, X0 + i * 2 * XL:X0 + (i + 1) * 2 * XL].rearrange("p (c t) -> p c t", c=2) for i in range(2)]
    ptmp = F32A[:, X0 + 4 * XL:X0 + 5 * XL]
    ptmp2 = F32A[:, X0 + 5 * XL:X0 + 6 * XL]
    cacc = [F32A[:, X0 + i * 1024:X0 + (i + 1) * 1024].rearrange("p (c t) -> p c t", c=2) for i in range(2)]
    pg_i = [0]

    def pagebuf():
        i = pg_i[0] % 4
        pg_i[0] += 1
        return pgs[i], f"pg{i}"

    sems = {}

    psf_i = [0]

    def psf():
        i = psf_i[0] % 6
        psf_i[0] += 1
        return psF_t[:, i, :], ('psF', i)

    psb_i = [0]

    def psb():
        i = psb_i[0] % 2
        psb_i[0] += 1
        return psB_t[:, i, :], ('psB', i)

    slab_i = [0]

    def next_slab():
        i = slab_i[0] % 4
        slab_i[0] += 1
        return slabs[i], f"slab{i}"

    def mm(out, lhsT, rhs, start, stop, reads, writes):
        S.op('pe', lambda e: e.matmul(out, lhsT, rhs, start=start, stop=stop), reads, writes)

    def tr(out, in_, ident, reads, writes):
        S.op('pe', lambda e: e.transpose(out, in_, ident), reads, writes)

    def act(out, in_, func, reads, writes, bias=None, scale=None):
        kw = {}
        if bias is not None:
            kw['bias'] = bias
        if scale is not None:
            kw['scale'] = scale
        S.op('act', lambda e: e.activation(out, in_, func, **kw), reads, writes)

    def vop(eng, name, args, reads, writes, **kw):
        S.op(eng, lambda e: getattr(e, name)(*args, **kw), reads, writes)

    def load(out, in_, semkey, reads=(), writes=(), q='sp', nowaw=False):
        S.dma(q, lambda e: e.dma_start(out=out, in_=in_), semkey, reads, writes, nowaw=nowaw)

    def load_nc(out, in_, semkey, reads=(), writes=(), q='sp', nowaw=False):
        S.dma(q, lambda e: e.dma_start(out=out, in_=in_, allow_slow_non_contiguous=True), semkey, reads, writes, nowaw=nowaw)

    S.op('pool', lambda e: e.memset(ident_f[:], 0.0), (), ['ident_f'])
    S.op('pool', lambda e: e.memset(onesm[:], 1.0), (), ['onesm'])
    S.op('pool', lambda e: e.affine_select(ident_f[:], onesm[:], [[-1, 128]], ALU.is_equal, 0.0, base=0, channel_multiplier=1),
         ['onesm'], ['ident_f'])
    vop('dve', 'tensor_copy', (ident_b[:], ident_f[:]), ['ident_f'], ['ident_b'])
    S.op('pool', lambda e: e.memset(ST[0][:], -8.0 * MASKV), (), ['ST0'])
    for i in range(4):
        S.op('pool', lambda e, i=i: e.affine_select(ST[1][:], ST[0][:], [[1, 512]], ALU.is_ge, 0.0, base=-128 * i, channel_multiplier=-1),
             ['ST0'], ['ST1'])
        vop('dve', 'tensor_copy', (diagMB[:, i, :], ST[1][:]), ['ST1'], [('diagMB', i)])
    vop('dve', 'tensor_scalar', (onesm[:], onesm[:], 1.0 / 256.0, None, ALU.mult), ['onesm'], ['onesm'])
    def conv_w(dst, src, rows, l, key):
        step = 128
        for r0 in range(0, rows, step):
            S.dma('pool', lambda e, r0=r0: e.dma_start(out=dst[l, r0:r0 + step, :], in_=src[l, r0:r0 + step, :]),
                  key, (), [(key, l)])

    def convert_layer(l):
        conv_w(wb_in, w_in, D, l, 'wb_in')
        conv_w(wb_ao, w_att_o, 512, l, 'wb_ao')
        conv_w(wb_pw, conv_pw, 256, l, 'wb_pw')
        pw2 = pool_w.rearrange("l g c n -> l (g c) n")
        conv_w(wb_pool, pw2, 256, l, 'wb_pool')
        conv_w(wb_out, w_out, D, l, 'wb_out')
        conv_w(wb_g, ffn_g, D, l, 'wb_g')
        conv_w(wb_u, ffn_u, D, l, 'wb_u')
        conv_w(wb_d, ffn_d, DFF, l, 'wb_d')

    def load_slab(src2d, r0, nk, c0, ncols, wkey):
        sl, sk = next_slab()
        v = src2d[r0:r0 + nk * 128, c0:c0 + ncols].rearrange("(k p) n -> p k n", p=128)
        load(sl[:, 0:nk, 0:ncols], v, sk, [wkey], [sk])
        return sl, sk

    def layer_params(l):
        load_nc(par[:, 0:24], b_gate[l].rearrange("(c p) -> p c", p=128), 'par', (), ['par'])
        load_nc(par[:, 24:32], pool_scale[l].rearrange("(c p) -> p c", p=128), 'par', (), [('par', 1)], nowaw=True)
        load_nc(par[:, 32:34], conv_b[l].rearrange("(c p) -> p c", p=128), 'par', (), [('par', 2)], nowaw=True)
        load_nc(par[:, 34:36], conv_ln_g[l].rearrange("(c p) -> p c", p=128), 'par', (), [('par', 3)], nowaw=True)
        load_nc(par[:, 36:38], conv_ln_b[l].rearrange("(c p) -> p c", p=128), 'par', (), [('par', 4)], nowaw=True)
        load_nc(par[:, 40:48], sb_bias[l:l + 1, :].partition_broadcast(128), 'par', (), [('par', 5)], nowaw=True)
        for c in range(2):
            load_nc(cdw[:, c, :], conv_dw[l][:, c * 128:(c + 1) * 128].rearrange("k p -> p k"), 'cdw', (), ['cdw'], nowaw=(c > 0))
        for i, t in enumerate((ln1_g, ln1_b, ln2_g, ln2_b)):
            load_nc(lnt[:, i, :], t[l:l + 1, :].partition_broadcast(128), 'lnt', (), [('lnt', i)], nowaw=True)
        vop('dve', 'tensor_scalar', (nbias[:, 4, 0, :], par[:, 40:48], -1.0, None, ALU.mult), [('par', 5)], ['nbias'])

    def build_xT(src_rows, ntok, tok0):
        nblk = (ntok + 127) // 128
        for b in range(nblk):
            n = min(128, ntok - b * 128)
            r = R32[b % 4]
            rk = f"R32_{b % 4}"
            load(r[0:n, :], src_rows[b * 128:b * 128 + n, :], rk, (), [rk])
            vop('dve' if b % 2 == 0 else 'pool', 'tensor_copy', (BF8[0:n, b, :], r[0:n, :]), [rk], [('BF8', b)])
            for g in range(2):
                ps, pk = psb()
                for c in range(4):
                    kc = g * 4 + c
                    tr(ps[:, c * 128:c * 128 + n], BF8[0:n, b, kc * 128:(kc + 1) * 128], ident_b[0:n, 0:n],
                       [('BF8', b), 'ident_b'], [pk])
                outv = xT[:, g * 4:(g + 1) * 4, b * 128:b * 128 + n]
                inv = ps[:, 0:512].rearrange("p (c t) -> p c t", c=4)[:, :, 0:n]
                if g == 0:
                    act(outv, inv, AF.Copy, [pk], [('xT', b)])
                else:
                    vop('dve', 'tensor_copy', (outv, inv), [pk], [('xT', b)])
        load(xT_scr[:, tok0:tok0 + ntok].rearrange("(k p) t -> p k t", p=128), xT[:, :, 0:ntok], 'xT', ['xT'], ['xT_scr'])

    RG = [[0, 1], [2, 3], [4, 5], [6, 7]]
    cnt_bar = [0]

    def barrier(keys, eng='dve'):
        c = 56 + (cnt_bar[0] % 8)
        cnt_bar[0] += 1
        vop(eng, 'memset', (small[:, c:c + 1], 0.0), (), list(keys) + [('small', c)])

    st_i = [0]

    def stage32():
        i = st_i[0] % 3
        st_i[0] += 1
        return ST[i], f"ST{i}"

    stb_i = [0]

    def stage16():
        i = stb_i[0] % 3
        stb_i[0] += 1
        return STb[i], f"STb{i}"

    stw_i = [0]

    def stage16w():
        i = stw_i[0] % 2
        stw_i[0] += 1
        return STw[i], f"STw{i}"

    def fm_group(sl, sk, ncols_chunks, ntok, nk=8):
        outs = []
        for nn in ncols_chunks:
            ps, pk = psf()
            for kc in range(nk):
                mm(ps[:, 0:ntok], sl[:, kc, nn * 128:(nn + 1) * 128], xT[:, kc, 0:ntok], kc == 0, kc == nk - 1,
                   [sk, 'xT', 'ident_b'], [pk])
            outs.append((ps, pk))
        return outs

    def rows_ln(y, yk, n, gi, outbuf, outk):
        for hf in range(2):
            vop('dve', 'bn_stats', (stats[0:n, hf, :], y[0:n, hf * 512:(hf + 1) * 512]), [yk], [('stats', hf)])
        vop('dve', 'bn_aggr', (mv[0:n, 0:2], stats[0:n, :, :].rearrange("p a b -> p (a b)")), ['stats'], [('mv', 0)])
        vop('dve', 'tensor_scalar', (mv[0:n, 2:3], mv[0:n, 1:2], EPS, None, ALU.add), [('mv', 0)], [('mv', 1)])
        act(mv[0:n, 2:3], mv[0:n, 2:3], AF.Sqrt, [('mv', 1)], [('mv', 1)])
        vop('dve', 'reciprocal', (mv[0:n, 2:3], mv[0:n, 2:3]), [('mv', 1)], [('mv', 1)])
        vop('dve', 'tensor_scalar', (y[0:n, :], y[0:n, :], mv[0:n, 0:1], mv[0:n, 2:3], ALU.subtract, ALU.mult),
            [yk, ('mv', 0), ('mv', 1)], [yk])
        vop('dve', 'tensor_tensor', (y[0:n, :], y[0:n, :], lnt[0:n, gi, :], ALU.mult), [yk, ('lnt', gi)], [yk])
        vop('pool', 'tensor_tensor', (outbuf[0:n, :], y[0:n, :], lnt[0:n, gi + 1, :], ALU.add), [yk, ('lnt', gi + 1)], [outk])

    def rows_to_xT(r, rk, b, n):
        vop('pool', 'tensor_copy', (BF8[0:n, b, :], r[0:n, :]), [rk], [('BF8', b)])
        for g in range(2):
            ps, pk = psb()
            for c in range(4):
                kc = g * 4 + c
                tr(ps[:, c * 128:c * 128 + n], BF8[0:n, b, kc * 128:(kc + 1) * 128], ident_b[0:n, 0:n],
                   [('BF8', b), 'ident_b'], [pk])
            outv = xT[:, g * 4:(g + 1) * 4, b * 128:b * 128 + n]
            inv = ps[:, 0:512].rearrange("p (c t) -> p c t", c=4)[:, :, 0:n]
            if g == 0:
                act(outv, inv, AF.Copy, [pk], [('xT', b)])
            else:
                vop('dve', 'tensor_copy', (outv, inv), [pk], [('xT', b)])

    def p1(l, x_src, ntok, tok0, slot):
        sample = slot is None
        nblk = (ntok + 127) // 128
        for b in range(nblk):
            n = min(128, ntok - b * 128)
            r, rk = R32[b % 4], f"R32_{b % 4}"
            load(r[0:n, :], x_src[b * 128:b * 128 + n, :], rk, (), [rk])
            rows_to_xT(r, rk, b, n)
        load(xT_scr[:, tok0:tok0 + ntok].rearrange("(k p) t -> p k t", p=128), xT[:, :, 0:ntok], 'xT', ['xT'], ['xT_scr'])
        wl = wb_in[l]
        sl, sk = load_slab(wl, 0, 8, 0, 512, ('wb_in', l))
        outs = fm_group(sl, sk, range(4), ntok)
        for c in range(2):
            ps, pk = outs[c]
            st, stk = stage32()
            act(st[:, 0:ntok], ps[:, 0:ntok], AF.Copy, [pk], [stk])
            load(ug_scr[c * 128:(c + 1) * 128, tok0:tok0 + ntok], st[:, 0:ntok], stk, [stk], [('ug_scr', 'u')], nowaw=True)
        for c in range(2):
            ps, pk = outs[2 + c]
            st, stk = stage16()
            vop('dve', 'tensor_copy', (st[:, 0:ntok], ps[:, 0:ntok]), [pk], [stk])
            load(qT_scr[c * 128:(c + 1) * 128, tok0:tok0 + ntok], st[:, 0:ntok], stk, [stk], ['qT_scr'])
        last_b = nblk - 1
        nl = min(128, ntok - last_b * 128)
        if sample or slot == NSLOT - 1:
            ps, pk = psf()
            for kc in range(8):
                mm(ps[0:nl, 0:256], xT[:, kc, last_b * 128:last_b * 128 + nl], sl[:, kc, 0:256], kc == 0, kc == 7, [sk, 'xT'], [pk])
            st, stk = stage32()
            act(st[0:nl, 0:256], ps[0:nl, 0:256], AF.Copy, [pk], [stk])
            if sample:
                for s_ in range(NSEQ):
                    load(pool_s[l, s_, 11:15, :], st[4 * s_:4 * s_ + 4, 0:256], stk, [stk], ['pool_s'], nowaw=True)
            else:
                load(pool_p[l, :, :], st[128 - 15:128, 0:256], stk, [stk], ['pool_p'])
        sl, sk = load_slab(wl, 0, 8, 512, 256, ('wb_in', l))
        outs = fm_group(sl, sk, range(2), ntok)
        for c in range(2):
            ps, pk = outs[c]
            st, stk = stage16()
            vop('dve', 'tensor_copy', (st[:, 0:ntok], ps[:, 0:ntok]), [pk], [stk])
            load(qT_scr[(2 + c) * 128:(3 + c) * 128, tok0:tok0 + ntok], st[:, 0:ntok], stk, [stk], ['qT_scr'])
        sl, sk = load_slab(wl, 0, 8, 768, 512, ('wb_in', l))
        outs = fm_group(sl, sk, range(4), ntok)
        for c in range(4):
            ps, pk = outs[c]
            st, stk = stage16()
            act(st[:, 0:ntok], ps[:, 0:ntok], AF.Copy, [pk], [stk])
            if sample:
                load_nc(ksT_scr[c * 128:(c + 1) * 128, :], st[:, 0:ntok], stk, [stk], ['ksT_scr'])
            else:
                load(KT_loc[c * 128:(c + 1) * 128, tok0:tok0 + ntok], st[:, 0:ntok], stk, [stk], ['KT_loc'])
        kout = k_s if sample else k_p
        vout = v_s if sample else v_p
        for b in range(nblk):
            n = min(128, ntok - b * 128)
            ps, pk = psf()
            for kc in range(8):
                mm(ps[0:n, :], xT[:, kc, b * 128:b * 128 + n], sl[:, kc, :], kc == 0, kc == 7, [sk, 'xT'], [pk])
            st, stk = stage32()
            act(st[0:n, :], ps[0:n, :], AF.Copy, [pk], [stk])
            load(kout[l, tok0 - (T if sample else 0) + b * 128:tok0 - (T if sample else 0) + b * 128 + n, :], st[0:n, :], stk, [stk], ['kout'])
        sl, sk = load_slab(wl, 0, 8, 1280, 512, ('wb_in', l))
        for b in range(nblk):
            n = min(128, ntok - b * 128)
            ps, pk = psf()
            for kc in range(8):
                mm(ps[0:n, :], xT[:, kc, b * 128:b * 128 + n], sl[:, kc, :], kc == 0, kc == 7, [sk, 'xT'], [pk])
            st, stk = stage32()
            act(st[0:n, :], ps[0:n, :], AF.Copy, [pk], [stk])
            t0_ = tok0 - (T if sample else 0) + b * 128
            load(vout[l, t0_:t0_ + n, :], st[0:n, :], stk, [stk], ['vout'])
            if not sample:
                sb_, sbk = stage16()
                vop('dve', 'tensor_copy', (sb_[0:n, :], st[0:n, :]), [stk], [sbk])
                load(V_loc[tok0 + b * 128:tok0 + b * 128 + n, :], sb_[0:n, :], sbk, [sbk], ['V_loc'])
        if sample:
            for s_ in range(NSEQ):
                ps, pk = psf()
                for kc in range(8):
                    mm(ps[0:4, :], xT[:, kc, 4 * s_:4 * s_ + 4], sl[:, kc, :], kc == 0, kc == 7, [sk, 'xT'], [pk])
                vop('dve', 'tensor_copy', (vnew[0:4, s_, :], ps[0:4, :]), [pk], [('vnew', s_)])
        sl, sk = load_slab(wl, 0, 8, 1792, 512, ('wb_in', l))
        outs = fm_group(sl, sk, range(4), ntok)
        for c in range(2):
            pa, pak = outs[c]
            pg, pgk = outs[2 + c]
            st, stk = stage32()
            act(st[:, 0:ntok], pg[:, 0:ntok], AF.Sigmoid, [pgk], [stk])
            st2, st2k = stage32()
            vop('dve', 'tensor_tensor', (st2[:, 0:ntok], pa[:, 0:ntok], st[:, 0:ntok], ALU.mult), [pak, stk], [st2k])
            load(ug_scr[256 + c * 128:256 + (c + 1) * 128, tok0:tok0 + ntok], st2[:, 0:ntok], st2k, [st2k], [('ug_scr', 'g')], nowaw=True)
        if sample or slot == NSLOT - 1:
            ps, pk = psf()
            for kc in range(8):
                mm(ps[0:nl, :], xT[:, kc, last_b * 128:last_b * 128 + nl], sl[:, kc, :], kc == 0, kc == 7, [sk, 'xT'], [pk])
            st, stk = stage32()
            act(st[0:nl, 0:256], ps[0:nl, 256:512], AF.Sigmoid, [pk], [stk])
            st2, st2k = stage32()
            vop('dve', 'tensor_tensor', (st2[0:nl, 0:256], ps[0:nl, 0:256], st[0:nl, 0:256], ALU.mult), [pk, stk], [st2k])
            if sample:
                for s_ in range(NSEQ):
                    load(conv_s[l, s_, 26:30, :], st2[4 * s_:4 * s_ + 4, 0:256], st2k, [st2k], ['conv_s'], nowaw=True)
            else:
                load(conv_p[l, :, :], st2[128 - 30:128, 0:256], st2k, [st2k], ['conv_p'])

    def attention(l, j, tok0):
        nchunk = j + 1
        load(qT[:, :, :], qT_scr[:, tok0:tok0 + CH].rearrange("(c p) t -> p c t", p=128), 'qT', ['qT_scr'], ['qT'])
        npiece = (nchunk + 1) // 2
        P = []
        for pr in range(4):
            for hh in range(2):
                for i in range(4):
                    for pc in range(npiece - 1, -1, -1):
                        P.append(dict(pr=pr, hh=hh, i=i, pc=pc, first=(pc == npiece - 1), last=(pc == 0), chain=(pr * 2 + hh) * 4 + i))
        slabs_pr = {}

        def get_slabs(pr):
            if pr not in slabs_pr:
                ksl, kk = next_slab()
                vsl, vk = next_slab()
                vview = vsl[:, :, :].rearrange("p a b -> p (a b)").rearrange("p (k c) -> p k c", c=128)
                load(ksl[:, 0:nchunk, :], KT_loc[pr * 128:(pr + 1) * 128, 0:nchunk * CH].rearrange("p (c t) -> p c t", t=CH), kk, ['KT_loc'], [kk])
                for c in range(nchunk):
                    load_nc(vview[:, 4 * c:4 * c + 4, :],
                            V_loc[c * CH:(c + 1) * CH, pr * 128:(pr + 1) * 128].rearrange("(b p) n -> p b n", p=128),
                            vk, ['V_loc'], [vk], nowaw=(c > 0))
                slabs_pr[pr] = (ksl, kk, vview, vk)
            return slabs_pr[pr]

        def keys(n):
            s_ = n % 2
            return s_, ('F32A', f'm{s_}'), ('F32A', f'P{s_}'), f'aB{s_}', f'aTB{s_}'

        def A1(n):
            p = P[n]
            pr, hh, i, pc = p['pr'], p['hh'], p['i'], p['pc']
            ksl, kk, vview, vk = get_slabs(pr)
            h = 2 * pr + hh
            hs = slice(hh * 64, (hh + 1) * 64)
            s_, mk, Pk, ak, atk = keys(n)
            nt = min(2, nchunk - 2 * pc)
            for tl in range(nt - 1, -1, -1):
                c = 2 * pc + tl
                zi = psf_i[0] % 4
                psf_i[0] += 1
                ps, pk = psF_t[:, zi, :], ('psF', zi)
                masked = (c == j)
                mm(ps, qT[hs, pr, i * 128:(i + 1) * 128], ksl[hs, c, :], True, not masked, ['qT', kk], [pk])
                if masked:
                    mm(ps, ident_b[:, :], diagMB[:, i, :], False, True, ['ident_b', 'diagMB'], [pk])
                act(mB[s_][:, tl * 512:(tl + 1) * 512], ps, AF.Sigmoid, [pk, 'nbias'], [mk],
                    bias=nbias[:, 4, 0, h:h + 1], scale=-0.125)

        def A2(n):
            p = P[n]
            pc = p['pc']
            s_, mk, Pk, ak, atk = keys(n)
            nt = min(2, nchunk - 2 * pc)
            Lp = nt * CH
            if p['first']:
                vop('dve', 'memset', (PB[s_][:, Lp:Lp + 1], 1.0), (), [Pk])
                init = 1.0
                rd = [mk]
            else:
                prev = (n - 1) % 2
                vop('dve', 'tensor_copy', (PB[s_][:, Lp:Lp + 1], PB[prev][:, 0:1]), [('F32A', f'P{prev}')], [Pk])
                init = PB[prev][:, 0:1]
                rd = [mk, ('F32A', f'P{prev}')]
            vop('dve', 'tensor_tensor_scan', (PB[s_][:, 0:Lp][:, ::-1], mB[s_][:, 0:Lp][:, ::-1], mB[s_][:, 0:Lp][:, ::-1], init, ALU.mult, ALU.min),
                rd, [Pk])
            vop('pool', 'tensor_tensor', (aB[s_][:, 0:Lp], PB[s_][:, 1:Lp + 1], PB[s_][:, 0:Lp], ALU.subtract), [Pk], [ak])

        def Bst(n):
            p = P[n]
            pr, hh, i, pc = p['pr'], p['hh'], p['i'], p['pc']
            ksl, kk, vview, vk = get_slabs(pr)
            hs = slice(hh * 64, (hh + 1) * 64)
            s_, mk, Pk, ak, atk = keys(n)
            nt = min(2, nchunk - 2 * pc)
            Lp = nt * CH
            oi = 4 + (p['chain'] % 2)
            ops_, opk = psF_t[hs, oi, 0:128], ('psF', oi)
            ps, pk = psb()
            for k in range(4 * nt):
                tr(ps[:, k * 128:(k + 1) * 128], aB[s_][:, k * 128:(k + 1) * 128], ident_b[:, :], [ak, 'ident_b'], [pk])
            act(aTB[s_][:, 0:Lp], ps[:, 0:Lp], AF.Copy, [pk], [atk])
            for k in range(4 * nt):
                blk = pc * 8 + k
                lastmm = (p['last'] and k == 4 * nt - 1)
                mm(ops_, vview[:, blk, hs], aTB[s_][:, k * 128:(k + 1) * 128], p['first'] and k == 0, lastmm, [vk, atk], [opk])
            if p['last']:
                vop('dve', 'tensor_copy', (oT_v[hs, pr, i * 128:(i + 1) * 128], ops_), [opk], [('BFA', 'oT')])

        N = len(P)
        for n in range(N + 2):
            if n < N:
                A1(n)
            if 1 <= n <= N:
                A2(n - 1)
            if n >= 2:
                Bst(n - 2)

    def poolconv(n, off, first16=False):
        L = 30 + n
        s2, s4 = cw[0], cw[1]
        vop('dve', 'tensor_tensor', (s2[:, :, 1:L], extu[:, :, 1:L], extu[:, :, 0:L - 1], ALU.add), [('F32A', 'extu')], [('F32A', 'cw0')])
        vop('dve', 'tensor_tensor', (s4[:, :, 3:L], s2[:, :, 3:L], s2[:, :, 1:L - 2], ALU.add), [('F32A', 'cw0')], [('F32A', 'cw1')])
        tk = ('F32A', 'tmp')
        vop('dve', 'tensor_tensor', (ptmp[:, 7:L], s4[:, 1, 7:L], s4[:, 1, 3:L - 4], ALU.add), [('F32A', 'cw1')], [tk])
        vop('dve', 'tensor_tensor', (ptmp2[:, 15:L], ptmp[:, 15:L], ptmp[:, 7:L - 8], ALU.add), [tk], [('F32A', 'tmp2')])
        srcs = [(s2, 0, 0, ('F32A', 'cw0')), (s4, 0, 1, ('F32A', 'cw1')), (None, 1, 0, tk), (None, 1, 1, ('F32A', 'tmp2'))]
        for g, (sbuf_, c, hh, key) in enumerate(srcs):
            hs = slice(hh * 64, (hh + 1) * 64)
            src = sbuf_[hs, c, 30:L] if sbuf_ is not None else (ptmp[hs, 30:L] if g == 2 else ptmp2[hs, 30:L])
            if first16:
                st, stk = stage32()
                vop('dve', 'tensor_tensor', (st[hs, 0:16], (sbuf_[hs, c, 30:46] if sbuf_ is not None else (ptmp[hs, 30:46] if g == 2 else ptmp2[hs, 30:46])),
                                             pscale_sb[hs, c, :], ALU.mult), [key, 'pscale_sb'], [stk])
                vop('dve', 'tensor_tensor', (pT_v[hs, c, off:off + 16], st[hs, 0:16], extu[hs, c, 30:46], ALU.subtract),
                    [stk, ('F32A', 'extu')], [('BFA', 'pT')])
                vop('dve', 'scalar_tensor_tensor', (pT_v[hs, c, off + 16:off + n], src[:, 16:n], invw[hs, c:c + 1], extu[hs, c, 46:L], ALU.mult, ALU.subtract),
                    [key, ('F32A', 'extu'), 'invw'], [('BFA', 'pT')])
            else:
                vop('dve', 'scalar_tensor_tensor', (pT_v[hs, c, off:off + n], src, invw[hs, c:c + 1], extu[hs, c, 30:L], ALU.mult, ALU.subtract),
                    [key, ('F32A', 'extu'), 'invw'], [('BFA', 'pT')])
        barrier(['F32A'])
        for c in range(2):
            eng = 'dve'
            for k in range(31):
                dst = cacc[k % 2][:, c, 0:n]
                dk = ('F32A', f'cacc{k % 2}_{c}')
                if k == 0:
                    vop(eng, 'tensor_scalar', (dst, extg[:, c, 0:n], cdw[:, c, 0:1], None, ALU.mult), [('F32A', 'extg'), 'cdw'], [dk])
                else:
                    srck = ('F32A', f'cacc{(k - 1) % 2}_{c}')
                    vop(eng, 'scalar_tensor_tensor', (dst, extg[:, c, k:k + n], cdw[:, c, k:k + 1], cacc[(k - 1) % 2][:, c, 0:n], ALU.mult, ALU.add),
                        [('F32A', 'extg'), 'cdw', srck], [dk])
            vop(eng, 'tensor_scalar', (cacc[0][:, c, 0:n], cacc[0][:, c, 0:n], par[:, 32 + c:33 + c], None, ALU.add),
                [('F32A', f'cacc0_{c}'), ('par', 2)], [('F32A', f'cacc0_{c}')])
            act(cacc[1][:, c, 0:n], cacc[0][:, c, 0:n], AF.Square, [('F32A', f'cacc0_{c}')], [('F32A', f'cacc1_{c}')])
        pm, pmk = psf()
        pe2, pe2k = psf()
        for c in range(2):
            mm(pm[:, 0:n], onesm[:, :], cacc[0][:, c, 0:n], c == 0, c == 1, ['onesm', ('F32A', f'cacc0_{c}')], [pmk])
        for c in range(2):
            mm(pe2[:, 0:n], onesm[:, :], cacc[1][:, c, 0:n], c == 0, c == 1, ['onesm', ('F32A', f'cacc1_{c}')], [pe2k])
        st, stk = stage32()
        act(st[:, 0:n], pm[:, 0:n], AF.Copy, [pmk], [stk])
        st2, st2k = stage32()
        vop('dve', 'tensor_tensor', (st2[:, 0:n], st[:, 0:n], st[:, 0:n], ALU.mult), [stk], [st2k])
        vop('dve', 'tensor_tensor', (st2[:, 0:n], pe2[:, 0:n], st2[:, 0:n], ALU.subtract), [pe2k, st2k], [st2k])
        vop('dve', 'tensor_scalar', (st2[:, 0:n], st2[:, 0:n], EPS, None, ALU.add), [st2k], [st2k])
        act(st2[:, 0:n], st2[:, 0:n], AF.Sqrt, [st2k], [st2k])
        vop('dve', 'reciprocal', (st2[:, 0:n], st2[:, 0:n]), [st2k], [st2k])
        for c in range(2):
            ck = ('F32A', f'cacc0_{c}')
            vop('dve', 'tensor_tensor', (cacc[0][:, c, 0:n], cacc[0][:, c, 0:n], st[:, 0:n], ALU.subtract), [ck, stk], [ck])
            vop('dve', 'tensor_tensor', (cacc[0][:, c, 0:n], cacc[0][:, c, 0:n], st2[:, 0:n], ALU.mult), [ck, st2k], [ck])
            act(sT_v[:, c, off:off + n], cacc[0][:, c, 0:n], AF.Silu, [ck, ('par', 3), ('par', 4)], [('BFA', 'sT')],
                bias=par[:, 36 + c:37 + c], scale=par[:, 34 + c:35 + c])

    def layer_weights(l):
        load(ao_sb[:, :, :], wb_ao[l].rearrange("(h p) n -> p h n", p=128), 'ao_sb', [('wb_ao', l)], ['ao_sb'])
        load(pw_sb[:, :, :], wb_pw[l].rearrange("(c p) n -> p c n", p=128), 'pw_sb', [('wb_pw', l)], ['pw_sb'])
        load_nc(poolw_sb[:, :, :], wb_pool[l].rearrange("(c p) n -> p c n", p=128), 'poolw_sb', [('wb_pool', l)], ['poolw_sb'])

    def mixer_ffn(l, x_src, xm_scr, out_dst, ntok, tok0):
        nblk = (ntok + 127) // 128
        wl = wb_in[l]
        load(xT[:, :, 0:ntok], xT_scr[:, tok0:tok0 + ntok].rearrange("(k p) t -> p k t", p=128), 'xT', ['xT_scr'], ['xT'])
        for grp in range(2):
            gsl = [load_slab(wl, 0, 8, 2304 + b_ * 1024 + grp * 512, 512, ('wb_in', l)) for b_ in range(3)]
            for nn in range(4):
                nch = grp * 4 + nn
                for b_ in range(3):
                    sl, sk = gsl[b_]
                    ps, pk = psf()
                    for kc in range(8):
                        mm(ps[:, 0:ntok], sl[:, kc, nn * 128:(nn + 1) * 128], xT[:, kc, 0:ntok], kc == 0, kc == 7, [sk, 'xT'], [pk])
                    act(gt_v[:, b_, 0:ntok], ps[:, 0:ntok], AF.Sigmoid, [pk, 'par'], [('BFA', f'gt{b_}')], bias=par[:, b_ * 8 + nch:b_ * 8 + nch + 1])
                g = nch // 2
                hs = slice((g % 2) * 64, (g % 2) * 64 + 64)
                pa, pak = psf()
                mm(pa[:, 0:ntok], poolw_sb[hs, g // 2, (nch % 2) * 128:(nch % 2) * 128 + 128], pT_v[hs, g // 2, 0:ntok], True, True,
                   ['poolw_sb', ('BFA', 'pT')], [pak])
                pb, pbk = psf()
                for pr in range(4):
                    mm(pb[:, 0:ntok], ao_sb[:, pr, nch * 128:(nch + 1) * 128], oT_v[:, pr, 0:ntok], pr == 0, pr == 3, ['ao_sb', ('BFA', 'oT')], [pbk])
                pc_, pck = psf()
                for c in range(2):
                    mm(pc_[:, 0:ntok], pw_sb[:, c, nch * 128:(nch + 1) * 128], sT_v[:, c, 0:ntok], c == 0, c == 1, ['pw_sb', ('BFA', 'sT')], [pck])
                t1, t1k = stage32()
                vop('dve', 'scalar_tensor_tensor', (t1[:, 0:ntok], pa[:, 0:ntok], par[:, 24 + nch:25 + nch], gt_v[:, 0, 0:ntok], ALU.mult, ALU.mult),
                    [pak, ('par', 1), ('BFA', 'gt0')], [t1k])
                t2, t2k = stage32()
                vop('dve', 'tensor_tensor', (t2[:, 0:ntok], pb[:, 0:ntok], gt_v[:, 1, 0:ntok], ALU.mult), [pbk, ('BFA', 'gt1')], [t2k])
                vop('pool', 'tensor_tensor', (t1[:, 0:ntok], t1[:, 0:ntok], t2[:, 0:ntok], ALU.add), [t1k, t2k], [t1k])
                t3, t3k = stage32()
                vop('dve', 'tensor_tensor', (t3[:, 0:ntok], pc_[:, 0:ntok], gt_v[:, 2, 0:ntok], ALU.mult), [pck, ('BFA', 'gt2')], [t3k])
                vop('pool', 'tensor_tensor', (mg_v[:, nch, 0:ntok], t1[:, 0:ntok], t3[:, 0:ntok], ALU.add), [t1k, t3k], [('BFA', f'mg{nch}')])
        wsl = [load_slab(wb_out[l], 0, 8, hf * 512, 512, ('wb_out', l)) for hf in range(2)]
        mgk = [('BFA', f'mg{k}') for k in range(8)]
        def stX(b):
            n = min(128, ntok - b * 128)
            xr, xrk = R32[b % 2], f"R32_{b % 2}"
            yr, yrk = R32[2 + (b % 2)], f"R32_{2 + (b % 2)}"
            load(xr[0:n, :], x_src[b * 128:b * 128 + n, :], xrk, (), [xrk])
            for hf in range(2):
                sl, sk = wsl[hf]
                ps, pk = psf()
                for kc in range(8):
                    mm(ps[0:n, :], mg_v[:, kc, b * 128:b * 128 + n], sl[:, kc, :], kc == 0, kc == 7, [sk] + mgk, [pk])
                vop('dve', 'scalar_tensor_tensor', (yr[0:n, hf * 512:(hf + 1) * 512], xr[0:n, hf * 512:(hf + 1) * 512], ALPHA, ps[0:n, :], ALU.mult, ALU.add),
                    [xrk, pk], [yrk])

        def stY(b):
            n = min(128, ntok - b * 128)
            om, omk = R32[b % 2], f"R32_{b % 2}"
            yr, yrk = R32[2 + (b % 2)], f"R32_{2 + (b % 2)}"
            rows_ln(yr, yrk, n, 0, om, omk)
            load(xm_scr[tok0 + b * 128:tok0 + b * 128 + n, :], om[0:n, :], omk, [omk], ['xm_scr'])

        def stZ(b):
            n = min(128, ntok - b * 128)
            rows_to_xT(R32[b % 2], f"R32_{b % 2}", b, n)

        for it in range(nblk + 1):
            if it < nblk:
                stX(it)
            if it >= 1:
                stY(it - 1)
                stZ(it - 1)
        barrier(['BFA'])
        nsl = (DFF + 511) // 512
        for si in range(nsl):
            ncol = min(512, DFF - si * 512)
            gs, gk = load_slab(wb_g[l], 0, 8, si * 512, ncol, ('wb_g', l))
            us, uk = load_slab(wb_u[l], 0, 8, si * 512, ncol, ('wb_u', l))
            for nn in range(ncol // 128):
                ch = si * 4 + nn
                pg, pgk = psf()
                for kc in range(8):
                    mm(pg[:, 0:ntok], gs[:, kc, nn * 128:(nn + 1) * 128], xT[:, kc, 0:ntok], kc == 0, kc == 7, [gk, 'xT'], [pgk])
                pu, puk = psf()
                for kc in range(8):
                    mm(pu[:, 0:ntok], us[:, kc, nn * 128:(nn + 1) * 128], xT[:, kc, 0:ntok], kc == 0, kc == 7, [uk, 'xT'], [puk])
                st, stk = stage32()
                act(st[:, 0:ntok], pg[:, 0:ntok], AF.Silu, [pgk], [stk])
                vop('dve', 'tensor_tensor', (hT_v[:, ch, 0:ntok], pu[:, 0:ntok], st[:, 0:ntok], ALU.mult), [puk, stk], [('BFA', f'h{ch}')])
        hk = [('BFA', f'h{k}') for k in range(22)]
        for hf in range(2):
            for kg in range(3):
                nk = 8 if kg < 2 else 6
                sl, sk = load_slab(wb_d[l], kg * 1024, nk, hf * 512, 512, ('wb_d', l))
                for b in range(nblk):
                    n = min(128, ntok - b * 128)
                    ps, pk = psF_t[:, b, :], ('psF', b)
                    for k in range(nk):
                        kc = kg * 8 + k
                        mm(ps[0:n, :], hT_v[:, kc, b * 128:b * 128 + n], sl[:, k, :], kc == 0, kc == 21, [sk] + hk, [pk])
            for b in range(nblk):
                n = min(128, ntok - b * 128)
                ps, pk = psF_t[:, b, :], ('psF', b)
                st, stk = stage32()
                load(st[0:n, :], xm_scr[tok0 + b * 128:tok0 + b * 128 + n, hf * 512:(hf + 1) * 512], stk, ['xm_scr'], [stk])
                yr, yrk = R32[b], f"R32_{b}"
                vop('dve', 'scalar_tensor_tensor', (yr[0:n, hf * 512:(hf + 1) * 512], st[0:n, :], ALPHA, ps[0:n, :], ALU.mult, ALU.add),
                    [stk, pk], [(yrk, hf)])
        for b in range(nblk):
            n = min(128, ntok - b * 128)
            yr, yrk = R32[b], f"R32_{b}"
            rows_ln(yr, yrk, n, 2, yr, yrk)
            load(out_dst[b * 128:b * 128 + n, :], yr[0:n, :], yrk, [yrk], ['out_dst'])

    def load_ext_prompt(j, tok0):
        for c in range(2):
            if j == 0:
                vop('pool', 'memset', (extu[:, c, 0:30], 0.0), (), [('F32A', 'extu')])
                vop('pool', 'memset', (extg[:, c, 0:30], 0.0), (), [('F32A', 'extg')])
                load(extu[:, c, 30:30 + CH], ug_scr[c * 128:(c + 1) * 128, 0:CH], 'extld', [('ug_scr', 'u')], [('F32A', 'extu')], nowaw=True)
                load(extg[:, c, 30:30 + CH], ug_scr[256 + c * 128:256 + (c + 1) * 128, 0:CH], 'extld', [('ug_scr', 'g')], [('F32A', 'extg')], nowaw=True)
            else:
                load(extu[:, c, 0:30 + CH], ug_scr[c * 128:(c + 1) * 128, tok0 - 30:tok0 + CH], 'extld', [('ug_scr', 'u')], [('F32A', 'extu')], nowaw=True)
                load(extg[:, c, 0:30 + CH], ug_scr[256 + c * 128:256 + (c + 1) * 128, tok0 - 30:tok0 + CH], 'extld', [('ug_scr', 'g')], [('F32A', 'extg')], nowaw=True)

    def load_ext_sample(l, s_):
        vop('pool', 'memset', (extu[:, :, 0:30], 0.0), (), [('F32A', 'extu')])
        for c in range(2):
            load_nc(extu[:, c, 15:30], state_pool[l, s_, :, c * 128:(c + 1) * 128].rearrange("t p -> p t"), 'extld', (), [('F32A', 'extu')], nowaw=True)
            load_nc(extg[:, c, 0:30], state_conv[l, s_, :, c * 128:(c + 1) * 128].rearrange("t p -> p t"), 'extld', (), [('F32A', 'extg')], nowaw=True)
            load_nc(extu[:, c, 30:34], ug_scr[c * 128:(c + 1) * 128, T + 4 * s_:T + 4 * s_ + 4], 'extld', [('ug_scr', 'u')], [('F32A', 'extu')], nowaw=True)
            load_nc(extg[:, c, 30:34], ug_scr[256 + c * 128:256 + (c + 1) * 128, T + 4 * s_:T + 4 * s_ + 4], 'extld', [('ug_scr', 'g')], [('F32A', 'extg')], nowaw=True)

    def sample_prep(l):
        load_nc(ptf_i[:, :], page_table[:, :].rearrange("s j -> (s j)").partition_broadcast(128), 'ptf', (), ['ptf_i'])
        vop('dve', 'tensor_copy', (ptf[:, :], ptf_i[:, :]), ['ptf_i'], ['ptf'])
        vop('dve', 'tensor_copy', (ptf2[0:64, :], ptf[0:64, 0::2]), ['ptf'], [('ptf2', 0)])
        vop('dve', 'tensor_copy', (ptf2[64:128, :], ptf[64:128, 1::2]), ['ptf'], [('ptf2', 1)])
        vop('dve', 'tensor_scalar', (ptf2[:, :], ptf2[:, :], 64.0, float(l * npool * 64), ALU.mult, ALU.add), ['ptf2'], ['ptf2'])
        vop('dve', 'tensor_scalar', (ptf2[:, :], ptf2[:, :], consts_sb[:, 13:14], None, ALU.add), ['ptf2', 'consts_sb'], ['ptf2'])
        vop('dve', 'tensor_copy', (pidx2[:, :], ptf2[:, :]), ['ptf2'], ['pidx2'])
        sk0 = ('small', 0)
        vop('dve', 'tensor_tensor', (small[:, 0:8], consts_sb[:, 0:8], nbias[:, 4, 0, :], ALU.mult), ['consts_sb', 'nbias'], [sk0])
        vop('dve', 'tensor_tensor', (small[:, 0:4], small[:, 0:4], small[:, 4:8], ALU.add), [sk0], [sk0])
        vop('dve', 'tensor_tensor', (small[:, 0:2], small[:, 0:2], small[:, 2:4], ALU.add), [sk0], [sk0])
        vop('dve', 'tensor_tensor', (rbias[:, 0:1], small[:, 0:1], small[:, 1:2], ALU.add), [sk0], ['rbias'])

    def sample_attention(l, g):
        oacc = [R32[2 + s_ // 2][:, (s_ % 2) * 512:(s_ % 2 + 1) * 512] for s_ in range(4)]
        oacck = [(f"R32_{2 + s_ // 2}", s_ % 2) for s_ in range(4)]
        t0s = T + 16 * g
        load_nc(qT[:, :, 0:16], qT_scr[:, t0s:t0s + 16].rearrange("(c p) t -> p c t", p=128), 'qT', ['qT_scr'], ['qT'])
        load_nc(kn[:, :, 0:16], ksT_scr[:, 16 * g:16 * g + 16].rearrange("(c p) t -> p c t", p=128), 'kn', ['ksT_scr'], ['kn'])
        vop('pool', 'memset', (QZ[:, :, :], 0.0), (), ['QZ'])
        for s_ in range(4):
            for pr in range(4):
                for hh in range(2):
                    r0 = s_ * 32 + (2 * pr + hh) * 4
                    vop('dve', 'tensor_copy', (QZ[hh * 64:(hh + 1) * 64, s_ * 4 + pr, r0:r0 + 4], qT[hh * 64:(hh + 1) * 64, pr, 4 * s_:4 * s_ + 4]),
                        ['qT', 'QZ'], [('QZ', s_ * 4 + pr)])
        ps, pk = psF_t[:, 0, 0:4], ('psF', 0)
        n_mm = 0
        for s_ in range(4):
            for pr in range(4):
                mm(ps, QZ[:, s_ * 4 + pr, :], kn[:, pr, 4 * s_:4 * s_ + 4], n_mm == 0, n_mm == 15, [('QZ', s_ * 4 + pr), 'kn'], [pk])
                n_mm += 1
        st, stk = stage32()
        vop('dve', 'tensor_tensor', (st[:, 0:4], ps, consts_sb[:, 8:12], ALU.add), [pk, 'consts_sb'], [stk])
        act(mnew[:, 0:4], st[:, 0:4], AF.Sigmoid, [stk, 'rbias'], ['mnew'], bias=rbias[:, 0:1], scale=-0.125)
        vop('dve', 'memset', (Pnew[:, 4:5], 1.0), (), ['Pnew'])
        vop('dve', 'tensor_tensor_scan', (Pnew[:, 0:4][:, ::-1], mnew[:, 0:4][:, ::-1], mnew[:, 0:4][:, ::-1], 1.0, ALU.mult, ALU.min),
            ['mnew'], ['Pnew'])
        vop('dve', 'tensor_tensor', (anew[:, 0:4], Pnew[:, 1:5], Pnew[:, 0:4], ALU.subtract), ['Pnew'], ['anew'])
        pst, pstk = psb()
        tr(pst[0:4, 0:128], anew[:, 0:4], ident_b[:, :], ['anew', 'ident_b'], [pstk])
        vop('dve', 'tensor_copy', (anT[0:4, :], pst[0:4, 0:128]), [pstk], ['anT'])
        for s_ in range(4):
            ps2, pk2 = psf()
            mm(ps2[0:32, :], anT[0:4, s_ * 32:(s_ + 1) * 32], vnew[0:4, 4 * g + s_, :], True, True, ['anT', ('vnew', 4 * g + s_)], [pk2])
            vop('dve', 'tensor_copy', (oacc[s_][0:32, :], ps2[0:32, :]), [pk2], [oacck[s_]])
        prevP = ('new', None)
        sidx = [0]
        for pc in range(7, -1, -1):
            s2_ = sidx[0] % 2
            sidx[0] += 1
            mk, Pk, ak, atk = ('F32A', f'm{s2_}'), ('F32A', f'P{s2_}'), f'aB{s2_}', f'aTB{s2_}'
            for tl in (1, 0):
                zi = 1 + tl
                psz, pzk = psF_t[:, zi, :], ('psF', zi)
                n_mm = 0
                for s_ in range(4):
                    sg = 4 * g + s_
                    ksl, kk = next_slab()
                    kv = ksl[:, 0:4, :]
                    for m2 in range(2):
                        mglob = pc * 4 + tl * 2 + m2
                        kpg, kpk = pagebuf()
                        col = sg * 32 + mglob
                        S.dma('pool', lambda e, kpg=kpg, col=col: e.indirect_dma_start(
                            out=kpg[:, :], out_offset=None, in_=cache_k[:, :],
                            in_offset=bass.IndirectOffsetOnAxis(ap=pidx2[:, col:col + 1], axis=0)), kpk, ['pidx2'], [kpk])
                        kb, kbk = stage16w()
                        vop('dve', 'tensor_copy', (kb[:, :], kpg[:, :]), [kpk], [kbk])
                        pt_, ptk = psb()
                        for a_ in range(2):
                            for pr in range(4):
                                tr(pt_[:, (a_ * 4 + pr) * 128:(a_ * 4 + pr + 1) * 128], kb[:, a_ * 512 + pr * 128:a_ * 512 + (pr + 1) * 128],
                                   ident_b[:, :], [kbk, 'ident_b'], [ptk])
                        for a_ in range(2):
                            act(kv[:, :, m2 * 256 + a_:(m2 + 1) * 256:2], pt_[:, a_ * 512:(a_ + 1) * 512].rearrange("p (c t) -> p c t", c=4),
                                AF.Copy, [ptk], [(kk, m2 * 2 + a_)])
                    for pr in range(4):
                        mm(psz, QZ[:, s_ * 4 + pr, :], kv[:, pr, :], n_mm == 0, n_mm == 15, [('QZ', s_ * 4 + pr), kk], [pzk])
                        n_mm += 1
                act(mB[s2_][:, tl * 512:(tl + 1) * 512], psz, AF.Sigmoid, [pzk, 'rbias'], [mk], bias=rbias[:, 0:1], scale=-0.125)
            if prevP[0] == 'new':
                src_c, srck = Pnew[:, 0:1], 'Pnew'
            else:
                src_c, srck = PB[prevP[1]][:, 0:1], ('F32A', f'P{prevP[1]}')
            vop('dve', 'tensor_copy', (PB[s2_][:, 1024:1025], src_c), [srck], [Pk])
            vop('dve', 'tensor_tensor_scan', (PB[s2_][:, 0:1024][:, ::-1], mB[s2_][:, ::-1], mB[s2_][:, ::-1], src_c, ALU.mult, ALU.min),
                [mk, srck], [Pk])
            vop('dve', 'tensor_tensor', (aB[s2_][:, :], PB[s2_][:, 1:1025], PB[s2_][:, 0:1024], ALU.subtract), [Pk], [ak])
            pst, pstk = psb()
            for k in range(8):
                mloc, a_ = k // 2, k % 2
                tr(pst[:, k * 128:(k + 1) * 128], aB[s2_][:, mloc * 256 + a_:(mloc + 1) * 256:2], ident_b[:, :], [ak, 'ident_b'], [pstk])
            act(aTB[s2_][:, :], pst, AF.Copy, [pstk], [atk])
            for s_ in range(4):
                sg = 4 * g + s_
                ps2, pk2 = psf()
                for mloc in range(4):
                    mglob = pc * 4 + mloc
                    vpg, vpk = pagebuf()
                    col = sg * 32 + mglob
                    S.dma('pool', lambda e, vpg=vpg, col=col: e.indirect_dma_start(
                        out=vpg[:, :], out_offset=None, in_=cache_v[:, :],
                        in_offset=bass.IndirectOffsetOnAxis(ap=pidx2[:, col:col + 1], axis=0)), vpk, ['pidx2'], [vpk])
                    vb, vbk = stage16w()
                    act(vb[:, :], vpg[:, :], AF.Copy, [vpk], [vbk])
                    for a_ in range(2):
                        k = mloc * 2 + a_
                        mm(ps2[0:32, :], aTB[s2_][:, k * 128 + s_ * 32:k * 128 + (s_ + 1) * 32], vb[:, a_ * 512:(a_ + 1) * 512], k == 0, k == 7, [atk, vbk], [pk2])
                vop('dve', 'tensor_tensor', (oacc[s_][0:32, :], oacc[s_][0:32, :], ps2[0:32, :], ALU.add), [pk2, oacck[s_]], [oacck[s_]])
            prevP = ('past', s2_)
        for s_ in range(4):
            ob, obk = aB[s_ // 2][:, (s_ % 2) * 512:(s_ % 2 + 1) * 512], f'aB{s_ // 2}'
            vop('pool', 'tensor_copy', (ob[0:32, :], oacc[s_][0:32, :]), [oacck[s_]], [obk])
            pst, pstk = psb()
            for pr in range(4):
                tr(pst[:, pr * 32:(pr + 1) * 32], ob[0:32, pr * 128:(pr + 1) * 128], ident_b[0:32, 0:32], [obk, 'ident_b'], [pstk])
            for pr in range(4):
                for hh in range(2):
                    h = 2 * pr + hh
                    vop('dve', 'tensor_copy', (oT_v[hh * 64:(hh + 1) * 64, pr, 16 * g + 4 * s_:16 * g + 4 * s_ + 4],
                                               pst[hh * 64:(hh + 1) * 64, pr * 32 + h * 4:pr * 32 + h * 4 + 4]),
                        [pstk], [('BFA', 'oT')])

    load(pscale_sb[:, :, :], pscale[:, :, :], 'pscale_sb', (), ['pscale_sb'])
    load(consts_sb[:, :], consts_in[:, :], 'consts_sb', (), ['consts_sb'])
    vop('pool', 'memset', (invw[0:64, 0:1], 1.0 / 2), (), [('invw', 0)])
    vop('pool', 'memset', (invw[64:128, 0:1], 1.0 / 4), (), [('invw', 1)])
    vop('pool', 'memset', (invw[0:64, 1:2], 1.0 / 8), (), [('invw', 2)])
    vop('pool', 'memset', (invw[64:128, 1:2], 1.0 / 16), (), [('invw', 3)])
    for l in range(n_layers):
        convert_layer(l)
    for l in range(n_layers):
        last = (l == n_layers - 1)
        x_src = xp if l == 0 else X1
        xs_src = xs if l == 0 else X1s
        o_dst = y_p if last else X1
        os_dst = y_s if last else X1s
        layer_params(l)
        layer_weights(l)
        for j in range(NSLOT):
            p1(l, x_src[j * CH:(j + 1) * CH, :], CH, j * CH, j)
        if with_sample:
            p1(l, xs_src, NS, T, None)
            for s_ in range(NSEQ):
                load(pool_s[l, s_, 0:11, :], state_pool[l, s_, 4:15, :], 'd2d', (), ['pool_s'], nowaw=True)
                load(conv_s[l, s_, 0:26, :], state_conv[l, s_, 4:30, :], 'd2d', (), ['conv_s'], nowaw=True)
        for j in range(NSLOT):
            if stop_after in ('p1', 'p1nox'):
                break
            tok0 = j * CH
            barrier(['F32A', 'BFA'])
            attention(l, j, tok0)
            barrier(['F32A'])
            load_ext_prompt(j, tok0)
            poolconv(CH, 0, first16=(j == 0))
            mixer_ffn(l, x_src[tok0:tok0 + CH, :], XM, o_dst[tok0:tok0 + CH, :], CH, tok0)
        if with_sample and stop_after not in ('p1', 'p1nox'):
            barrier(['F32A', 'BFA'])
            sample_prep(l)
            for g in range(NSEQ // 4):
                sample_attention(l, g)
            barrier(['F32A'])
            for s_ in range(NSEQ):
                load_ext_sample(l, s_)
                poolconv(4, 4 * s_)
                barrier(['F32A'])
            mixer_ffn(l, xs_src, XM, os_dst, NS, T)

    semnames = {}
    for e_ in ENGS:
        semnames[e_] = es.enter_context(nc.semaphore(f"sem_{e_}"))
    for sk in S.dma_cnt:
        semnames[sk] = es.enter_context(nc.semaphore("dsem_" + str(sk[1]).replace(" ", "").replace("'", "").replace("(", "").replace(")", "").replace(",", "_")))
    print("n semaphores", len(semnames), "ops", {e_: len(S.ops[e_]) for e_ in ENGS}, "seq", S.seq)

    import os
    if os.environ.get("KDUMP"):
        with open(os.environ["KDUMP"], "w") as f_:
            for rec in S.log:
                f_.write(repr(rec) + "\n")
    LIMIT = int(os.environ.get("KLIMIT", "0")) or 10 ** 9
    final_cnt = {}
    for e_ in ENGS:
        for waits, fn, inc, seq in S.ops[e_]:
            if seq <= LIMIT and isinstance(inc[0], tuple):
                final_cnt[inc[0]] = final_cnt.get(inc[0], 0) + 16

    def emit(engname, e):
        for waits, fn, inc, seq in S.ops[engname]:
            if seq > LIMIT:
                continue
            for sk, v in waits:
                e.wait_ge(semnames[sk], v)
            ins = fn(e)
            ins.then_inc(semnames[inc[0]], inc[1])
        if engname == 'sp':
            for sk, c in final_cnt.items():
                e.wait_ge(semnames[sk], c)

    with nc.Block() as block:
        @block.tensor
        def _(e):
            emit('pe', e)

        @block.scalar
        def _(e):
            emit('act', e)

        @block.vector
        def _(e):
            emit('dve', e)

        @block.gpsimd
        def _(e):
            emit('pool', e)

        @block.sync
        def _(e):
            emit('sp', e)
    es.close()
    return nc


_CACHE = {}


def _get_prog(key):
    if key not in _CACHE:
        _CACHE[key] = build_program(*key)
    return _CACHE[key]


def kernel(x_prompt, x_sample, cache_k, cache_v, state_pool, state_conv, page_table,
           w_in, b_gate, sb_bias, pool_w, pool_scale, w_att_o, conv_dw, conv_b, conv_ln_g, conv_ln_b,
           conv_pw, w_out, ln1_g, ln1_b, ffn_w_gate, ffn_w_up, ffn_w_down, ln2_g, ln2_b,
           _with_sample=True, _n_layers=DEPTH, _stop_after=None):
    f32 = np.float32
    nc = _get_prog((_with_sample, _n_layers, _stop_after))
    x_prompt = np.asarray(x_prompt, f32)
    shared = dict(
        w_in=np.asarray(w_in, f32), b_gate=np.asarray(b_gate, f32), sb_bias=np.asarray(sb_bias, f32),
        pool_w=np.asarray(pool_w, f32), pool_scale=np.asarray(pool_scale, f32), w_att_o=np.asarray(w_att_o, f32),
        conv_dw=np.asarray(conv_dw, f32), conv_b=np.asarray(conv_b, f32), conv_ln_g=np.asarray(conv_ln_g, f32),
        conv_ln_b=np.asarray(conv_ln_b, f32), conv_pw=np.asarray(conv_pw, f32), w_out=np.asarray(w_out, f32),
        ln1_g=np.asarray(ln1_g, f32), ln1_b=np.asarray(ln1_b, f32), ffn_w_gate=np.asarray(ffn_w_gate, f32),
        ffn_w_up=np.asarray(ffn_w_up, f32), ffn_w_down=np.asarray(ffn_w_down, f32), ln2_g=np.asarray(ln2_g, f32),
        ln2_b=np.asarray(ln2_b, f32))
    if _with_sample:
        shared['cache_k'] = np.asarray(cache_k, f32).reshape(DEPTH * 2560 * 64, 1024)
        shared['cache_v'] = np.asarray(cache_v, f32).reshape(DEPTH * 2560 * 64, 1024)
    consts = np.zeros((128, 16), f32)
    for r in range(128):
        h_, t_ = (r // 4) % 8, r % 4
        consts[r, h_] = 1.0
        for tn in range(4):
            consts[r, 8 + tn] = 0.0 if tn < t_ else -8.0 * MASKV
        consts[r, 12] = r
        consts[r, 13] = r % 64
    ps_ = np.zeros((128, 2, 16), f32)
    for cc in range(2):
        for p in range(128):
            w = (2, 4, 8, 16)[cc * 2 + p // 64]
            for t_ in range(16):
                ps_[p, cc, t_] = 1.0 / min(t_ + 1, w)
    in_maps = []
    for c in range(NCORES):
        m = dict(shared)
        m['xp'] = np.ascontiguousarray(x_prompt[c])
        m['xs'] = np.ascontiguousarray(np.asarray(x_sample, f32)[NSEQ * c:NSEQ * (c + 1)].reshape(NS, D))
        m['pscale'] = ps_
        m['consts_in'] = consts
        if _with_sample:
            m['state_pool'] = np.ascontiguousarray(np.asarray(state_pool, f32)[:, NSEQ * c:NSEQ * (c + 1)])
            m['state_conv'] = np.ascontiguousarray(np.asarray(state_conv, f32)[:, NSEQ * c:NSEQ * (c + 1)])
            m['page_table'] = np.ascontiguousarray(np.asarray(page_table, np.int32)[NSEQ * c:NSEQ * (c + 1)])
        in_maps.append(m)
    res = run_bass_kernel_spmd(nc, in_maps, core_ids=list(range(NCORES)))
    R = res.results
    y_prompt = np.zeros((4, 4096, D), f32)
    k_prompt = np.zeros((DEPTH, 4, 4096, 8, 64), f32)
    v_prompt = np.zeros((DEPTH, 4, 4096, 8, 64), f32)
    pool_prompt = np.zeros((DEPTH, 4, 15, 256), f32)
    conv_prompt = np.zeros((DEPTH, 4, 30, 256), f32)
    y_sample = np.zeros((32, 4, D), f32)
    k_sample = np.zeros((DEPTH, 32, 4, 8, 64), f32)
    v_sample = np.zeros((DEPTH, 32, 4, 8, 64), f32)
    pool_sample = np.zeros((DEPTH, 32, 15, 256), f32)
    conv_sample = np.zeros((DEPTH, 32, 30, 256), f32)
    for c in range(NCORES):
        o = R[c]
        sl = slice(NSEQ * c, NSEQ * (c + 1))
        y_prompt[c] = o['y_p']
        k_prompt[:, c] = o['k_p'].reshape(DEPTH, T, 8, 64)
        v_prompt[:, c] = o['v_p'].reshape(DEPTH, T, 8, 64)
        pool_prompt[:, c] = o['pool_p']
        conv_prompt[:, c] = o['conv_p']
        y_sample[sl] = o['y_s'].reshape(NSEQ, 4, D)
        k_sample[:, sl] = o['k_s'].reshape(DEPTH, NSEQ, 4, 8, 64)
        v_sample[:, sl] = o['v_s'].reshape(DEPTH, NSEQ, 4, 8, 64)
        pool_sample[:, sl] = o['pool_s']
        conv_sample[:, sl] = o['conv_s']
    return (y_prompt, y_sample, k_prompt, v_prompt, pool_prompt, conv_prompt,
            k_sample, v_sample, pool_sample, conv_sample)
```

```python
import numpy as np
import concourse.bass as bass
import concourse.mybir as mybir
from concourse.bass_utils import run_bass_kernel_spmd

F32 = mybir.dt.float32
BF16 = mybir.dt.bfloat16
I32 = mybir.dt.int32
ALU = mybir.AluOpType
AF = mybir.ActivationFunctionType
AX = mybir.AxisListType

D = 1024
NIN = 5376
DFF = 2816
T = 4096
CH = 512
NSLOT = 8
NCORES = 4
NSEQ = 32 // NCORES
NS = NSEQ * 4
NPAGE = 64
DEPTH = 2
ALPHA = (2.0 * DEPTH) ** 0.25
EPS = 1e-5
CHUNKS = ([0, 3, 4, 7], [1, 2, 5, 6])
MASKV = 60.0
ENGS = ['pe', 'act', 'dve', 'pool', 'sp']


def chunk_loc(c):
    for r in range(2):
        if c in CHUNKS[r]:
            return r, CHUNKS[r].index(c)
    raise ValueError(c)


class Sched:
    def __init__(self):
        self.ops = {e: [] for e in ENGS}
        self.cnt = {e: 0 for e in ENGS}
        self.waited = {e: {} for e in ENGS}
        self.st = {}
        self.dma_cnt = {}
        self.seq = 0
        self.log = []

    def _split(self, k):
        return k if isinstance(k, tuple) else (k, None)

    def _base(self, b):
        return self.st.setdefault(b, {'w': None, 'r': {}, 'subs': {}})

    def _deps(self, reads, writes):
        deps = []

        def addev(e):
            if e is not None:
                deps.append(e)

        for k in reads:
            b, sub = self._split(k)
            B = self._base(b)
            addev(B['w'])
            if sub is None:
                for S in B['subs'].values():
                    addev(S['w'])
            else:
                S = B['subs'].setdefault(sub, {'w': None, 'r': {}})
                addev(S['w'])
        for k in writes:
            b, sub = self._split(k)
            B = self._base(b)
            addev(B['w'])
            deps.extend(B['r'].items())
            if sub is None:
                for S in B['subs'].values():
                    addev(S['w'])
                    deps.extend(S['r'].items())
            else:
                S = B['subs'].setdefault(sub, {'w': None, 'r': {}})
                addev(S['w'])
                deps.extend(S['r'].items())
        return deps

    def _update(self, reads, writes, ev):
        sk, v = ev
        for k in reads:
            b, sub = self._split(k)
            B = self._base(b)
            R = B['r'] if sub is None else B['subs'].setdefault(sub, {'w': None, 'r': {}})['r']
            R[sk] = max(R.get(sk, 0), v)
        for k in writes:
            b, sub = self._split(k)
            B = self._base(b)
            if sub is None:
                B['w'] = ev
                B['r'] = {}
                B['subs'] = {}
            else:
                S = B['subs'].setdefault(sub, {'w': None, 'r': {}})
                S['w'] = ev
                S['r'] = {}

    def _waits(self, eng, deps):
        need = {}
        for sk, v in deps:
            if sk == 'pe' and eng == 'pe':
                continue
            if isinstance(sk, tuple):
                v = max(v, 16 * self.dma_cnt.get(sk, 0))
            need[sk] = max(need.get(sk, 0), v)
        out = []
        W = self.waited[eng]
        for sk, v in need.items():
            if W.get(sk, 0) < v:
                W[sk] = v
                out.append((sk, v))
        return out

    def op(self, eng, fn, reads=(), writes=()):
        deps = self._deps(reads, writes)
        waits = self._waits(eng, deps)
        self.cnt[eng] += 1
        ev = (eng, self.cnt[eng])
        self.seq += 1
        self.log.append((self.seq, eng, 'op', list(reads), list(writes)))
        self.ops[eng].append((waits, fn, (eng, 1), self.seq))
        self._update(reads, writes, ev)

    def dma(self, q, fn, semkey, reads=(), writes=(), nowaw=False):
        deps = self._deps(reads, writes)
        if nowaw:
            deps = [d for d in deps if d[0] != ('d', semkey)]
        waits = self._waits(q, deps)
        sk = ('d', semkey)
        self.dma_cnt[sk] = self.dma_cnt.get(sk, 0) + 1
        ev = (sk, 16 * self.dma_cnt[sk])
        self.seq += 1
        self.log.append((self.seq, q, 'dma:' + str(semkey), list(reads), list(writes)))
        self.ops[q].append((waits, fn, (sk, 16), self.seq))
        self._update(reads, writes, ev)


def build_program(with_sample=True, n_layers=DEPTH, stop_after=None, npool=2560):
    nc = bass.Bass("TRN2", target_bir_lowering=False)
    S = Sched()

    def din(name, shape, dt=F32):
        return nc.dram_tensor(name, list(shape), dt, kind="ExternalInput").ap()

    def dout(name, shape, dt=F32):
        return nc.dram_tensor(name, list(shape), dt, kind="ExternalOutput").ap()

    def dscr(name, shape, dt):
        return nc.dram_tensor(name, list(shape), dt, kind="Internal").ap()

    xp = din("xp", [T, D])
    xs = din("xs", [NS, D])
    pscale = din("pscale", [128, 2, 16])
    consts_in = din("consts_in", [128, 16])
    w_in = din("w_in", [DEPTH, D, NIN])
    b_gate = din("b_gate", [DEPTH, 3 * D])
    sb_bias = din("sb_bias", [DEPTH, 8])
    pool_w = din("pool_w", [DEPTH, 4, 64, 256])
    pool_scale = din("pool_scale", [DEPTH, D])
    w_att_o = din("w_att_o", [DEPTH, 512, D])
    conv_dw = din("conv_dw", [DEPTH, 31, 256])
    conv_b = din("conv_b", [DEPTH, 256])
    conv_ln_g = din("conv_ln_g", [DEPTH, 256])
    conv_ln_b = din("conv_ln_b", [DEPTH, 256])
    conv_pw = din("conv_pw", [DEPTH, 256, D])
    w_out = din("w_out", [DEPTH, D, D])
    ln1_g = din("ln1_g", [DEPTH, D])
    ln1_b = din("ln1_b", [DEPTH, D])
    ffn_g = din("ffn_w_gate", [DEPTH, D, DFF])
    ffn_u = din("ffn_w_up", [DEPTH, D, DFF])
    ffn_d = din("ffn_w_down", [DEPTH, DFF, D])
    ln2_g = din("ln2_g", [DEPTH, D])
    ln2_b = din("ln2_b", [DEPTH, D])
    if with_sample:
        cache_k = din("cache_k", [DEPTH * npool * 64, 1024])
        cache_v = din("cache_v", [DEPTH * npool * 64, 1024])
        state_pool = din("state_pool", [DEPTH, NSEQ, 15, 256])
        state_conv = din("state_conv", [DEPTH, NSEQ, 30, 256])
        page_table = din("page_table", [NSEQ, NPAGE], I32)

    y_p = dout("y_p", [T, D])
    k_p = dout("k_p", [DEPTH, T, 512])
    v_p = dout("v_p", [DEPTH, T, 512])
    pool_p = dout("pool_p", [DEPTH, 15, 256])
    conv_p = dout("conv_p", [DEPTH, 30, 256])
    y_s = dout("y_s", [NS, D])
    k_s = dout("k_s", [DEPTH, NS, 512])
    v_s = dout("v_s", [DEPTH, NS, 512])
    pool_s = dout("pool_s", [DEPTH, NSEQ, 15, 256])
    conv_s = dout("conv_s", [DEPTH, NSEQ, 30, 256])

    wb_in = dscr("wb_in", [DEPTH, D, NIN], BF16)
    wb_ao = dscr("wb_ao", [DEPTH, 512, D], BF16)
    wb_pw = dscr("wb_pw", [DEPTH, 256, D], BF16)
    wb_out = dscr("wb_out", [DEPTH, D, D], BF16)
    wb_g = dscr("wb_g", [DEPTH, D, DFF], BF16)
    wb_u = dscr("wb_u", [DEPTH, D, DFF], BF16)
    wb_d = dscr("wb_d", [DEPTH, DFF, D], BF16)
    wb_pool = dscr("wb_pool", [DEPTH, 256, 256], BF16)
    TT = T + NS
    xT_scr = dscr("xT_scr", [D, TT], BF16)
    qT_scr = dscr("qT_scr", [512, TT], BF16)
    KT_loc = dscr("KT_loc", [512, T], BF16)
    V_loc = dscr("V_loc", [T, 512], BF16)
    ug_scr = dscr("ug_scr", [512, TT], F32)
    X1 = dscr("X1", [T, D], F32)
    X1s = dscr("X1s", [NS, D], F32)
    ksT_scr = dscr("ksT_scr", [512, NS], BF16)
    XM = dscr("XM", [TT, D], F32)

    import contextlib
    es = contextlib.ExitStack()

    def sb(name, shape, dt):
        return es.enter_context(nc.sbuf_tensor(name, list(shape), dt))

    ident_b = sb("ident_b", [128, 128], BF16)
    ident_f = sb("ident_f", [128, 128], F32)
    onesm = sb("onesm", [128, 128], F32)
    diagMB = sb("diagMB", [128, 4, 512], BF16)
    par = sb("par", [128, 64], F32)
    nbias = sb("nbias", [128, 5, 2, 8], F32)
    lnt = sb("lnt", [128, 4, D], F32)
    cdw = sb("cdw", [128, 2, 31], F32)
    slabs = [sb(f"slab{i}", [128, 8, 512], BF16) for i in range(4)]
    R32 = [sb(f"R32_{i}", [128, D], F32) for i in range(4)]
    BF8 = sb("BF8", [128, 4, D], BF16)
    xT = sb("xT", [128, 8, CH], BF16)
    BFA = sb("BFA", [128, 22 * CH], BF16)
    ao_sb = sb("ao_sb", [128, 4, D], BF16)
    pw_sb = sb("pw_sb", [128, 2, D], BF16)
    poolw_sb = sb("poolw_sb", [128, 2, 256], BF16)
    pscale_sb = sb("pscale_sb", [128, 2, 16], F32)
    stats = sb("stats", [128, 2, 6], F32)
    mv = sb("mv", [128, 4], F32)
    F32A = sb("F32A", [128, 5440], F32)
    vnew = sb("vnew", [128, NSEQ, 512], BF16)
    invw = sb("invw", [128, 2], F32)
    consts_sb = sb("consts_sb", [128, 16], F32)
    kn = sb("kn", [128, 4, NS], BF16)
    QZ = sb("QZ", [128, 16, 128], BF16)
    ptf_i = sb("ptf_i", [128, NSEQ * NPAGE], I32)
    ptf = sb("ptf", [128, NSEQ * NPAGE], F32)
    pidx = ptf_i
    mnew = sb("mnew", [128, 8], F32)
    Pnew = sb("Pnew", [128, 8], F32)
    anew = sb("anew", [128, 8], BF16)
    anT = sb("anT", [128, 128], BF16)
    rbias = sb("rbias", [128, 8], F32)
    pgs = [sb(f"pg{i}", [128, 1024], F32) for i in range(4)]
    STw = [sb(f"STw{i}", [128, 1024], BF16) for i in range(2)]
    ptf2 = sb("ptf2", [128, NSEQ * 32], F32)
    pidx2 = sb("pidx2", [128, NSEQ * 32], I32)
    aB = [sb(f"aB{i}", [128, 1024], BF16) for i in range(2)]
    aTB = [sb(f"aTB{i}", [128, 1024], BF16) for i in range(2)]
    qT = sb("qT", [128, 4, CH], BF16)
    ST = [sb(f"ST{i}", [128, 512], F32) for i in range(3)]
    STb = [sb(f"STb{i}", [128, 512], BF16) for i in range(3)]
    small = sb("small", [128, 64], F32)
    psF_t = es.enter_context(nc.psum_tensor("psF", [128, 6, 512], F32))
    psB_t = es.enter_context(nc.psum_tensor("psB", [128, 2, 1024], BF16))

    oT_v = BFA[:, 0:4 * CH].rearrange("p (h t) -> p h t", h=4)
    mg_v = BFA[:, 4 * CH:12 * CH].rearrange("p (c t) -> p c t", c=8)
    gt_v = BFA[:, 12 * CH:15 * CH].rearrange("p (c t) -> p c t", c=3)
    pT_v = BFA[:, 15 * CH:17 * CH].rearrange("p (c t) -> p c t", c=2)
    sT_v = BFA[:, 17 * CH:19 * CH].rearrange("p (c t) -> p c t", c=2)
    hT_v = BFA[:, 0:22 * CH].rearrange("p (c t) -> p c t", c=22)
    mB = [F32A[:, i * 1024:(i + 1) * 1024] for i in range(2)]
    PB = [F32A[:, 2048 + i * 1025:2048 + (i + 1) * 1025] for i in range(2)]
    EXT = 30 + CH
    XL = 544
    extu = F32A[:, 0:2 * EXT].rearrange("p (c t) -> p c t", c=2)
    extg = F32A[:, 2 * EXT:4 * EXT].rearrange("p (c t) -> p c t", c=2)
    X0 = 4 * EXT
    cw = [F32A[:, X0 + i * 2 * XL:X0 + (i + 1) * 2 * XL].rearrange("p (c t) -> p c t", c=2) for i in range(2)]
    ptmp = F32A[:, X0 + 4 * XL:X0 + 5 * XL]
    ptmp2 = F32A[:, X0 + 5 * XL:X0 + 6 * XL]
    cacc = [F32A[:, X0 + i * 1024:X0 + (i + 1) * 1024].rearrange("p (c t) -> p c t", c=2) for i in range(2)]
    pg_i = [0]

    def pagebuf():
        i = pg_i[0] % 4
        pg_i[0] += 1
        return pgs[i], f"pg{i}"

    sems = {}

    psf_i = [0]

    def psf():
        i = psf_i[0] % 6
        psf_i[0] += 1
        return psF_t[:, i, :], ('psF', i)

    psb_i = [0]

    def psb():
        i = psb_i[0] % 2
        psb_i[0] += 1
        return psB_t[:, i, :], ('psB', i)

    slab_i = [0]

    def next_slab():
        i = slab_i[0] % 4
        slab_i[0] += 1
        return slabs[i], f"slab{i}"

    def mm(out, lhsT, rhs, start, stop, reads, writes):
        S.op('pe', lambda e: e.matmul(out, lhsT, rhs, start=start, stop=stop), reads, writes)

    def tr(out, in_, ident, reads, writes):
        S.op('pe', lambda e: e.transpose(out, in_, ident), reads, writes)

    def act(out, in_, func, reads, writes, bias=None, scale=None):
        kw = {}
        if bias is not None:
            kw['bias'] = bias
        if scale is not None:
            kw['scale'] = scale
        S.op('act', lambda e: e.activation(out, in_, func, **kw), reads, writes)

    def vop(eng, name, args, reads, writes, **kw):
        S.op(eng, lambda e: getattr(e, name)(*args, **kw), reads, writes)

    def load(out, in_, semkey, reads=(), writes=(), q='sp', nowaw=False):
        S.dma(q, lambda e: e.dma_start(out=out, in_=in_), semkey, reads, writes, nowaw=nowaw)

    def load_nc(out, in_, semkey, reads=(), writes=(), q='sp', nowaw=False):
        S.dma(q, lambda e: e.dma_start(out=out, in_=in_, allow_slow_non_contiguous=True), semkey, reads, writes, nowaw=nowaw)

    S.op('pool', lambda e: e.memset(ident_f[:], 0.0), (), ['ident_f'])
    S.op('pool', lambda e: e.memset(onesm[:], 1.0), (), ['onesm'])
    S.op('pool', lambda e: e.affine_select(ident_f[:], onesm[:], [[-1, 128]], ALU.is_equal, 0.0, base=0, channel_multiplier=1),
         ['onesm'], ['ident_f'])
    vop('dve', 'tensor_copy', (ident_b[:], ident_f[:]), ['ident_f'], ['ident_b'])
    S.op('pool', lambda e: e.memset(ST[0][:], -8.0 * MASKV), (), ['ST0'])
    for i in range(4):
        S.op('pool', lambda e, i=i: e.affine_select(ST[1][:], ST[0][:], [[1, 512]], ALU.is_ge, 0.0, base=-128 * i, channel_multiplier=-1),
             ['ST0'], ['ST1'])
        vop('dve', 'tensor_copy', (diagMB[:, i, :], ST[1][:]), ['ST1'], [('diagMB', i)])
    vop('dve', 'tensor_scalar', (onesm[:], onesm[:], 1.0 / 256.0, None, ALU.mult), ['onesm'], ['onesm'])
    def conv_w(dst, src, rows, l, key):
        step = 128
        for r0 in range(0, rows, step):
            S.dma('pool', lambda e, r0=r0: e.dma_start(out=dst[l, r0:r0 + step, :], in_=src[l, r0:r0 + step, :]),
                  key, (), [(key, l)])

    def convert_layer(l):
        conv_w(wb_in, w_in, D, l, 'wb_in')
        conv_w(wb_ao, w_att_o, 512, l, 'wb_ao')
        conv_w(wb_pw, conv_pw, 256, l, 'wb_pw')
        pw2 = pool_w.rearrange("l g c n -> l (g c) n")
        conv_w(wb_pool, pw2, 256, l, 'wb_pool')
        conv_w(wb_out, w_out, D, l, 'wb_out')
        conv_w(wb_g, ffn_g, D, l, 'wb_g')
        conv_w(wb_u, ffn_u, D, l, 'wb_u')
        conv_w(wb_d, ffn_d, DFF, l, 'wb_d')

    def load_slab(src2d, r0, nk, c0, ncols, wkey):
        sl, sk = next_slab()
        v = src2d[r0:r0 + nk * 128, c0:c0 + ncols].rearrange("(k p) n -> p k n", p=128)
        load(sl[:, 0:nk, 0:ncols], v, sk, [wkey], [sk])
        return sl, sk

    def layer_params(l):
        load_nc(par[:, 0:24], b_gate[l].rearrange("(c p) -> p c", p=128), 'par', (), ['par'])
        load_nc(par[:, 24:32], pool_scale[l].rearrange("(c p) -> p c", p=128), 'par', (), [('par', 1)], nowaw=True)
        load_nc(par[:, 32:34], conv_b[l].rearrange("(c p) -> p c", p=128), 'par', (), [('par', 2)], nowaw=True)
        load_nc(par[:, 34:36], conv_ln_g[l].rearrange("(c p) -> p c", p=128), 'par', (), [('par', 3)], nowaw=True)
        load_nc(par[:, 36:38], conv_ln_b[l].rearrange("(c p) -> p c", p=128), 'par', (), [('par', 4)], nowaw=True)
        load_nc(par[:, 40:48], sb_bias[l:l + 1, :].partition_broadcast(128), 'par', (), [('par', 5)], nowaw=True)
        for c in range(2):
            load_nc(cdw[:, c, :], conv_dw[l][:, c * 128:(c + 1) * 128].rearrange("k p -> p k"), 'cdw', (), ['cdw'], nowaw=(c > 0))
        for i, t in enumerate((ln1_g, ln1_b, ln2_g, ln2_b)):
            load_nc(lnt[:, i, :], t[l:l + 1, :].partition_broadcast(128), 'lnt', (), [('lnt', i)], nowaw=True)
        vop('dve', 'tensor_scalar', (nbias[:, 4, 0, :], par[:, 40:48], -1.0, None, ALU.mult), [('par', 5)], ['nbias'])

    def build_xT(src_rows, ntok, tok0):
        nblk = (ntok + 127) // 128
        for b in range(nblk):
            n = min(128, ntok - b * 128)
            r = R32[b % 4]
            rk = f"R32_{b % 4}"
            load(r[0:n, :], src_rows[b * 128:b * 128 + n, :], rk, (), [rk])
            vop('dve' if b % 2 == 0 else 'pool', 'tensor_copy', (BF8[0:n, b, :], r[0:n, :]), [rk], [('BF8', b)])
            for g in range(2):
                ps, pk = psb()
                for c in range(4):
                    kc = g * 4 + c
                    tr(ps[:, c * 128:c * 128 + n], BF8[0:n, b, kc * 128:(kc + 1) * 128], ident_b[0:n, 0:n],
                       [('BF8', b), 'ident_b'], [pk])
                outv = xT[:, g * 4:(g + 1) * 4, b * 128:b * 128 + n]
                inv = ps[:, 0:512].rearrange("p (c t) -> p c t", c=4)[:, :, 0:n]
                if g == 0:
                    act(outv, inv, AF.Copy, [pk], [('xT', b)])
                else:
                    vop('dve', 'tensor_copy', (outv, inv), [pk], [('xT', b)])
        load(xT_scr[:, tok0:tok0 + ntok].rearrange("(k p) t -> p k t", p=128), xT[:, :, 0:ntok], 'xT', ['xT'], ['xT_scr'])

    RG = [[0, 1], [2, 3], [4, 5], [6, 7]]
    cnt_bar = [0]

    def barrier(keys, eng='dve'):
        c = 56 + (cnt_bar[0] % 8)
        cnt_bar[0] += 1
        vop(eng, 'memset', (small[:, c:c + 1], 0.0), (), list(keys) + [('small', c)])

    st_i = [0]

    def stage32():
        i = st_i[0] % 3
        st_i[0] += 1
        return ST[i], f"ST{i}"

    stb_i = [0]

    def stage16():
        i = stb_i[0] % 3
        stb_i[0] += 1
        return STb[i], f"STb{i}"

    stw_i = [0]

    def stage16w():
        i = stw_i[0] % 2
        stw_i[0] += 1
        return STw[i], f"STw{i}"

    def fm_group(sl, sk, ncols_chunks, ntok, nk=8):
        outs = []
        for nn in ncols_chunks:
            ps, pk = psf()
            for kc in range(nk):
                mm(ps[:, 0:ntok], sl[:, kc, nn * 128:(nn + 1) * 128], xT[:, kc, 0:ntok], kc == 0, kc == nk - 1,
                   [sk, 'xT', 'ident_b'], [pk])
            outs.append((ps, pk))
        return outs

    def rows_ln(y, yk, n, gi, outbuf, outk):
        for hf in range(2):
            vop('dve', 'bn_stats', (stats[0:n, hf, :], y[0:n, hf * 512:(hf + 1) * 512]), [yk], [('stats', hf)])
        vop('dve', 'bn_aggr', (mv[0:n, 0:2], stats[0:n, :, :].rearrange("p a b -> p (a b)")), ['stats'], [('mv', 0)])
        vop('dve', 'tensor_scalar', (mv[0:n, 2:3], mv[0:n, 1:2], EPS, None, ALU.add), [('mv', 0)], [('mv', 1)])
        act(mv[0:n, 2:3], mv[0:n, 2:3], AF.Sqrt, [('mv', 1)], [('mv', 1)])
        vop('dve', 'reciprocal', (mv[0:n, 2:3], mv[0:n, 2:3]), [('mv', 1)], [('mv', 1)])
        vop('dve', 'tensor_scalar', (y[0:n, :], y[0:n, :], mv[0:n, 0:1], mv[0:n, 2:3], ALU.subtract, ALU.mult),
            [yk, ('mv', 0), ('mv', 1)], [yk])
        vop('dve', 'tensor_tensor', (y[0:n, :], y[0:n, :], lnt[0:n, gi, :], ALU.mult), [yk, ('lnt', gi)], [yk])
        vop('pool', 'tensor_tensor', (outbuf[0:n, :], y[0:n, :], lnt[0:n, gi + 1, :], ALU.add), [yk, ('lnt', gi + 1)], [outk])

    def rows_to_xT(r, rk, b, n):
        vop('pool', 'tensor_copy', (BF8[0:n, b, :], r[0:n, :]), [rk], [('BF8', b)])
        for g in range(2):
            ps, pk = psb()
            for c in range(4):
                kc = g * 4 + c
                tr(ps[:, c * 128:c * 128 + n], BF8[0:n, b, kc * 128:(kc + 1) * 128], ident_b[0:n, 0:n],
                   [('BF8', b), 'ident_b'], [pk])
            outv = xT[:, g * 4:(g + 1) * 4, b * 128:b * 128 + n]
            inv = ps[:, 0:512].rearrange("p (c t) -> p c t", c=4)[:, :, 0:n]
            if g == 0:
                act(outv, inv, AF.Copy, [pk], [('xT', b)])
            else:
                vop('dve', 'tensor_copy', (outv, inv), [pk], [('xT', b)])

    def p1(l, x_src, ntok, tok0, slot):
        sample = slot is None
        nblk = (ntok + 127) // 128
        for b in range(nblk):
            n = min(128, ntok - b * 128)
            r, rk = R32[b % 4], f"R32_{b % 4}"
            load(r[0:n, :], x_src[b * 128:b * 128 + n, :], rk, (), [rk])
            rows_to_xT(r, rk, b, n)
        load(xT_scr[:, tok0:tok0 + ntok].rearrange("(k p) t -> p k t", p=128), xT[:, :, 0:ntok], 'xT', ['xT'], ['xT_scr'])
        wl = wb_in[l]
        sl, sk = load_slab(wl, 0, 8, 0, 512, ('wb_in', l))
        outs = fm_group(sl, sk, range(4), ntok)
        for c in range(2):
            ps, pk = outs[c]
            st, stk = stage32()
            act(st[:, 0:ntok], ps[:, 0:ntok], AF.Copy, [pk], [stk])
            load(ug_scr[c * 128:(c + 1) * 128, tok0:tok0 + ntok], st[:, 0:ntok], stk, [stk], [('ug_scr', 'u')], nowaw=True)
        for c in range(2):
            ps, pk = outs[2 + c]
            st, stk = stage16()
            vop('dve', 'tensor_copy', (st[:, 0:ntok], ps[:, 0:ntok]), [pk], [stk])
            load(qT_scr[c * 128:(c + 1) * 128, tok0:tok0 + ntok], st[:, 0:ntok], stk, [stk], ['qT_scr'])
        last_b = nblk - 1
        nl = min(128, ntok - last_b * 128)
        if sample or slot == NSLOT - 1:
            ps, pk = psf()
            for kc in range(8):
                mm(ps[0:nl, 0:256], xT[:, kc, last_b * 128:last_b * 128 + nl], sl[:, kc, 0:256], kc == 0, kc == 7, [sk, 'xT'], [pk])
            st, stk = stage32()
            act(st[0:nl, 0:256], ps[0:nl, 0:256], AF.Copy, [pk], [stk])
            if sample:
                for s_ in range(NSEQ):
                    load(pool_s[l, s_, 11:15, :], st[4 * s_:4 * s_ + 4, 0:256], stk, [stk], ['pool_s'], nowaw=True)
            else:
                load(pool_p[l, :, :], st[128 - 15:128, 0:256], stk, [stk], ['pool_p'])
        sl, sk = load_slab(wl, 0, 8, 512, 256, ('wb_in', l))
        outs = fm_group(sl, sk, range(2), ntok)
        for c in range(2):
            ps, pk = outs[c]
            st, stk = stage16()
            vop('dve', 'tensor_copy', (st[:, 0:ntok], ps[:, 0:ntok]), [pk], [stk])
            load(qT_scr[(2 + c) * 128:(3 + c) * 128, tok0:tok0 + ntok], st[:, 0:ntok], stk, [stk], ['qT_scr'])
        sl, sk = load_slab(wl, 0, 8, 768, 512, ('wb_in', l))
        outs = fm_group(sl, sk, range(4), ntok)
        for c in range(4):
            ps, pk = outs[c]
            st, stk = stage16()
            act(st[:, 0:ntok], ps[:, 0:ntok], AF.Copy, [pk], [stk])
            if sample:
                load_nc(ksT_scr[c * 128:(c + 1) * 128, :], st[:, 0:ntok], stk, [stk], ['ksT_scr'])
            else:
                load(KT_loc[c * 128:(c + 1) * 128, tok0:tok0 + ntok], st[:, 0:ntok], stk, [stk], ['KT_loc'])
        kout = k_s if sample else k_p
        vout = v_s if sample else v_p
        for b in range(nblk):
            n = min(128, ntok - b * 128)
            ps, pk = psf()
            for kc in range(8):
                mm(ps[0:n, :], xT[:, kc, b * 128:b * 128 + n], sl[:, kc, :], kc == 0, kc == 7, [sk, 'xT'], [pk])
            st, stk = stage32()
            act(st[0:n, :], ps[0:n, :], AF.Copy, [pk], [stk])
            load(kout[l, tok0 - (T if sample else 0) + b * 128:tok0 - (T if sample else 0) + b * 128 + n, :], st[0:n, :], stk, [stk], ['kout'])
        sl, sk = load_slab(wl, 0, 8, 1280, 512, ('wb_in', l))
        for b in range(nblk):
            n = min(128, ntok - b * 128)
            ps, pk = psf()
            for kc in range(8):
                mm(ps[0:n, :], xT[:, kc, b * 128:b * 128 + n], sl[:, kc, :], kc == 0, kc == 7, [sk, 'xT'], [pk])
            st, stk = stage32()
            act(st[0:n, :], ps[0:n, :], AF.Copy, [pk], [stk])
            t0_ = tok0 - (T if sample else 0) + b * 128
            load(vout[l, t0_:t0_ + n, :], st[0:n, :], stk, [stk], ['vout'])
            if not sample:
                sb_, sbk = stage16()
                vop('dve', 'tensor_copy', (sb_[0:n, :], st[0:n, :]), [stk], [sbk])
                load(V_loc[tok0 + b * 128:tok0 + b * 128 + n, :], sb_[0:n, :], sbk, [sbk], ['V_loc'])
        if sample:
            for s_ in range(NSEQ):
                ps, pk = psf()
                for kc in range(8):
                    mm(ps[0:4, :], xT[:, kc, 4 * s_:4 * s_ + 4], sl[:, kc, :], kc == 0, kc == 7, [sk, 'xT'], [pk])
                vop('dve', 'tensor_copy', (vnew[0:4, s_, :], ps[0:4, :]), [pk], [('vnew', s_)])
        sl, sk = load_slab(wl, 0, 8, 1792, 512, ('wb_in', l))
        outs = fm_group(sl, sk, range(4), ntok)
        for c in range(2):
            pa, pak = outs[c]
            pg, pgk = outs[2 + c]
            st, stk = stage32()
            act(st[:, 0:ntok], pg[:, 0:ntok], AF.Sigmoid, [pgk], [stk])
            st2, st2k = stage32()
            vop('dve', 'tensor_tensor', (st2[:, 0:ntok], pa[:, 0:ntok], st[:, 0:ntok], ALU.mult), [pak, stk], [st2k])
            load(ug_scr[256 + c * 128:256 + (c + 1) * 128, tok0:tok0 + ntok], st2[:, 0:ntok], st2k, [st2k], [('ug_scr', 'g')], nowaw=True)
        if sample or slot == NSLOT - 1:
            ps, pk = psf()
            for kc in range(8):
                mm(ps[0:nl, :], xT[:, kc, last_b * 128:last_b * 128 + nl], sl[:, kc, :], kc == 0, kc == 7, [sk, 'xT'], [pk])
            st, stk = stage32()
            act(st[0:nl, 0:256], ps[0:nl, 256:512], AF.Sigmoid, [pk], [stk])
            st2, st2k = stage32()
            vop('dve', 'tensor_tensor', (st2[0:nl, 0:256], ps[0:nl, 0:256], st[0:nl, 0:256], ALU.mult), [pk, stk], [st2k])
            if sample:
                for s_ in range(NSEQ):
                    load(conv_s[l, s_, 26:30, :], st2[4 * s_:4 * s_ + 4, 0:256], st2k, [st2k], ['conv_s'], nowaw=True)
            else:
                load(conv_p[l, :, :], st2[128 - 30:128, 0:256], st2k, [st2k], ['conv_p'])

    def attention(l, j, tok0):
        nchunk = j + 1
        load(qT[:, :, :], qT_scr[:, tok0:tok0 + CH].rearrange("(c p) t -> p c t", p=128), 'qT', ['qT_scr'], ['qT'])
        npiece = (nchunk + 1) // 2
        P = []
        for pr in range(4):
            for hh in range(2):
                for i in range(4):
                    for pc in range(npiece - 1, -1, -1):
                        P.append(dict(pr=pr, hh=hh, i=i, pc=pc, first=(pc == npiece - 1), last=(pc == 0), chain=(pr * 2 + hh) * 4 + i))
        slabs_pr = {}

        def get_slabs(pr):
            if pr not in slabs_pr:
                ksl, kk = next_slab()
                vsl, vk = next_slab()
                vview = vsl[:, :, :].rearrange("p a b -> p (a b)").rearrange("p (k c) -> p k c", c=128)
                load(ksl[:, 0:nchunk, :], KT_loc[pr * 128:(pr + 1) * 128, 0:nchunk * CH].rearrange("p (c t) -> p c t", t=CH), kk, ['KT_loc'], [kk])
                for c in range(nchunk):
                    load_nc(vview[:, 4 * c:4 * c + 4, :],
                            V_loc[c * CH:(c + 1) * CH, pr * 128:(pr + 1) * 128].rearrange("(b p) n -> p b n", p=128),
                            vk, ['V_loc'], [vk], nowaw=(c > 0))
                slabs_pr[pr] = (ksl, kk, vview, vk)
            return slabs_pr[pr]

        def keys(n):
            s_ = n % 2
            return s_, ('F32A', f'm{s_}'), ('F32A', f'P{s_}'), f'aB{s_}', f'aTB{s_}'

        def A1(n):
            p = P[n]
            pr, hh, i, pc = p['pr'], p['hh'], p['i'], p['pc']
            ksl, kk, vview, vk = get_slabs(pr)
            h = 2 * pr + hh
            hs = slice(hh * 64, (hh + 1) * 64)
            s_, mk, Pk, ak, atk = keys(n)
            nt = min(2, nchunk - 2 * pc)
            for tl in range(nt - 1, -1, -1):
                c = 2 * pc + tl
                zi = psf_i[0] % 4
                psf_i[0] += 1
                ps, pk = psF_t[:, zi, :], ('psF', zi)
                masked = (c == j)
                mm(ps, qT[hs, pr, i * 128:(i + 1) * 128], ksl[hs, c, :], True, not masked, ['qT', kk], [pk])
                if masked:
                    mm(ps, ident_b[:, :], diagMB[:, i, :], False, True, ['ident_b', 'diagMB'], [pk])
                act(mB[s_][:, tl * 512:(tl + 1) * 512], ps, AF.Sigmoid, [pk, 'nbias'], [mk],
                    bias=nbias[:, 4, 0, h:h + 1], scale=-0.125)

        def A2(n):
            p = P[n]
            pc = p['pc']
            s_, mk, Pk, ak, atk = keys(n)
            nt = min(2, nchunk - 2 * pc)
            Lp = nt * CH
            if p['first']:
                vop('dve', 'memset', (PB[s_][:, Lp:Lp + 1], 1.0), (), [Pk])
                init = 1.0
                rd = [mk]
            else:
                prev = (n - 1) % 2
                vop('dve', 'tensor_copy', (PB[s_][:, Lp:Lp + 1], PB[prev][:, 0:1]), [('F32A', f'P{prev}')], [Pk])
                init = PB[prev][:, 0:1]
                rd = [mk, ('F32A', f'P{prev}')]
            vop('dve', 'tensor_tensor_scan', (PB[s_][:, 0:Lp][:, ::-1], mB[s_][:, 0:Lp][:, ::-1], mB[s_][:, 0:Lp][:, ::-1], init, ALU.mult, ALU.min),
                rd, [Pk])
            vop('pool', 'tensor_tensor', (aB[s_][:, 0:Lp], PB[s_][:, 1:Lp + 1], PB[s_][:, 0:Lp], ALU.subtract), [Pk], [ak])

        def Bst(n):
            p = P[n]
            pr, hh, i, pc = p['pr'], p['hh'], p['i'], p['pc']
            ksl, kk, vview, vk = get_slabs(pr)
            hs = slice(hh * 64, (hh + 1) * 64)
            s_, mk, Pk, ak, atk = keys(n)
            nt = min(2, nchunk - 2 * pc)
            Lp = nt * CH
            oi = 4 + (p['chain'] % 2)
            ops_, opk = psF_t[hs, oi, 0:128], ('psF', oi)
            ps, pk = psb()
            for k in range(4 * nt):
                tr(ps[:, k * 128:(k + 1) * 128], aB[s_][:, k * 128:(k + 1) * 128], ident_b[:, :], [ak, 'ident_b'], [pk])
            act(aTB[s_][:, 0:Lp], ps[:, 0:Lp], AF.Copy, [pk], [atk])
            for k in range(4 * nt):
                blk = pc * 8 + k
                lastmm = (p['last'] and k == 4 * nt - 1)
                mm(ops_, vview[:, blk, hs], aTB[s_][:, k * 128:(k + 1) * 128], p['first'] and k == 0, lastmm, [vk, atk], [opk])
            if p['last']:
                act(oT_v[hs, pr, i * 128:(i + 1) * 128], ops_, AF.Copy, [opk], [('BFA', 'oT')])

        N = len(P)
        for n in range(N + 2):
            if n < N:
                A1(n)
            if 1 <= n <= N:
                A2(n - 1)
            if n >= 2:
                Bst(n - 2)

    def poolconv(n, off, first16=False):
        L = 30 + n
        s2, s4 = cw[0], cw[1]
        vop('dve', 'tensor_tensor', (s2[:, :, 1:L], extu[:, :, 1:L], extu[:, :, 0:L - 1], ALU.add), [('F32A', 'extu')], [('F32A', 'cw0')])
        vop('dve', 'tensor_tensor', (s4[:, :, 3:L], s2[:, :, 3:L], s2[:, :, 1:L - 2], ALU.add), [('F32A', 'cw0')], [('F32A', 'cw1')])
        tk = ('F32A', 'tmp')
        vop('dve', 'tensor_tensor', (ptmp[:, 7:L], s4[:, 1, 7:L], s4[:, 1, 3:L - 4], ALU.add), [('F32A', 'cw1')], [tk])
        vop('dve', 'tensor_tensor', (ptmp2[:, 15:L], ptmp[:, 15:L], ptmp[:, 7:L - 8], ALU.add), [tk], [('F32A', 'tmp2')])
        srcs = [(s2, 0, 0, ('F32A', 'cw0')), (s4, 0, 1, ('F32A', 'cw1')), (None, 1, 0, tk), (None, 1, 1, ('F32A', 'tmp2'))]
        for g, (sbuf_, c, hh, key) in enumerate(srcs):
            hs = slice(hh * 64, (hh + 1) * 64)
            src = sbuf_[hs, c, 30:L] if sbuf_ is not None else (ptmp[hs, 30:L] if g == 2 else ptmp2[hs, 30:L])
            if first16:
                st, stk = stage32()
                vop('dve', 'tensor_tensor', (st[hs, 0:16], (sbuf_[hs, c, 30:46] if sbuf_ is not None else (ptmp[hs, 30:46] if g == 2 else ptmp2[hs, 30:46])),
                                             pscale_sb[hs, c, :], ALU.mult), [key, 'pscale_sb'], [stk])
                vop('dve', 'tensor_tensor', (pT_v[hs, c, off:off + 16], st[hs, 0:16], extu[hs, c, 30:46], ALU.subtract),
                    [stk, ('F32A', 'extu')], [('BFA', 'pT')])
                vop('dve', 'scalar_tensor_tensor', (pT_v[hs, c, off + 16:off + n], src[:, 16:n], invw[hs, c:c + 1], extu[hs, c, 46:L], ALU.mult, ALU.subtract),
                    [key, ('F32A', 'extu'), 'invw'], [('BFA', 'pT')])
            else:
                vop('dve', 'scalar_tensor_tensor', (pT_v[hs, c, off:off + n], src, invw[hs, c:c + 1], extu[hs, c, 30:L], ALU.mult, ALU.subtract),
                    [key, ('F32A', 'extu'), 'invw'], [('BFA', 'pT')])
        barrier(['F32A'])
        for c in range(2):
            eng = 'dve'
            for k in range(31):
                dst = cacc[k % 2][:, c, 0:n]
                dk = ('F32A', f'cacc{k % 2}_{c}')
                if k == 0:
                    vop(eng, 'tensor_scalar', (dst, extg[:, c, 0:n], cdw[:, c, 0:1], None, ALU.mult), [('F32A', 'extg'), 'cdw'], [dk])
                else:
                    srck = ('F32A', f'cacc{(k - 1) % 2}_{c}')
                    vop(eng, 'scalar_tensor_tensor', (dst, extg[:, c, k:k + n], cdw[:, c, k:k + 1], cacc[(k - 1) % 2][:, c, 0:n], ALU.mult, ALU.add),
                        [('F32A', 'extg'), 'cdw', srck], [dk])
            vop(eng, 'tensor_scalar', (cacc[0][:, c, 0:n], cacc[0][:, c, 0:n], par[:, 32 + c:33 + c], None, ALU.add),
                [('F32A', f'cacc0_{c}'), ('par', 2)], [('F32A', f'cacc0_{c}')])
            act(cacc[1][:, c, 0:n], cacc[0][:, c, 0:n], AF.Square, [('F32A', f'cacc0_{c}')], [('F32A', f'cacc1_{c}')])
        pm, pmk = psf()
        pe2, pe2k = psf()
        for c in range(2):
            mm(pm[:, 0:n], onesm[:, :], cacc[0][:, c, 0:n], c == 0, c == 1, ['onesm', ('F32A', f'cacc0_{c}')], [pmk])
        for c in range(2):
            mm(pe2[:, 0:n], onesm[:, :], cacc[1][:, c, 0:n], c == 0, c == 1, ['onesm', ('F32A', f'cacc1_{c}')], [pe2k])
        st, stk = stage32()
        act(st[:, 0:n], pm[:, 0:n], AF.Copy, [pmk], [stk])
        st2, st2k = stage32()
        vop('dve', 'tensor_tensor', (st2[:, 0:n], st[:, 0:n], st[:, 0:n], ALU.mult), [stk], [st2k])
        vop('dve', 'tensor_tensor', (st2[:, 0:n], pe2[:, 0:n], st2[:, 0:n], ALU.subtract), [pe2k, st2k], [st2k])
        vop('dve', 'tensor_scalar', (st2[:, 0:n], st2[:, 0:n], EPS, None, ALU.add), [st2k], [st2k])
        act(st2[:, 0:n], st2[:, 0:n], AF.Sqrt, [st2k], [st2k])
        vop('dve', 'reciprocal', (st2[:, 0:n], st2[:, 0:n]), [st2k], [st2k])
        for c in range(2):
            ck = ('F32A', f'cacc0_{c}')
            vop('dve', 'tensor_tensor', (cacc[0][:, c, 0:n], cacc[0][:, c, 0:n], st[:, 0:n], ALU.subtract), [ck, stk], [ck])
            vop('dve', 'tensor_tensor', (cacc[0][:, c, 0:n], cacc[0][:, c, 0:n], st2[:, 0:n], ALU.mult), [ck, st2k], [ck])
            act(sT_v[:, c, off:off + n], cacc[0][:, c, 0:n], AF.Silu, [ck, ('par', 3), ('par', 4)], [('BFA', 'sT')],
                bias=par[:, 36 + c:37 + c], scale=par[:, 34 + c:35 + c])

    def layer_weights(l):
        load(ao_sb[:, :, :], wb_ao[l].rearrange("(h p) n -> p h n", p=128), 'ao_sb', [('wb_ao', l)], ['ao_sb'])
        load(pw_sb[:, :, :], wb_pw[l].rearrange("(c p) n -> p c n", p=128), 'pw_sb', [('wb_pw', l)], ['pw_sb'])
        load_nc(poolw_sb[:, :, :], wb_pool[l].rearrange("(c p) n -> p c n", p=128), 'poolw_sb', [('wb_pool', l)], ['poolw_sb'])

    def mixer_ffn(l, x_src, xm_scr, out_dst, ntok, tok0):
        nblk = (ntok + 127) // 128
        wl = wb_in[l]
        load(xT[:, :, 0:ntok], xT_scr[:, tok0:tok0 + ntok].rearrange("(k p) t -> p k t", p=128), 'xT', ['xT_scr'], ['xT'])
        for grp in range(2):
            gsl = [load_slab(wl, 0, 8, 2304 + b_ * 1024 + grp * 512, 512, ('wb_in', l)) for b_ in range(3)]
            for nn in range(4):
                nch = grp * 4 + nn
                for b_ in range(3):
                    sl, sk = gsl[b_]
                    ps, pk = psf()
                    for kc in range(8):
                        mm(ps[:, 0:ntok], sl[:, kc, nn * 128:(nn + 1) * 128], xT[:, kc, 0:ntok], kc == 0, kc == 7, [sk, 'xT'], [pk])
                    act(gt_v[:, b_, 0:ntok], ps[:, 0:ntok], AF.Sigmoid, [pk, 'par'], [('BFA', f'gt{b_}')], bias=par[:, b_ * 8 + nch:b_ * 8 + nch + 1])
                g = nch // 2
                hs = slice((g % 2) * 64, (g % 2) * 64 + 64)
                pa, pak = psf()
                mm(pa[:, 0:ntok], poolw_sb[hs, g // 2, (nch % 2) * 128:(nch % 2) * 128 + 128], pT_v[hs, g // 2, 0:ntok], True, True,
                   ['poolw_sb', ('BFA', 'pT')], [pak])
                pb, pbk = psf()
                for pr in range(4):
                    mm(pb[:, 0:ntok], ao_sb[:, pr, nch * 128:(nch + 1) * 128], oT_v[:, pr, 0:ntok], pr == 0, pr == 3, ['ao_sb', ('BFA', 'oT')], [pbk])
                pc_, pck = psf()
                for c in range(2):
                    mm(pc_[:, 0:ntok], pw_sb[:, c, nch * 128:(nch + 1) * 128], sT_v[:, c, 0:ntok], c == 0, c == 1, ['pw_sb', ('BFA', 'sT')], [pck])
                t1, t1k = stage32()
                vop('dve', 'scalar_tensor_tensor', (t1[:, 0:ntok], pa[:, 0:ntok], par[:, 24 + nch:25 + nch], gt_v[:, 0, 0:ntok], ALU.mult, ALU.mult),
                    [pak, ('par', 1), ('BFA', 'gt0')], [t1k])
                t2, t2k = stage32()
                vop('dve', 'tensor_tensor', (t2[:, 0:ntok], pb[:, 0:ntok], gt_v[:, 1, 0:ntok], ALU.mult), [pbk, ('BFA', 'gt1')], [t2k])
                vop('pool', 'tensor_tensor', (t1[:, 0:ntok], t1[:, 0:ntok], t2[:, 0:ntok], ALU.add), [t1k, t2k], [t1k])
                t3, t3k = stage32()
                vop('dve', 'tensor_tensor', (t3[:, 0:ntok], pc_[:, 0:ntok], gt_v[:, 2, 0:ntok], ALU.mult), [pck, ('BFA', 'gt2')], [t3k])
                vop('pool', 'tensor_tensor', (mg_v[:, nch, 0:ntok], t1[:, 0:ntok], t3[:, 0:ntok], ALU.add), [t1k, t3k], [('BFA', f'mg{nch}')])
        wsl = [load_slab(wb_out[l], 0, 8, hf * 512, 512, ('wb_out', l)) for hf in range(2)]
        mgk = [('BFA', f'mg{k}') for k in range(8)]
        def stX(b):
            n = min(128, ntok - b * 128)
            xr, xrk = R32[b % 2], f"R32_{b % 2}"
            yr, yrk = R32[2 + (b % 2)], f"R32_{2 + (b % 2)}"
            load(xr[0:n, :], x_src[b * 128:b * 128 + n, :], xrk, (), [xrk])
            for hf in range(2):
                sl, sk = wsl[hf]
                ps, pk = psf()
                for kc in range(8):
                    mm(ps[0:n, :], mg_v[:, kc, b * 128:b * 128 + n], sl[:, kc, :], kc == 0, kc == 7, [sk] + mgk, [pk])
                vop('dve', 'scalar_tensor_tensor', (yr[0:n, hf * 512:(hf + 1) * 512], xr[0:n, hf * 512:(hf + 1) * 512], ALPHA, ps[0:n, :], ALU.mult, ALU.add),
                    [xrk, pk], [yrk])

        def stY(b):
            n = min(128, ntok - b * 128)
            om, omk = R32[b % 2], f"R32_{b % 2}"
            yr, yrk = R32[2 + (b % 2)], f"R32_{2 + (b % 2)}"
            rows_ln(yr, yrk, n, 0, om, omk)
            load(xm_scr[tok0 + b * 128:tok0 + b * 128 + n, :], om[0:n, :], omk, [omk], ['xm_scr'])

        def stZ(b):
            n = min(128, ntok - b * 128)
            rows_to_xT(R32[b % 2], f"R32_{b % 2}", b, n)

        for it in range(nblk + 1):
            if it < nblk:
                stX(it)
            if it >= 1:
                stY(it - 1)
                stZ(it - 1)
        barrier(['BFA'])
        nsl = (DFF + 511) // 512
        for si in range(nsl):
            ncol = min(512, DFF - si * 512)
            gs, gk = load_slab(wb_g[l], 0, 8, si * 512, ncol, ('wb_g', l))
            us, uk = load_slab(wb_u[l], 0, 8, si * 512, ncol, ('wb_u', l))
            for nn in range(ncol // 128):
                ch = si * 4 + nn
                pg, pgk = psf()
                for kc in range(8):
                    mm(pg[:, 0:ntok], gs[:, kc, nn * 128:(nn + 1) * 128], xT[:, kc, 0:ntok], kc == 0, kc == 7, [gk, 'xT'], [pgk])
                pu, puk = psf()
                for kc in range(8):
                    mm(pu[:, 0:ntok], us[:, kc, nn * 128:(nn + 1) * 128], xT[:, kc, 0:ntok], kc == 0, kc == 7, [uk, 'xT'], [puk])
                st, stk = stage32()
                act(st[:, 0:ntok], pg[:, 0:ntok], AF.Silu, [pgk], [stk])
                vop('dve', 'tensor_tensor', (hT_v[:, ch, 0:ntok], pu[:, 0:ntok], st[:, 0:ntok], ALU.mult), [puk, stk], [('BFA', f'h{ch}')])
        hk = [('BFA', f'h{k}') for k in range(22)]
        for hf in range(2):
            for kg in range(3):
                nk = 8 if kg < 2 else 6
                sl, sk = load_slab(wb_d[l], kg * 1024, nk, hf * 512, 512, ('wb_d', l))
                for b in range(nblk):
                    n = min(128, ntok - b * 128)
                    ps, pk = psF_t[:, b, :], ('psF', b)
                    for k in range(nk):
                        kc = kg * 8 + k
                        mm(ps[0:n, :], hT_v[:, kc, b * 128:b * 128 + n], sl[:, k, :], kc == 0, kc == 21, [sk] + hk, [pk])
            for b in range(nblk):
                n = min(128, ntok - b * 128)
                ps, pk = psF_t[:, b, :], ('psF', b)
                st, stk = stage32()
                load(st[0:n, :], xm_scr[tok0 + b * 128:tok0 + b * 128 + n, hf * 512:(hf + 1) * 512], stk, ['xm_scr'], [stk])
                yr, yrk = R32[b], f"R32_{b}"
                vop('dve', 'scalar_tensor_tensor', (yr[0:n, hf * 512:(hf + 1) * 512], st[0:n, :], ALPHA, ps[0:n, :], ALU.mult, ALU.add),
                    [stk, pk], [(yrk, hf)])
        for b in range(nblk):
            n = min(128, ntok - b * 128)
            yr, yrk = R32[b], f"R32_{b}"
            rows_ln(yr, yrk, n, 2, yr, yrk)
            load(out_dst[b * 128:b * 128 + n, :], yr[0:n, :], yrk, [yrk], ['out_dst'])

    def load_ext_prompt(j, tok0):
        for c in range(2):
            if j == 0:
                vop('pool', 'memset', (extu[:, c, 0:30], 0.0), (), [('F32A', 'extu')])
                vop('pool', 'memset', (extg[:, c, 0:30], 0.0), (), [('F32A', 'extg')])
                load(extu[:, c, 30:30 + CH], ug_scr[c * 128:(c + 1) * 128, 0:CH], 'extld', [('ug_scr', 'u')], [('F32A', 'extu')], nowaw=True)
                load(extg[:, c, 30:30 + CH], ug_scr[256 + c * 128:256 + (c + 1) * 128, 0:CH], 'extld', [('ug_scr', 'g')], [('F32A', 'extg')], nowaw=True)
            else:
                load(extu[:, c, 0:30 + CH], ug_scr[c * 128:(c + 1) * 128, tok0 - 30:tok0 + CH], 'extld', [('ug_scr', 'u')], [('F32A', 'extu')], nowaw=True)
                load(extg[:, c, 0:30 + CH], ug_scr[256 + c * 128:256 + (c + 1) * 128, tok0 - 30:tok0 + CH], 'extld', [('ug_scr', 'g')], [('F32A', 'extg')], nowaw=True)

    def load_ext_sample(l, s_):
        vop('pool', 'memset', (extu[:, :, 0:30], 0.0), (), [('F32A', 'extu')])
        for c in range(2):
            load_nc(extu[:, c, 15:30], state_pool[l, s_, :, c * 128:(c + 1) * 128].rearrange("t p -> p t"), 'extld', (), [('F32A', 'extu')], nowaw=True)
            load_nc(extg[:, c, 0:30], state_conv[l, s_, :, c * 128:(c + 1) * 128].rearrange("t p -> p t"), 'extld', (), [('F32A', 'extg')], nowaw=True)
            load_nc(extu[:, c, 30:34], ug_scr[c * 128:(c + 1) * 128, T + 4 * s_:T + 4 * s_ + 4], 'extld', [('ug_scr', 'u')], [('F32A', 'extu')], nowaw=True)
            load_nc(extg[:, c, 30:34], ug_scr[256 + c * 128:256 + (c + 1) * 128, T + 4 * s_:T + 4 * s_ + 4], 'extld', [('ug_scr', 'g')], [('F32A', 'extg')], nowaw=True)

    def sample_prep(l):
        load_nc(ptf_i[:, :], page_table[:, :].rearrange("s j -> (s j)").partition_broadcast(128), 'ptf', (), ['ptf_i'])
        vop('dve', 'tensor_copy', (ptf[:, :], ptf_i[:, :]), ['ptf_i'], ['ptf'])
        vop('dve', 'tensor_copy', (ptf2[0:64, :], ptf[0:64, 0::2]), ['ptf'], [('ptf2', 0)])
        vop('dve', 'tensor_copy', (ptf2[64:128, :], ptf[64:128, 1::2]), ['ptf'], [('ptf2', 1)])
        vop('dve', 'tensor_scalar', (ptf2[:, :], ptf2[:, :], 64.0, float(l * npool * 64), ALU.mult, ALU.add), ['ptf2'], ['ptf2'])
        vop('dve', 'tensor_scalar', (ptf2[:, :], ptf2[:, :], consts_sb[:, 13:14], None, ALU.add), ['ptf2', 'consts_sb'], ['ptf2'])
        vop('dve', 'tensor_copy', (pidx2[:, :], ptf2[:, :]), ['ptf2'], ['pidx2'])
        sk0 = ('small', 0)
        vop('dve', 'tensor_tensor', (small[:, 0:8], consts_sb[:, 0:8], nbias[:, 4, 0, :], ALU.mult), ['consts_sb', 'nbias'], [sk0])
        vop('dve', 'tensor_tensor', (small[:, 0:4], small[:, 0:4], small[:, 4:8], ALU.add), [sk0], [sk0])
        vop('dve', 'tensor_tensor', (small[:, 0:2], small[:, 0:2], small[:, 2:4], ALU.add), [sk0], [sk0])
        vop('dve', 'tensor_tensor', (rbias[:, 0:1], small[:, 0:1], small[:, 1:2], ALU.add), [sk0], ['rbias'])

    def sample_attention(l, g):
        oacc = [R32[2 + s_ // 2][:, (s_ % 2) * 512:(s_ % 2 + 1) * 512] for s_ in range(4)]
        oacck = [(f"R32_{2 + s_ // 2}", s_ % 2) for s_ in range(4)]
        t0s = T + 16 * g
        load_nc(qT[:, :, 0:16], qT_scr[:, t0s:t0s + 16].rearrange("(c p) t -> p c t", p=128), 'qT', ['qT_scr'], ['qT'])
        load_nc(kn[:, :, 0:16], ksT_scr[:, 16 * g:16 * g + 16].rearrange("(c p) t -> p c t", p=128), 'kn', ['ksT_scr'], ['kn'])
        vop('pool', 'memset', (QZ[:, :, :], 0.0), (), ['QZ'])
        for s_ in range(4):
            for pr in range(4):
                for hh in range(2):
                    r0 = s_ * 32 + (2 * pr + hh) * 4
                    vop('dve', 'tensor_copy', (QZ[hh * 64:(hh + 1) * 64, s_ * 4 + pr, r0:r0 + 4], qT[hh * 64:(hh + 1) * 64, pr, 4 * s_:4 * s_ + 4]),
                        ['qT', 'QZ'], [('QZ', s_ * 4 + pr)])
        ps, pk = psF_t[:, 0, 0:4], ('psF', 0)
        n_mm = 0
        for s_ in range(4):
            for pr in range(4):
                mm(ps, QZ[:, s_ * 4 + pr, :], kn[:, pr, 4 * s_:4 * s_ + 4], n_mm == 0, n_mm == 15, [('QZ', s_ * 4 + pr), 'kn'], [pk])
                n_mm += 1
        st, stk = stage32()
        vop('dve', 'tensor_tensor', (st[:, 0:4], ps, consts_sb[:, 8:12], ALU.add), [pk, 'consts_sb'], [stk])
        act(mnew[:, 0:4], st[:, 0:4], AF.Sigmoid, [stk, 'rbias'], ['mnew'], bias=rbias[:, 0:1], scale=-0.125)
        vop('dve', 'memset', (Pnew[:, 4:5], 1.0), (), ['Pnew'])
        vop('dve', 'tensor_tensor_scan', (Pnew[:, 0:4][:, ::-1], mnew[:, 0:4][:, ::-1], mnew[:, 0:4][:, ::-1], 1.0, ALU.mult, ALU.min),
            ['mnew'], ['Pnew'])
        vop('dve', 'tensor_tensor', (anew[:, 0:4], Pnew[:, 1:5], Pnew[:, 0:4], ALU.subtract), ['Pnew'], ['anew'])
        pst, pstk = psb()
        tr(pst[0:4, 0:128], anew[:, 0:4], ident_b[:, :], ['anew', 'ident_b'], [pstk])
        vop('dve', 'tensor_copy', (anT[0:4, :], pst[0:4, 0:128]), [pstk], ['anT'])
        for s_ in range(4):
            ps2, pk2 = psf()
            mm(ps2[0:32, :], anT[0:4, s_ * 32:(s_ + 1) * 32], vnew[0:4, 4 * g + s_, :], True, True, ['anT', ('vnew', 4 * g + s_)], [pk2])
            vop('dve', 'tensor_copy', (oacc[s_][0:32, :], ps2[0:32, :]), [pk2], [oacck[s_]])
        prevP = ('new', None)
        sidx = [0]
        for pc in range(7, -1, -1):
            s2_ = sidx[0] % 2
            sidx[0] += 1
            mk, Pk, ak, atk = ('F32A', f'm{s2_}'), ('F32A', f'P{s2_}'), f'aB{s2_}', f'aTB{s2_}'
            for tl in (1, 0):
                zi = 1 + tl
                psz, pzk = psF_t[:, zi, :], ('psF', zi)
                n_mm = 0
                for s_ in range(4):
                    sg = 4 * g + s_
                    ksl, kk = next_slab()
                    kv = ksl[:, 0:4, :]
                    for m2 in range(2):
                        mglob = pc * 4 + tl * 2 + m2
                        kpg, kpk = pagebuf()
                        col = sg * 32 + mglob
                        S.dma('pool', lambda e, kpg=kpg, col=col: e.indirect_dma_start(
                            out=kpg[:, :], out_offset=None, in_=cache_k[:, :],
                            in_offset=bass.IndirectOffsetOnAxis(ap=pidx2[:, col:col + 1], axis=0)), kpk, ['pidx2'], [kpk])
                        kb, kbk = stage16w()
                        vop('dve', 'tensor_copy', (kb[:, :], kpg[:, :]), [kpk], [kbk])
                        pt_, ptk = psb()
                        for a_ in range(2):
                            for pr in range(4):
                                tr(pt_[:, (a_ * 4 + pr) * 128:(a_ * 4 + pr + 1) * 128], kb[:, a_ * 512 + pr * 128:a_ * 512 + (pr + 1) * 128],
                                   ident_b[:, :], [kbk, 'ident_b'], [ptk])
                        for a_ in range(2):
                            act(kv[:, :, m2 * 256 + a_:(m2 + 1) * 256:2], pt_[:, a_ * 512:(a_ + 1) * 512].rearrange("p (c t) -> p c t", c=4),
                                AF.Copy, [ptk], [(kk, m2 * 2 + a_)])
                    for pr in range(4):
                        mm(psz, QZ[:, s_ * 4 + pr, :], kv[:, pr, :], n_mm == 0, n_mm == 15, [('QZ', s_ * 4 + pr), kk], [pzk])
                        n_mm += 1
                act(mB[s2_][:, tl * 512:(tl + 1) * 512], psz, AF.Sigmoid, [pzk, 'rbias'], [mk], bias=rbias[:, 0:1], scale=-0.125)
            if prevP[0] == 'new':
                src_c, srck = Pnew[:, 0:1], 'Pnew'
            else:
                src_c, srck = PB[prevP[1]][:, 0:1], ('F32A', f'P{prevP[1]}')
            vop('dve', 'tensor_copy', (PB[s2_][:, 1024:1025], src_c), [srck], [Pk])
            vop('dve', 'tensor_tensor_scan', (PB[s2_][:, 0:1024][:, ::-1], mB[s2_][:, ::-1], mB[s2_][:, ::-1], src_c, ALU.mult, ALU.min),
                [mk, srck], [Pk])
            vop('dve', 'tensor_tensor', (aB[s2_][:, :], PB[s2_][:, 1:1025], PB[s2_][:, 0:1024], ALU.subtract), [Pk], [ak])
            pst, pstk = psb()
            for k in range(8):
                mloc, a_ = k // 2, k % 2
                tr(pst[:, k * 128:(k + 1) * 128], aB[s2_][:, mloc * 256 + a_:(mloc + 1) * 256:2], ident_b[:, :], [ak, 'ident_b'], [pstk])
            act(aTB[s2_][:, :], pst, AF.Copy, [pstk], [atk])
            for s_ in range(4):
                sg = 4 * g + s_
                ps2, pk2 = psf()
                for mloc in range(4):
                    mglob = pc * 4 + mloc
                    vpg, vpk = pagebuf()
                    col = sg * 32 + mglob
                    S.dma('pool', lambda e, vpg=vpg, col=col: e.indirect_dma_start(
                        out=vpg[:, :], out_offset=None, in_=cache_v[:, :],
                        in_offset=bass.IndirectOffsetOnAxis(ap=pidx2[:, col:col + 1], axis=0)), vpk, ['pidx2'], [vpk])
                    vb, vbk = stage16w()
                    act(vb[:, :], vpg[:, :], AF.Copy, [vpk], [vbk])
                    for a_ in range(2):
                        k = mloc * 2 + a_
                        mm(ps2[0:32, :], aTB[s2_][:, k * 128 + s_ * 32:k * 128 + (s_ + 1) * 32], vb[:, a_ * 512:(a_ + 1) * 512], k == 0, k == 7, [atk, vbk], [pk2])
                vop('dve', 'tensor_tensor', (oacc[s_][0:32, :], oacc[s_][0:32, :], ps2[0:32, :], ALU.add), [pk2, oacck[s_]], [oacck[s_]])
            prevP = ('past', s2_)
        for s_ in range(4):
            ob, obk = aB[s_ // 2][:, (s_ % 2) * 512:(s_ % 2 + 1) * 512], f'aB{s_ // 2}'
            vop('pool', 'tensor_copy', (ob[0:32, :], oacc[s_][0:32, :]), [oacck[s_]], [obk])
            pst, pstk = psb()
            for pr in range(4):
                tr(pst[:, pr * 32:(pr + 1) * 32], ob[0:32, pr * 128:(pr + 1) * 128], ident_b[0:32, 0:32], [obk, 'ident_b'], [pstk])
            for pr in range(4):
                for hh in range(2):
                    h = 2 * pr + hh
                    vop('dve', 'tensor_copy', (oT_v[hh * 64:(hh + 1) * 64, pr, 16 * g + 4 * s_:16 * g + 4 * s_ + 4],
                                               pst[hh * 64:(hh + 1) * 64, pr * 32 + h * 4:pr * 32 + h * 4 + 4]),
                        [pstk], [('BFA', 'oT')])

    load(pscale_sb[:, :, :], pscale[:, :, :], 'pscale_sb', (), ['pscale_sb'])
    load(consts_sb[:, :], consts_in[:, :], 'consts_sb', (), ['consts_sb'])
    vop('pool', 'memset', (invw[0:64, 0:1], 1.0 / 2), (), [('invw', 0)])
    vop('pool', 'memset', (invw[64:128, 0:1], 1.0 / 4), (), [('invw', 1)])
    vop('pool', 'memset', (invw[0:64, 1:2], 1.0 / 8), (), [('invw', 2)])
    vop('pool', 'memset', (invw[64:128, 1:2], 1.0 / 16), (), [('invw', 3)])
    for l in range(n_layers):
        convert_layer(l)
    for l in range(n_layers):
        last = (l == n_layers - 1)
        x_src = xp if l == 0 else X1
        xs_src = xs if l == 0 else X1s
        o_dst = y_p if last else X1
        os_dst = y_s if last else X1s
        layer_params(l)
        layer_weights(l)
        for j in range(NSLOT):
            p1(l, x_src[j * CH:(j + 1) * CH, :], CH, j * CH, j)
        if with_sample:
            p1(l, xs_src, NS, T, None)
            for s_ in range(NSEQ):
                load(pool_s[l, s_, 0:11, :], state_pool[l, s_, 4:15, :], 'd2d', (), ['pool_s'], nowaw=True)
                load(conv_s[l, s_, 0:26, :], state_conv[l, s_, 4:30, :], 'd2d', (), ['conv_s'], nowaw=True)
        for j in range(NSLOT):
            if stop_after in ('p1', 'p1nox'):
                break
            tok0 = j * CH
            barrier(['F32A', 'BFA'])
            attention(l, j, tok0)
            barrier(['F32A'])
            load_ext_prompt(j, tok0)
            poolconv(CH, 0, first16=(j == 0))
            mixer_ffn(l, x_src[tok0:tok0 + CH, :], XM, o_dst[tok0:tok0 + CH, :], CH, tok0)
        if with_sample and stop_after not in ('p1', 'p1nox'):
            barrier(['F32A', 'BFA'])
            sample_prep(l)
            for g in range(NSEQ // 4):
                sample_attention(l, g)
            barrier(['F32A'])
            for s_ in range(NSEQ):
                load_ext_sample(l, s_)
                poolconv(4, 4 * s_)
                barrier(['F32A'])
            mixer_ffn(l, xs_src, XM, os_dst, NS, T)

    semnames = {}
    for e_ in ENGS:
        semnames[e_] = es.enter_context(nc.semaphore(f"sem_{e_}"))
    for sk in S.dma_cnt:
        semnames[sk] = es.enter_context(nc.semaphore("dsem_" + str(sk[1]).replace(" ", "").replace("'", "").replace("(", "").replace(")", "").replace(",", "_")))
    print("n semaphores", len(semnames), "ops", {e_: len(S.ops[e_]) for e_ in ENGS}, "seq", S.seq)

    import os
    if os.environ.get("KDUMP"):
        with open(os.environ["KDUMP"], "w") as f_:
            for rec in S.log:
                f_.write(repr(rec) + "\n")
    LIMIT = int(os.environ.get("KLIMIT", "0")) or 10 ** 9
    final_cnt = {}
    for e_ in ENGS:
        for waits, fn, inc, seq in S.ops[e_]:
            if seq <= LIMIT and isinstance(inc[0], tuple):
                final_cnt[inc[0]] = final_cnt.get(inc[0], 0) + 16

    def emit(engname, e):
        for waits, fn, inc, seq in S.ops[engname]:
            if seq > LIMIT:
                continue
            for sk, v in waits:
                e.wait_ge(semnames[sk], v)
            ins = fn(e)
            ins.then_inc(semnames[inc[0]], inc[1])
        if engname == 'sp':
            for sk, c in final_cnt.items():
                e.wait_ge(semnames[sk], c)

    with nc.Block() as block:
        @block.tensor
        def _(e):
            emit('pe', e)

        @block.scalar
        def _(e):
            emit('act', e)

        @block.vector
        def _(e):
            emit('dve', e)

        @block.gpsimd
        def _(e):
            emit('pool', e)

        @block.sync
        def _(e):
            emit('sp', e)
    es.close()
    return nc


_CACHE = {}


def _get_prog(key):
    if key not in _CACHE:
        _CACHE[key] = build_program(*key)
    return _CACHE[key]


def kernel(x_prompt, x_sample, cache_k, cache_v, state_pool, state_conv, page_table,
           w_in, b_gate, sb_bias, pool_w, pool_scale, w_att_o, conv_dw, conv_b, conv_ln_g, conv_ln_b,
           conv_pw, w_out, ln1_g, ln1_b, ffn_w_gate, ffn_w_up, ffn_w_down, ln2_g, ln2_b,
           _with_sample=True, _n_layers=DEPTH, _stop_after=None):
    f32 = np.float32
    nc = _get_prog((_with_sample, _n_layers, _stop_after))
    x_prompt = np.asarray(x_prompt, f32)
    shared = dict(
        w_in=np.asarray(w_in, f32), b_gate=np.asarray(b_gate, f32), sb_bias=np.asarray(sb_bias, f32),
        pool_w=np.asarray(pool_w, f32), pool_scale=np.asarray(pool_scale, f32), w_att_o=np.asarray(w_att_o, f32),
        conv_dw=np.asarray(conv_dw, f32), conv_b=np.asarray(conv_b, f32), conv_ln_g=np.asarray(conv_ln_g, f32),
        conv_ln_b=np.asarray(conv_ln_b, f32), conv_pw=np.asarray(conv_pw, f32), w_out=np.asarray(w_out, f32),
        ln1_g=np.asarray(ln1_g, f32), ln1_b=np.asarray(ln1_b, f32), ffn_w_gate=np.asarray(ffn_w_gate, f32),
        ffn_w_up=np.asarray(ffn_w_up, f32), ffn_w_down=np.asarray(ffn_w_down, f32), ln2_g=np.asarray(ln2_g, f32),
        ln2_b=np.asarray(ln2_b, f32))
    if _with_sample:
        shared['cache_k'] = np.asarray(cache_k, f32).reshape(DEPTH * 2560 * 64, 1024)
        shared['cache_v'] = np.asarray(cache_v, f32).reshape(DEPTH * 2560 * 64, 1024)
    consts = np.zeros((128, 16), f32)
    for r in range(128):
        h_, t_ = (r // 4) % 8, r % 4
        consts[r, h_] = 1.0
        for tn in range(4):
            consts[r, 8 + tn] = 0.0 if tn < t_ else -8.0 * MASKV
        consts[r, 12] = r
        consts[r, 13] = r % 64
    ps_ = np.zeros((128, 2, 16), f32)
    for cc in range(2):
        for p in range(128):
            w = (2, 4, 8, 16)[cc * 2 + p // 64]
            for t_ in range(16):
                ps_[p, cc, t_] = 1.0 / min(t_ + 1, w)
    in_maps = []
    for c in range(NCORES):
        m = dict(shared)
        m['xp'] = np.ascontiguousarray(x_prompt[c])
        m['xs'] = np.ascontiguousarray(np.asarray(x_sample, f32)[NSEQ * c:NSEQ * (c + 1)].reshape(NS, D))
        m['pscale'] = ps_
        m['consts_in'] = consts
        if _with_sample:
            m['state_pool'] = np.ascontiguousarray(np.asarray(state_pool, f32)[:, NSEQ * c:NSEQ * (c + 1)])
            m['state_conv'] = np.ascontiguousarray(np.asarray(state_conv, f32)[:, NSEQ * c:NSEQ * (c + 1)])
            m['page_table'] = np.ascontiguousarray(np.asarray(page_table, np.int32)[NSEQ * c:NSEQ * (c + 1)])
        in_maps.append(m)
    res = run_bass_kernel_spmd(nc, in_maps, core_ids=list(range(NCORES)))
    R = res.results
    y_prompt = np.zeros((4, 4096, D), f32)
    k_prompt = np.zeros((DEPTH, 4, 4096, 8, 64), f32)
    v_prompt = np.zeros((DEPTH, 4, 4096, 8, 64), f32)
    pool_prompt = np.zeros((DEPTH, 4, 15, 256), f32)
    conv_prompt = np.zeros((DEPTH, 4, 30, 256), f32)
    y_sample = np.zeros((32, 4, D), f32)
    k_sample = np.zeros((DEPTH, 32, 4, 8, 64), f32)
    v_sample = np.zeros((DEPTH, 32, 4, 8, 64), f32)
    pool_sample = np.zeros((DEPTH, 32, 15, 256), f32)
    conv_sample = np.zeros((DEPTH, 32, 30, 256), f32)
    for c in range(NCORES):
        o = R[c]
        sl = slice(NSEQ * c, NSEQ * (c + 1))
        y_prompt[c] = o['y_p']
        k_prompt[:, c] = o['k_p'].reshape(DEPTH, T, 8, 64)
        v_prompt[:, c] = o['v_p'].reshape(DEPTH, T, 8, 64)
        pool_prompt[:, c] = o['pool_p']
        conv_prompt[:, c] = o['conv_p']
        y_sample[sl] = o['y_s'].reshape(NSEQ, 4, D)
        k_sample[:, sl] = o['k_s'].reshape(DEPTH, NSEQ, 4, 8, 64)
        v_sample[:, sl] = o['v_s'].reshape(DEPTH, NSEQ, 4, 8, 64)
        pool_sample[:, sl] = o['pool_s']
        conv_sample[:, sl] = o['conv_s']
    return (y_prompt, y_sample, k_prompt, v_prompt, pool_prompt, conv_prompt,
            k_sample, v_sample, pool_sample, conv_sample)
```
